# Optimizing a Trainium2 kernel written in Bass

```python
import math
import jax, jax.numpy as jnp
from jax import lax
import numpy as np

D_MODEL = 2048
BATCH = 4
SEQ = 2048
DEPTH = 2
DEC_BATCH = 128
DEC_SEQ = 8
PAST_LEN = 16384
PAGE_SIZE = 128

HEAD_DIM = 128
H_A = 6
H_B = 6
H_C = 4
W_A = H_A * HEAD_DIM
W_B = H_B * HEAD_DIM
W_C = H_C * HEAD_DIM
MIX = W_A + W_B + W_C
N_HEADS_ALL = H_A + H_B + H_C
N_META = 16
CONV_W = 4
CHUNK_A = 64
CHUNK_B = 64
CHUNK_C = 16
EPS = 1e-6
NEG_BIG = -1e30
EXP_CLIP = 60.0
SPLIT_SIZES = (W_A, W_A, W_A, W_A, H_A, H_A,
               W_B, W_B, W_B, H_B, H_B,
               W_C, W_C, W_C,
               MIX)
SPLIT_POINTS = tuple(int(s) for s in np.cumsum(SPLIT_SIZES)[:-1])
N_IN = int(sum(SPLIT_SIZES))

kernel_name = 'hymba_mlstm_gdn_hgrn2_step'


def _rmsnorm(x, w):
    xf = x.astype(jnp.float32)
    y = xf * lax.rsqrt(jnp.mean(xf * xf, axis=-1, keepdims=True) + EPS)
    return (y * w.astype(jnp.float32)).astype(x.dtype)


def _to_chunks(x, c):
    b, l = x.shape[:2]
    x = x.reshape((b, l // c, c) + x.shape[2:])
    return jnp.swapaxes(jnp.moveaxis(x, 1, 0), 2, 3)


def _from_chunks(x):
    nc, b, h, c = x.shape[:4]
    x = jnp.moveaxis(jnp.swapaxes(x, 2, 3), 0, 1)
    return x.reshape((b, nc * c) + x.shape[3:])


def _mlstm_segment(q, k, v, i_pre, f_pre, state, chunk):
    c = math.gcd(q.shape[1], chunk)
    qc = _to_chunks(q, c)
    kc = _to_chunks(k * (HEAD_DIM ** -0.5), c)
    vc = _to_chunks(v, c)
    lic = _to_chunks(i_pre, c)
    lfc = _to_chunks(jax.nn.log_sigmoid(f_pre), c)
    causal = jnp.tril(jnp.ones((c, c), dtype=bool))

    def step(carry, inp):
        C, n, m = carry
        q_, k_, v_, li, lf = inp
        F = jnp.cumsum(lf, axis=-1)
        Dm = jnp.where(causal, F[..., :, None] - F[..., None, :] + li[..., None, :], NEG_BIG)
        inter = F + m[..., None]
        mt = jnp.maximum(inter, jnp.max(Dm, axis=-1))
        a = jnp.exp(inter - mt)
        P = jnp.exp(Dm - mt[..., None]) * jnp.einsum('bhtd,bhsd->bhts', q_, k_)
        num = a[..., None] * jnp.einsum('bhtd,bhde->bhte', q_, C) + jnp.einsum('bhts,bhse->bhte', P, v_)
        den = a * jnp.einsum('bhtd,bhd->bht', q_, n) + jnp.sum(P, axis=-1)
        h = num / jnp.maximum(jnp.abs(den), jnp.exp(jnp.minimum(-mt, EXP_CLIP)))[..., None]
        m_new = mt[..., -1]
        a_s = jnp.exp(F[..., -1] + m - m_new)
        w = jnp.exp(F[..., -1:] - F + li - m_new[..., None])
        C_new = a_s[..., None, None] * C + jnp.einsum('bhs,bhsd,bhse->bhde', w, k_, v_)
        n_new = a_s[..., None] * n + jnp.einsum('bhs,bhsd->bhd', w, k_)
        return (C_new, n_new, m_new), h

    new_state, h = lax.scan(step, state, (qc, kc, vc, lic, lfc))
    return _from_chunks(h), new_state


def _gdn_segment(q, k, v, g, beta, S0, chunk):
    c = math.gcd(q.shape[1], chunk)
    qc, kc, vc = _to_chunks(q, c), _to_chunks(k, c), _to_chunks(v, c)
    gc, bc = _to_chunks(g, c), _to_chunks(beta, c)
    G = jnp.cumsum(gc, axis=-1)
    incl = jnp.tril(jnp.ones((c, c), dtype=bool))
    strict = jnp.tril(jnp.ones((c, c), dtype=bool), -1)
    decay = jnp.exp(jnp.where(incl, G[..., :, None] - G[..., None, :], NEG_BIG))
    A = jnp.where(strict, bc[..., None] * decay * jnp.einsum('nbhtd,nbhsd->nbhts', kc, kc), 0.0)
    M = A + jnp.eye(c, dtype=A.dtype)
    rhs = jnp.concatenate([bc[..., None] * vc, (bc * jnp.exp(G))[..., None] * kc], axis=-1)
    sol = lax.linalg.triangular_solve(M, rhs, left_side=True, lower=True)
    U, Wk = sol[..., :HEAD_DIM], sol[..., HEAD_DIM:]
    Aqk = decay * jnp.einsum('nbhtd,nbhsd->nbhts', qc, kc)

    def step(S, inp):
        q_, k_, U_, Wk_, Aqk_, G_ = inp
        W = U_ - jnp.einsum('bhtd,bhde->bhte', Wk_, S)
        o = jnp.exp(G_)[..., None] * jnp.einsum('bhtd,bhde->bhte', q_, S) + jnp.einsum('bhts,bhse->bhte', Aqk_, W)
        Ge = G_[..., -1:]
        S_new = jnp.exp(Ge)[..., None] * S + jnp.einsum('bhsd,bhse->bhde', k_ * jnp.exp(Ge - G_)[..., None], W)
        return S_new, o

    S_new, o = lax.scan(step, S0, (qc, kc, U, Wk, Aqk, G))
    return _from_chunks(o), S_new


def _hgrn_segment(q, k, v, logf, S0, chunk):
    c = math.gcd(q.shape[1], chunk)
    qc, kc, vc = _to_chunks(q, c), _to_chunks(k, c), _to_chunks(v, c)
    bcum = jnp.cumsum(_to_chunks(logf, c), axis=-2)
    incl = jnp.tril(jnp.ones((c, c), dtype=bool))[..., None]

    def step(S, inp):
        q_, k_, v_, b_ = inp
        diff = jnp.where(incl, b_[..., :, None, :] - b_[..., None, :, :], NEG_BIG)
        A = jnp.einsum('bhtd,bhsd,bhtsd->bhts', q_, k_, jnp.exp(diff))
        o = jnp.einsum('bhtd,bhde->bhte', q_ * jnp.exp(b_), S) + jnp.einsum('bhts,bhse->bhte', A, v_)
        be = b_[..., -1:, :]
        S_new = jnp.exp(be[..., 0, :])[..., None] * S + jnp.einsum('bhsd,bhse->bhde', k_ * jnp.exp(be - b_), v_)
        return S_new, o

    S_new, o = lax.scan(step, S0, (qc, kc, vc, bcum))
    return _from_chunks(o), S_new


def _l2norm(x):
    return x * lax.rsqrt(jnp.sum(x * x, axis=-1, keepdims=True) + EPS)


def _layer(x, state, seg_lens, norm_w, w_in, mlstm_b, A_log, dt_bias, conv_w, lb, out_norm_w, w_out):
    f32 = jnp.float32
    C_a, n_a, m_a, S_b, conv_b, S_c = state
    bsz, L, _ = x.shape
    h = _rmsnorm(x, norm_w)
    proj = jnp.einsum('bld,de->ble', h, w_in)
    (aq, ak, av, ao, ai, af, bq, bk, bv, ba, bb, cq, cf, ci, z) = jnp.split(proj, SPLIT_POINTS, axis=-1)

    def heads(t, nh):
        return t.astype(f32).reshape(bsz, L, nh, HEAD_DIM)

    mlstm_b = mlstm_b.astype(f32)
    qa, ka, va = heads(aq, H_A), heads(ak, H_A), heads(av, H_A)
    i_pre = ai.astype(f32) + mlstm_b[0]
    f_pre = af.astype(f32) + mlstm_b[1]
    u = jnp.concatenate([bq, bk, bv], axis=-1)
    u_ext = jnp.concatenate([conv_b.astype(u.dtype), u], axis=1)
    conv = u_ext[:, 0:L].astype(f32) * conv_w[0].astype(f32)
    for j in range(1, CONV_W):
        conv = conv + u_ext[:, j:j + L].astype(f32) * conv_w[j].astype(f32)
    conv = jax.nn.silu(conv)
    new_conv = u_ext[:, -(CONV_W - 1):]
    qb = _l2norm(heads(conv[..., :W_B], H_B)) * (HEAD_DIM ** -0.5)
    kb = _l2norm(heads(conv[..., W_B:2 * W_B], H_B))
    vb = heads(conv[..., 2 * W_B:], H_B)
    g_b = -jnp.exp(A_log.astype(f32)) * jax.nn.softplus(ba.astype(f32) + dt_bias.astype(f32))
    beta_b = jax.nn.sigmoid(bb.astype(f32))
    lbh = lb.reshape(H_C, HEAD_DIM)
    cfh = heads(cf, H_C)
    logf_c = jax.nn.log_sigmoid(cfh) + jnp.log1p(lbh * jnp.exp(jnp.minimum(-cfh, EXP_CLIP)))
    kc = (1.0 - lbh) * jax.nn.sigmoid(-cfh)
    qc = jax.nn.silu(heads(cq, H_C))
    vc = heads(ci, H_C)

    st_a = (C_a.astype(f32), n_a.astype(f32), m_a.astype(f32))
    st_b = S_b.astype(f32)
    st_c = S_c.astype(f32)
    outs_a, outs_b, outs_c = [], [], []
    off = 0
    for ls in seg_lens:
        sl = slice(off, off + ls)
        o, st_a = _mlstm_segment(qa[:, sl], ka[:, sl], va[:, sl], i_pre[:, sl], f_pre[:, sl], st_a, CHUNK_A)
        outs_a.append(o)
        o, st_b = _gdn_segment(qb[:, sl], kb[:, sl], vb[:, sl], g_b[:, sl], beta_b[:, sl], st_b, CHUNK_B)
        outs_b.append(o)
        o, st_c = _hgrn_segment(qc[:, sl], kc[:, sl], vc[:, sl], logf_c[:, sl], st_c, CHUNK_C)
        outs_c.append(o)
        off += ls
    ha = jnp.concatenate(outs_a, axis=1) * jax.nn.sigmoid(heads(ao, H_A))
    hb = jnp.concatenate(outs_b, axis=1)
    hc = jnp.concatenate(outs_c, axis=1)
    hcat = jnp.concatenate([ha, hb, hc], axis=2)
    hcat = hcat * lax.rsqrt(jnp.mean(hcat * hcat, axis=-1, keepdims=True) + EPS)
    y = hcat.reshape(bsz, L, MIX) * out_norm_w.astype(f32) * jax.nn.silu(z.astype(f32))
    out = jnp.einsum('ble,ed->bld', y.astype(x.dtype), w_out)
    return x + out, (st_a[0], st_a[1], st_a[2], st_b, new_conv, st_c)


def _zero_state(bsz, dtype):
    f32 = jnp.float32
    return (jnp.zeros((bsz, H_A, HEAD_DIM, HEAD_DIM), f32), jnp.zeros((bsz, H_A, HEAD_DIM), f32),
            jnp.zeros((bsz, H_A), f32), jnp.zeros((bsz, H_B, HEAD_DIM, HEAD_DIM), f32),
            jnp.zeros((bsz, CONV_W - 1, 3 * W_B), dtype), jnp.zeros((bsz, H_C, HEAD_DIM, HEAD_DIM), f32))


def setup_inputs(seed: int = 0) -> dict:
    key = jax.random.key(seed)
    ks = jax.random.split(key, 24)
    f32 = jnp.float32

    def nrm(k, shape, s):
        return jax.random.normal(k, shape, f32) * s

    dt = jnp.exp(jax.random.uniform(ks[13], (DEPTH, H_B), f32, math.log(1e-3), math.log(1e-1)))
    mlstm_gate_b = jnp.stack([nrm(ks[10], (DEPTH, H_A), 0.1),
                              jax.random.uniform(ks[11], (DEPTH, H_A), f32, 3.0, 6.0)], axis=1)
    return {
        'x_prompt': nrm(ks[0], (BATCH, SEQ, D_MODEL), 1.0),
        'x_sample': nrm(ks[1], (DEC_BATCH, DEC_SEQ, D_MODEL), 1.0),
        'state_mlstm_C': nrm(ks[2], (DEPTH, DEC_BATCH, H_A, HEAD_DIM, HEAD_DIM), HEAD_DIM ** -0.5),
        'state_mlstm_n': nrm(ks[3], (DEPTH, DEC_BATCH, H_A, HEAD_DIM), HEAD_DIM ** -0.5),
        'state_mlstm_m': nrm(ks[4], (DEPTH, DEC_BATCH, H_A), 1.0),
        'state_gdn_S': nrm(ks[5], (DEPTH, DEC_BATCH, H_B, HEAD_DIM, HEAD_DIM), 0.5),
        'state_gdn_conv': nrm(ks[6], (DEPTH, DEC_BATCH, CONV_W - 1, 3 * W_B), 1.0),
        'state_hgrn_S': nrm(ks[7], (DEPTH, DEC_BATCH, H_C, HEAD_DIM, HEAD_DIM), 1.0),
        'meta_tokens': nrm(ks[8], (N_META, D_MODEL), 1.0),
        'norm_w': 1.0 + nrm(ks[9], (DEPTH, D_MODEL), 0.1),
        'w_in': nrm(ks[12], (DEPTH, D_MODEL, N_IN), D_MODEL ** -0.5),
        'mlstm_gate_b': mlstm_gate_b,
        'gdn_A_log': jnp.log(jax.random.uniform(ks[14], (DEPTH, H_B), f32, 1.0, 16.0)),
        'gdn_dt_bias': dt + jnp.log(-jnp.expm1(-dt)),
        'gdn_conv_w': nrm(ks[15], (DEPTH, CONV_W, 3 * W_B), CONV_W ** -0.5),
        'hgrn_lower_bounds': nrm(ks[16], (DEPTH, W_C), 1.0),
        'out_norm_w': 1.0 + nrm(ks[17], (DEPTH, MIX), 0.1),
        'w_out': nrm(ks[18], (DEPTH, MIX, D_MODEL), MIX ** -0.5),
        'final_norm_w': 1.0 + nrm(ks[19], (D_MODEL,), 0.1),
    }


def reference(x_prompt, x_sample, state_mlstm_C, state_mlstm_n, state_mlstm_m, state_gdn_S, state_gdn_conv,
              state_hgrn_S, meta_tokens, norm_w, w_in, mlstm_gate_b, gdn_A_log, gdn_dt_bias, gdn_conv_w,
              hgrn_lower_bounds, out_norm_w, w_out, final_norm_w):
    bsz, seq_len, _ = x_prompt.shape
    dec_seq = x_sample.shape[1]
    sm = jax.nn.softmax(hgrn_lower_bounds.astype(jnp.float32), axis=0)
    lb_all = jnp.cumsum(sm, axis=0) - sm[0]
    meta = jnp.broadcast_to(meta_tokens.astype(x_prompt.dtype)[None], (bsz, N_META, D_MODEL))
    xp = jnp.concatenate([meta, x_prompt], axis=1)
    xs = x_sample
    new_p = [[] for _ in range(6)]
    new_s = [[] for _ in range(6)]
    for l in range(DEPTH):
        params = (norm_w[l], w_in[l], mlstm_gate_b[l], gdn_A_log[l], gdn_dt_bias[l], gdn_conv_w[l], lb_all[l],
                  out_norm_w[l], w_out[l])
        xp, st_p = _layer(xp, _zero_state(bsz, xp.dtype), (N_META, seq_len), *params)
        past = (state_mlstm_C[l], state_mlstm_n[l], state_mlstm_m[l], state_gdn_S[l], state_gdn_conv[l],
                state_hgrn_S[l])
        xs, st_s = _layer(xs, past, (dec_seq,), *params)
        for j in range(6):
            new_p[j].append(st_p[j])
            new_s[j].append(st_s[j])
    y_prompt = _rmsnorm(xp[:, N_META:], final_norm_w)
    y_sample = _rmsnorm(xs, final_norm_w)
    mC_p, mn_p, mm_p, gS_p, gc_p, hS_p = [jnp.stack(a, axis=0) for a in new_p]
    mC_s, mn_s, mm_s, gS_s, gc_s, hS_s = [jnp.stack(a, axis=0) for a in new_s]
    return (y_prompt, y_sample, mC_p, mn_p, mm_p, gS_p, gc_p, hS_p, mC_s, mn_s, mm_s, gS_s, gc_s, hS_s)
```

```python
import contextlib
import numpy as np
import concourse.bass as bass
import concourse.mybir as mybir
from concourse.bass_utils import run_bass_kernel_spmd

F32 = mybir.dt.float32
BF16 = mybir.dt.bfloat16
I32 = mybir.dt.int32
AF = mybir.ActivationFunctionType
ALU = mybir.AluOpType
AX = mybir.AxisListType


class Tk:
    def __init__(self, name, psum=False):
        self.name = name
        self.psum = psum
        self.w = None
        self.r = []


class V:
    def __init__(self, tk, ap):
        self.tk = tk
        self.ap = ap

    def __getitem__(self, k):
        return V(self.tk, self.ap[k])

    def bc(self, shape):
        return V(self.tk, self.ap.broadcast_to(list(shape)))

    def un(self, axis):
        return V(self.tk, self.ap.unsqueeze(axis))

    def re(self, pat, **kw):
        return V(self.tk, self.ap.rearrange(pat, **kw))

    @property
    def shape(self):
        return self.ap.shape


class Sched:
    ENGS = ("pe", "act", "dve", "pool", "sp")
    ROT = 30000

    def __init__(self, nc):
        self.nc = nc
        self.es = contextlib.ExitStack()
        self.q = {e: [] for e in self.ENGS}
        self.cnt = {e: 0 for e in self.ENGS}
        self.nsem = 0
        self.sem = {e: self._newsem(e) for e in self.ENGS}
        self.waited = {e: {} for e in self.ENGS}
        self.dsem = {}
        self.n_ops = 0
        self.sb_off = 0
        self.sb_max = 0
        self.nalloc = 0
        self.rec = None
        self.offs = {"P": 16512}
        self.cur = "P"

    def split(self):
        r0 = self.offs["P"]
        self.offs["H"] = r0
        self.offs["O"] = r0

    def region(self, r):
        self.cur = r

    def _newsem(self, tag):
        self.nsem += 1
        return self.es.enter_context(self.nc.semaphore(f"s{self.nsem}_{tag}"))

    def sb(self, name, shape, dtype):
        nb = 2 if dtype == BF16 else 4
        n = 1
        for d in shape[1:]:
            n *= d
        size = (n * nb + 63) // 64 * 64
        off = self.offs[self.cur]
        self.offs[self.cur] = off + size
        self.sb_off = off + size
        self.sb_max = max(self.sb_max, self.sb_off)
        assert self.sb_off <= 229376, f"SBUF overflow at {name}: {self.sb_off} region {self.cur}"
        self.nalloc += 1
        t = self.nc.alloc_sbuf_tensor_at(f"{name}_{self.nalloc}", list(shape), dtype, offset=off)
        return V(Tk(name), t[:])

    def ps(self, name, shape, dtype=None):
        t = self.nc.alloc_psum_tensor(name, list(shape), F32)
        return V(Tk(name, psum=True), t[:])

    def I(self, eng, meth, **kw):
        reads, writes, args = [], [], {}
        for k, v in kw.items():
            if isinstance(v, V):
                if k in ("out", "accum_out") or v.tk.psum:
                    writes.append(v.tk)
                else:
                    reads.append(v.tk)
                args[k] = v.ap
            else:
                args[k] = v
        self.op(eng, lambda e: getattr(e, meth)(**args), reads, writes)

    def D(self, eng, out, in_, **kw):
        reads, writes = [], []
        names = []
        if isinstance(in_, V):
            reads.append(in_.tk)
            names.append(in_.tk.name)
            in_ = in_.ap
        if isinstance(out, V):
            writes.append(out.tk)
            names.append(out.tk.name)
            out = out.ap
        sbn = [n for n in names if not n.startswith("dram:")]
        key = sbn[0] if sbn else names[0]
        self.dma(eng, out, in_, reads, writes, key=key, **kw)

    def replay(self, lists, offset=0):
        self.rec = None
        idx = [0] * len(lists)
        step = 0
        while any(idx[k] < len(lists[k]) for k in range(len(lists))):
            for k, lst in enumerate(lists):
                if step < k * offset or idx[k] >= len(lst):
                    continue
                depth = 0
                while idx[k] < len(lst):
                    it = lst[idx[k]]
                    idx[k] += 1
                    if it[0] == "gs":
                        depth += 1
                    elif it[0] == "ge":
                        depth -= 1
                    elif it[0] == "op":
                        self.op(*it[1:])
                    else:
                        self.dma(it[1], it[2], it[3], it[4], it[5], key=it[6], **it[7])
                    if depth == 0:
                        break
            step += 1

    def barrier(self):
        evs = [(self.sem[e], self.cnt[e]) for e in self.ENGS if self.cnt[e] > 0]
        evs += [(sem, val) for (sem, val) in self.dsem.values() if val > 0]
        for e in self.ENGS:
            wd = self.waited[e]
            for (sem, val) in evs:
                if sem is self.sem[e] and e == "pe":
                    continue
                if wd.get(id(sem), (None, 0))[1] >= val:
                    continue
                wd[id(sem)] = (sem, val)
                self.q[e].append(("wait", sem, val))

    def _deps(self, eng, reads, writes):
        deps = []
        for t in reads:
            if t.w is not None:
                deps.append(t.w)
        for t in writes:
            if t.w is not None:
                deps.append(t.w)
            deps.extend(t.r)
        wd = self.waited[eng]
        need = {}
        for (sem, val, e2) in deps:
            if e2 == "pe" and eng == "pe":
                continue
            k = id(sem)
            if wd.get(k, (None, 0))[1] >= val:
                continue
            if k not in need or need[k][1] < val:
                need[k] = (sem, val)
        for k, (sem, val) in need.items():
            wd[k] = (sem, val)
            self.q[eng].append(("wait", sem, val))

    def _record(self, ev, reads, writes):
        for t in reads:
            t.r.append(ev)
        for t in writes:
            t.w = ev
            t.r = []

    def op(self, eng, fn, reads, writes):
        if self.rec is not None:
            self.rec.append(("op", eng, fn, reads, writes))
            return
        self._deps(eng, reads, writes)
        if self.cnt[eng] >= self.ROT:
            self.sem[eng] = self._newsem(eng)
            self.cnt[eng] = 0
        self.cnt[eng] += 1
        ev = (self.sem[eng], self.cnt[eng], eng)
        self.q[eng].append(("op", fn, self.sem[eng], 1))
        self._record(ev, reads, writes)
        self.n_ops += 1

    def dma(self, eng, out, in_, reads, writes, key=None, **kw):
        if self.rec is not None:
            self.rec.append(("dma", eng, out, in_, reads, writes, key, kw))
            return
        self._deps(eng, reads, writes)
        if key is None:
            key = (writes[0] if writes else reads[0]).name
        if key not in self.dsem:
            self.dsem[key] = [self._newsem("d"), 0]
        ds = self.dsem[key]
        ds[1] += 16
        ev = (ds[0], ds[1], "dma")
        self.q[eng].append(("op", (lambda e, out=out, in_=in_, kw=kw: e.dma_start(out=out, in_=in_, **kw)), ds[0], 16))
        self._record(ev, reads, writes)
        self.n_ops += 1

    def finish(self):
        for key, (sem, val) in self.dsem.items():
            if val > 0:
                self.q["sp"].append(("wait", sem, val))
        for e in self.ENGS:
            if e != "sp" and self.cnt[e] > 0:
                self.q["sp"].append(("wait", self.sem[e], self.cnt[e]))
        nc = self.nc
        q = self.q

        def emit(engine, lst):
            for it in lst:
                if it[0] == "wait":
                    engine.wait_ge(it[1], it[2])
                else:
                    ins = it[1](engine)
                    ins.then_inc(it[2], it[3])

        with nc.allow_non_contiguous_dma(reason="small strided state/param transfers"), nc.Block() as block:
            @block.tensor
            def _(e):
                emit(e, q["pe"])

            @block.scalar
            def _(e):
                emit(e, q["act"])

            @block.vector
            def _(e):
                emit(e, q["dve"])

            @block.gpsimd
            def _(e):
                emit(e, q["pool"])

            @block.sync
            def _(e):
                emit(e, q["sp"])
        self.es.close()


D = 2048
NCH = 16
NP = 2064
NS = 128
NT = NP + NS
N_IN = 8984
A_Q, A_K, A_V, A_O, A_I, A_F = 0, 768, 1536, 2304, 3072, 3078
B_Q, B_K, B_V, B_A, B_B = 3084, 3852, 4620, 5388, 5394
C_Q, C_F, C_I, Z0 = 5400, 5912, 6424, 6936
EPS = 1e-6
RS = 128 ** -0.5
LANE_OFFSET = 110
_blk_cols = ([A_Q + i * 128 for i in range(24)] + [B_Q + i * 128 for i in range(18)]
             + [C_Q + i * 128 for i in range(12)] + [Z0 + i * 128 for i in range(16)])
BLK_OF = {c: i for i, c in enumerate(_blk_cols)}
_gate_cols = list(range(A_I, A_I + 12)) + list(range(B_A, B_A + 12))
GC_OF = {c: i for i, c in enumerate(_gate_cols)}
MASK_IDX = {"U128": 0, "U64": 1, "SU64": 2, "U16": 3, "U8": 4, "SU8": 5}
RESET_IDX = {128: 0, 64: 1, 16: 2, 8: 3}
RM_OFF = {64: (0, 2), 16: (2, 8), 8: (10, 16)}


def host_consts():
    idx = np.arange(128)
    masks = np.zeros((6, 128, 128), np.float32)
    for name, gs, strict in (("U128", 128, False), ("U64", 64, False), ("SU64", 64, True),
                             ("U16", 16, False), ("U8", 8, False), ("SU8", 8, True)):
        same = (idx[:, None] // gs) == (idx[None, :] // gs)
        tri = (idx[:, None] < idx[None, :]) if strict else (idx[:, None] <= idx[None, :])
        masks[MASK_IDX[name]] = (same & tri).astype(np.float32)
    resets = np.ones((4, 128, 128), np.float32)
    for gs, i in RESET_IDX.items():
        resets[i][:, (idx % gs) == 0] = 0.0
    rm = np.zeros((128, 26), np.float32)
    for gs, (o, g) in RM_OFF.items():
        for j in range(g):
            rm[(idx // gs) == j, o + j] = 1.0
    return {"c_ident": np.eye(128, dtype=np.float32), "c_masks": masks, "c_resets": resets, "c_rm": rm}


def make_tiles():
    tiles = [dict(kind="p", t0=0, c=16, gA=16, gB=16, gC=16, first=True, last=False)]
    for i in range(16):
        tiles.append(dict(kind="p", t0=16 + 128 * i, c=128, gA=128, gB=64, gC=16, first=False, last=(i == 15)))
    tiles.append(dict(kind="s", t0=NP, c=128, gA=8, gB=8, gC=8, first=True, last=True))
    return tiles


def build_nc(n_layers=2, heads=None, do_out=True, debug=False):
    nc = bass.Bass("TRN2", target_bir_lowering=False)
    S = Sched(nc)

    def din(name, shape):
        return nc.dram_tensor(name, list(shape), F32, kind="ExternalInput").ap()

    def dout(name, shape):
        return nc.dram_tensor(name, list(shape), F32, kind="ExternalOutput").ap()

    xp = din("xp", [NP, D]); xs = din("xs", [NS, D])
    w_in = din("w_in", [2, 70, 128, NCH * 128]); w_gc = din("w_gc", [2, 128, NCH, 24]); w_out = din("w_out", [2, D, D])
    norm_w = din("norm_w", [2, D]); out_norm_w = din("out_norm_w", [2, D]); final_norm_w = din("final_norm_w", [D])
    gate_b = din("mlstm_gate_b", [2, 2, 6]); A_log = din("gdn_A_log", [2, 6]); dt_bias = din("gdn_dt_bias", [2, 6])
    conv_w = din("gdn_conv_w", [2, 4, 2304]); lbp = din("hgrn_lower_bounds", [2, 512])
    sC = din("sC", [2, 16, 6, 128, 128]); sn = din("sn", [2, 16, 6, 128]); sm = din("sm", [2, 16, 6])
    sS = din("sS", [2, 16, 6, 128, 128]); sconv = din("sconv", [2, 16, 3, 2304]); sH = din("sH", [2, 16, 4, 128, 128])
    c_ident = din("c_ident", [128, 128]); c_masks = din("c_masks", [6, 128, 128])
    c_resets = din("c_resets", [4, 128, 128]); c_rm = din("c_rm", [128, 26])

    yp = dout("yp", [NP, D]); ys = dout("ys", [NS, D])
    pC = dout("pC", [2, 6, 128, 128]); pn = dout("pn", [2, 6, 128]); pm = dout("pm", [2, 6])
    pS = dout("pS", [2, 6, 128, 128]); pconv = dout("pconv", [2, 3, 2304]); pH = dout("pH", [2, 4, 128, 128])
    oC = dout("oC", [2, 16, 6, 128, 128]); on = dout("on", [2, 16, 6, 128]); om = dout("om", [2, 16, 6])
    oS = dout("oS", [2, 16, 6, 128, 128]); oconv = dout("oconv", [2, 16, 3, 2304]); oH = dout("oH", [2, 16, 4, 128, 128])

    skind = "ExternalOutput" if debug else "Internal"
    xres_t = nc.dram_tensor("xres", [NCH, 128, NT], F32, kind=skind).ap()
    yT_t = nc.dram_tensor("yTs", [16, 128, NT], BF16, kind=skind).ap()
    xres = V(Tk("dram:xres"), xres_t)
    yTd = V(Tk("dram:yT"), yT_t)

    TILES = make_tiles()

    def ACT(out, in_, func, scale=1.0, bias=None):
        kw = dict(out=out, in_=in_, func=func, scale=scale)
        if bias is not None:
            kw["bias"] = bias
        S.I("act", "activation", **kw)

    def TT(out, in0, in1, op, eng="dve"):
        S.I(eng, "tensor_tensor", out=out, in0=in0, in1=in1, op=op)

    def TS(out, in0, s1, op0, s2=None, op1=None, eng="dve"):
        if op1 is None:
            S.I(eng, "tensor_scalar", out=out, in0=in0, scalar1=s1, scalar2=None, op0=op0)
        else:
            S.I(eng, "tensor_scalar", out=out, in0=in0, scalar1=s1, scalar2=s2, op0=op0, op1=op1)

    def STT(out, in0, scalar, in1, op0, op1):
        S.I("dve", "scalar_tensor_tensor", out=out, in0=in0, scalar=scalar, in1=in1, op0=op0, op1=op1)

    def MM(out, lhsT, rhs, start=True, stop=True):
        S.I("pe", "matmul", out=out, lhsT=lhsT, rhs=rhs, start=start, stop=stop)

    def CP(out, in_, eng="act"):
        if eng == "act":
            S.I("act", "copy", out=out, in_=in_)
        else:
            S.I(eng, "tensor_copy", out=out, in_=in_)

    def RECIP(out, in_):
        ACT(out, in_, AF.Ln)
        ACT(out, out, AF.Exp, scale=-1.0)

    def RECIP1P(out, in_):
        ACT(out, in_, AF.Ln, bias=c_one[0:out.shape[0], :])
        ACT(out, out, AF.Exp, scale=-1.0)

    rings = {}
    cur = [-1]
    LANES = []

    slots = {}

    def tmp(tag, shape, dtype, n=2):
        lane_tag = tag[:2] in ("a_", "b_", "c_") or tag[:3] == "yf_"
        if lane_tag:
            n = 1
        if tag[:2] in ("a_", "b_", "c_"):
            k = (tag[:2], tuple(shape), str(dtype), n)
            d = slots.setdefault(k, {})
            if tag not in d:
                d[tag] = len(d)
            tag = f"mix_{tuple(shape)}_{dtype}_{n}_{d[tag]}"
        if lane_tag:
            tag = f"{tag}@{max(cur[0], 0)}"
        if tag not in rings:
            rings[tag] = [[S.sb(f"{tag}{i}", shape, dtype) for i in range(n)], 0]
        r = rings[tag]
        v = r[0][r[1] % n]
        r[1] += 1
        return v

    ident = S.sb("ident", [128, 128], F32)
    masks = S.sb("masks", [128, 6, 128], F32)
    resets = S.sb("resets", [128, 4, 128], F32)
    rmk = S.sb("rmk", [128, 26], F32)
    ones_f = S.sb("ones_f", [128, 128], F32)
    ones_b = S.sb("ones_b", [128, 128], BF16)
    c_one = S.sb("c_one", [128, 1], F32)
    c_eps = S.sb("c_eps", [128, 1], F32)
    S.D("sp", ident, c_ident)
    S.D("sp", masks, c_masks.rearrange("m p n -> p m n"))
    S.D("sp", resets, c_resets.rearrange("m p n -> p m n"))
    S.D("sp", rmk, c_rm)
    S.I("dve", "memset", ap=ones_f, constant=1.0) if False else S.op("dve", lambda e: e.memset(ones_f.ap, 1.0), [], [ones_f.tk])
    S.op("dve", lambda e: e.memset(ones_b.ap, 1.0), [], [ones_b.tk])
    S.op("dve", lambda e: e.memset(c_one.ap, 1.0), [], [c_one.tk])
    S.op("dve", lambda e: e.memset(c_eps.ap, EPS), [], [c_eps.tk])

    def mask(name, c):
        return masks[:c, MASK_IDX[name], :c]

    def TR(out, in_):
        k = in_.shape[0]
        S.I("pe", "transpose", out=out, in_=in_, identity=ident[:k, :k])

    pb = [S.ps(f"pb{i}", [128, 512]) for i in range(8)]
    pj_sets = [(pb[0], pb[1]), (pb[2], pb[3])]
    pring = [0]

    def pbank():
        if cur[0] < 0:
            b = pb[4 + pring[0] % 4]
            pring[0] += 1
            return b
        L = LANES[cur[0]]
        b = L["ring"][L["rp"] % 2]
        L["rp"] += 1
        return b

    gb_row = S.sb("gb_row", [1, 24], F32)
    ngb_row = S.sb("ngb_row", [1, 24], F32)
    al_row = S.sb("al_row", [1, 12], F32)
    dtb_row = S.sb("dtb_row", [1, 12], F32)
    S.D("sp", gb_row, gate_b.rearrange("l w h -> (l w h)").unsqueeze(0))
    S.D("sp", al_row, A_log.rearrange("l h -> (l h)").unsqueeze(0))
    S.D("sp", dtb_row, dt_bias.rearrange("l h -> (l h)").unsqueeze(0))
    TS(ngb_row, gb_row, -1.0, ALU.mult)
    ACT(al_row, al_row, AF.Exp)
    lbraw = S.sb("lbraw", [128, 2, 4], F32)
    S.D("sp", lbraw, lbp.rearrange("l (h p) -> p l h", p=128))
    lb = S.sb("lb", [128, 2, 4], F32)
    oml = S.sb("oml", [128, 2, 4], F32)
    S.op("dve", lambda e: e.memset(lb.ap, 0.0), [], [lb.tk])
    TT(lb[:, 1, :], lbraw[:, 0, :], lbraw[:, 1, :], ALU.subtract)
    ACT(lb[:, 1, :], lb[:, 1, :], AF.Exp)
    RECIP1P(lb[:, 1, :], lb[:, 1, :])
    TS(oml, lb, -1.0, ALU.mult, 1.0, ALU.add)
    nw = S.sb("nw", [128, 2, NCH], F32)
    onw = S.sb("onw", [128, 2, NCH], F32)
    fnw = S.sb("fnw", [128, NCH], F32)
    S.D("sp", nw, norm_w.rearrange("l (j p) -> p l j", p=128))
    S.D("sp", onw, out_norm_w.rearrange("l (j p) -> p l j", p=128))
    S.D("sp", fnw, final_norm_w.rearrange("(j p) -> p j", p=128))
    cw = S.sb("cw", [128, 2, 18, 4], F32)
    for l_ in range(2):
        for j_ in range(4):
            S.D("sp", cw[:, l_, :, j_], conv_w[l_, j_, :].rearrange("(b p) -> p b", p=128))

    xnT = S.sb("xnT", [128, NCH, NT], BF16)

    def finish_x(ti, tile, xT, layer_next):
        c, t0 = tile["c"], tile["t0"]
        if layer_next < n_layers:
            S.D("sp", xrv[ti].re("j p t -> p j t"), xT[:, :, :c])
        sq = tmp("fx_sq", [128, NCH, 128], BF16, 1)
        S.I("act", "activation", out=sq[:, :, :c], in_=xT[:, :, :c], func=AF.Square)
        ps = pbank()
        for j in range(NCH):
            MM(ps[:, :c], ones_b, sq[:, j, :c], start=(j == 0), stop=(j == NCH - 1))
        rstd = tmp("fx_rstd", [128, 128], F32, 2)
        ACT(rstd[:, :c], ps[:, :c], AF.Ln, scale=1.0 / D, bias=c_eps)
        ACT(rstd[:, :c], rstd[:, :c], AF.Exp, scale=-0.5)
        t1 = tmp("fx_t1", [128, NCH, 128], F32, 1)
        TT(t1[:, :, :c], xT[:, :, :c], rstd[:, :c].un(1).bc([128, NCH, c]), ALU.mult)
        if layer_next < n_layers:
            TT(xnT[:, :, t0:t0 + c], t1[:, :, :c], nw[:, layer_next, :].un(2).bc([128, NCH, c]), ALU.mult)
        else:
            TT(t1[:, :, :c], t1[:, :, :c], fnw.un(2).bc([128, NCH, c]), ALU.mult)
            ot = tmp("sa_x", [128, D], F32, 1)
            for q4 in range(4):
                pt = pbank()
                for jj in range(4):
                    j = q4 * 4 + jj
                    TR(pt[:c, jj * 128:(jj + 1) * 128], t1[:, j, :c])
                CP(ot[:c, q4 * 512:(q4 + 1) * 512], pt[:c, :])
            dst = yp[t0:t0 + c, :] if tile["kind"] == "p" else ys[:, :]
            S.D("sp", dst, ot[:c, :])

    def stage_a():
        S.region("O")
        for ti, tile in enumerate(TILES):
            c, t0 = tile["c"], tile["t0"]
            xt = tmp("sa_x", [128, D], F32, 1)
            src = xp[t0:t0 + c, :] if tile["kind"] == "p" else xs[:, :]
            S.D("sp", xt[:c, :], src)
            xT = tmp("xT", [128, NCH, 128], F32, 2)
            for q4 in range(4):
                pt = pbank()
                for jj in range(4):
                    j = q4 * 4 + jj
                    TR(pt[:, jj * 128:jj * 128 + c], xt[:c, j * 128:(j + 1) * 128])
                CP(xT[:, q4 * 4:(q4 + 1) * 4, :c], pt.re("p (a b) -> p a b", a=4)[:, :, :c])
            finish_x(ti, tile, xT, 0)


    S.split()
    S.region("H")
    sA = [S.sb("sA0", [128, 16, 129], F32)]
    sSb = S.sb("sSb", [128, 16, 128], BF16)
    sB = [sA[0][:, :, 0:128]]
    sBb = sSb
    sCc = [sA[0][:, :, 0:128]]
    sCb = sSb
    mrow_s = S.sb("mrow_s", [1, 16], F32)
    ue_s = [S.sb(f"ue_s{i}", [128, 16, 11], F32) for i in range(3)]
    Cb = S.sb("Cb", [128, 16, 128], BF16)
    nbb = S.sb("nbb", [128, 16, 128], BF16)
    for i_ in range(2):
        bk = pb[4 * i_:4 * i_ + 4]
        LANES.append(dict(
            i=i_, PJ=bk[0], X=bk[1], ring=[bk[2], bk[3]], rp=0,
            wring=[S.sb(f"wblk{i_}_{k}", [128, NCH, 128], BF16) for k in range(5)], wp=0,
            wg=S.sb(f"wg{i_}", [128, NCH, 2], BF16),
            pA=S.sb(f"pA{i_}", [128, 1, 129], F32), mrow_p=S.sb(f"mrow_p{i_}", [1, 16], F32),
            pBs=S.sb(f"pBs{i_}", [128, 1, 128], F32), pBb=S.sb(f"pBb{i_}", [128, 1, 128], BF16),
            pCs=S.sb(f"pCs{i_}", [128, 1, 128], F32), pCb=S.sb(f"pCb{i_}", [128, 1, 128], BF16),
            ue_p=[S.sb(f"ue_p{i_}_{k}", [128, 1, 131], F32) for k in range(3)],
            Cbp=S.sb(f"Cbp{i_}", [128, 1, 128], BF16), nbp=S.sb(f"nbp{i_}", [128, 1, 128], BF16),
        ))

    def wblock(l, col0):
        L = LANES[cur[0]]
        wb = L["wring"][L["wp"] % 5]
        L["wp"] += 1
        S.D("pool", wb, w_in[l, BLK_OF[col0], :, :].rearrange("p (j n) -> p j n", j=NCH))
        return wb

    def wgate(l, col_a, col_b):
        wg = LANES[cur[0]]["wg"]
        S.D("pool", wg[:, :, 0:1], w_gc[l, :, :, GC_OF[col_a]:GC_OF[col_a] + 1])
        S.D("pool", wg[:, :, 1:2], w_gc[l, :, :, GC_OF[col_b]:GC_OF[col_b] + 1])
        return wg

    def proj(tile, wbs, wg):
        if S.rec is not None:
            S.rec.append(("gs",))
        r = proj_(tile, wbs, wg)
        if S.rec is not None:
            S.rec.append(("ge",))
        return r

    def proj_(tile, wbs, wg):
        c, t0 = tile["c"], tile["t0"]
        L = LANES[cur[0]]
        outs = []
        for b, wb in enumerate(wbs):
            bank = L["PJ"] if b < 4 else L["X"]
            o = bank[:, (b % 4) * 128:(b % 4) * 128 + c]
            for j in range(NCH):
                MM(o, wb[:, j, :], xnT[:, j, t0:t0 + c], start=(j == 0), stop=(j == NCH - 1))
            outs.append(o)
        gr = []
        if wg is not None:
            for i in range(2):
                o = L["X"][0:1, 256 + i * 128:256 + i * 128 + c]
                for j in range(NCH):
                    MM(o, wg[:, j, i:i + 1], xnT[:, j, t0:t0 + c], start=(j == 0), stop=(j == NCH - 1))
                gr.append(o)
        return outs, gr

    def scan(out, msk, data):
        S.I("dve", "tensor_tensor_scan", out=out, data0=msk, data1=data, initial=0.0, op0=ALU.mult, op1=ALU.add)

    def T2(tag, dtype=F32, n=2):
        return tmp(tag, [128, 128], dtype, n)

    def R1(tag, w=128, n=2):
        return tmp(tag, [1, w], F32, n)

    def g3(v, c, G):
        return v[:, :c].re("p (a b) -> p a b", a=G)

    def y_finalize(l, hg, ti, tile, hT, z_ps):
        c, t0 = tile["c"], tile["t0"]
        ez = T2("yf_ez")
        ACT(ez[:, :c], z_ps, AF.Exp, scale=-1.0)
        RECIP1P(ez[:, :c], ez[:, :c])
        zs = T2("yf_zs")
        STT(zs[:, :c], z_ps, onw[:, l, hg:hg + 1], ez[:, :c], ALU.mult, ALU.mult)
        sq = T2("yf_sq", BF16)
        ACT(sq[:, :c], hT[:, :c], AF.Square)
        ps = pbank()
        MM(ps[:, :c], ones_b, sq[:, :c])
        rstd = T2("yf_rstd")
        ACT(rstd[:, :c], ps[:, :c], AF.Ln, scale=1.0 / 128, bias=c_eps)
        ACT(rstd[:, :c], rstd[:, :c], AF.Exp, scale=-0.5)
        y32 = T2("yf_y32")
        TT(y32[:, :c], hT[:, :c], rstd[:, :c], ALU.mult)
        yb = T2("yf_yb", BF16)
        TT(yb[:, :c], y32[:, :c], zs[:, :c], ALU.mult)
        S.D("sp", yTv[ti][hg, :, :], yb[:, :c])

    yTv = [V(Tk(f"dram:yT{ti}"), yT_t[:, :, t["t0"]:t["t0"] + t["c"]]) for ti, t in enumerate(TILES)]
    xrv = [V(Tk(f"dram:xr{ti}"), xres_t[:, :, t["t0"]:t["t0"] + t["c"]]) for ti, t in enumerate(TILES)]

    def mlstm_setup(l, h):
        LN = LANES[cur[0]]
        LN["wp"] = 0
        pA = LN["pA"]
        mrow_p = LN["mrow_p"]
        wbs = [wblock(l, A_Q + h * 128), wblock(l, A_K + h * 128), wblock(l, A_V + h * 128),
               wblock(l, A_O + h * 128), wblock(l, Z0 + h * 128)]
        wg = wgate(l, A_I + h, A_F + h)
        S.op("dve", lambda e: e.memset(pA.ap, 0.0), [], [pA.tk])
        S.op("dve", lambda e: e.memset(mrow_p.ap, 0.0), [], [mrow_p.tk])
        return wbs, wg

    def mlstm_tile(l, h, ti, tile, pj, prefetch):
        LN = LANES[cur[0]]
        pA = LN["pA"]
        mrow_p = LN["mrow_p"]
        c, gs, t0 = tile["c"], tile["gA"], tile["t0"]
        G = c // gs
        smp = tile["kind"] == "s"
        if smp:
            St = sA[0]
            S.D("sp", St[:, :, 0:128], sC[l, :, h, :, :].rearrange("g d e -> d g e"))
            S.D("sp", St[:, :, 128:129], sn[l, :, h, :].rearrange("g d -> d g").unsqueeze(2))
            mrow = mrow_s
            S.D("sp", mrow, sm[l, :, h].unsqueeze(0))
        else:
            St, mrow = pA, mrow_p
        (q_ps, k_ps, v_ps, o_ps, z_ps), (gi_ps, gf_ps) = pj
        qT = T2("a_qT", BF16); CP(qT[:, :c], q_ps)
        kTf = T2("a_kTf"); S.I("act", "mul", out=kTf[:, :c], in_=k_ps, mul=RS)
        kT = T2("a_kT", BF16); CP(kT[:, :c], kTf[:, :c], eng="dve")
        vTf = T2("a_vTf"); CP(vTf[:, :c], v_ps)
        eo = T2("a_eo"); ACT(eo[:, :c], o_ps, AF.Exp, scale=-1.0)
        zf = T2("a_zf"); CP(zf[:, :c], z_ps)
        li = R1("a_li"); TS(li[:, :c], gi_ps, gb_row[0:1, l * 12 + h:l * 12 + h + 1], ALU.add)
        e1 = R1("a_e1"); ACT(e1[:, :c], gf_ps, AF.Exp, scale=-1.0, bias=ngb_row[0:1, l * 12 + 6 + h:l * 12 + 7 + h])
        prefetch()
        yield
        sp_ = R1("a_sp"); ACT(sp_[:, :c], e1[:, :c], AF.Ln, bias=c_one[0:1, :])
        Fn = R1("a_Fn"); scan(Fn[:, :c], resets[0:1, RESET_IDX[gs], :c], sp_[:, :c])
        gg = R1("a_g"); TT(gg[:, :c], li[:, :c], Fn[:, :c], ALU.add)
        gmax = R1("a_gmax", 16)
        S.I("dve", "tensor_reduce", out=gmax[:, :G], in_=g3(gg, c, G), axis=AX.X, op=ALU.max)
        Mb = R1("a_Mb", 16); TT(Mb[:, :G], gmax[:, :G], mrow[:, :G], ALU.max)
        al = R1("a_al", 16); TT(al[:, :G], mrow[:, :G], Mb[:, :G], ALU.subtract)
        ACT(al[:, :G], al[:, :G], AF.Exp)
        TT(mrow[:, :G], Mb[:, :G], g3(Fn, c, G)[:, :, gs - 1], ALU.subtract)
        w = R1("a_w"); TT(g3(w, c, G), g3(gg, c, G), Mb[:, :G].un(2).bc([1, G, gs]), ALU.subtract)
        ACT(w[:, :c], w[:, :c], AF.Exp)
        thr = R1("a_thr"); TT(g3(thr, c, G), g3(Fn, c, G), Mb[:, :G].un(2).bc([1, G, gs]), ALU.subtract)
        yield
        pa = pbank()
        MM(pa[:, 0:c], ones_f[0:1, :], thr[:, :c])
        MM(pa[:, 128:128 + G], ones_f[0:1, :], al[:, :G])
        MM(pa[:c, 256:257], w[:, :c], ones_f[0:1, 0:1])
        thrS = T2("a_thrS"); ACT(thrS[:, :c], pa[:, 0:c], AF.Exp)
        ab = tmp("a_ab", [128, 16], F32); CP(ab[:, :G], pa[:, 128:128 + G])
        wcol = tmp("a_wcol", [128, 1], F32); CP(wcol[:c, :], pa[:c, 256:257])
        yield
        pbt = pbank()
        TR(pbt[:c, 0:128], vTf[:, :c])
        TR(pbt[:c, 128:256], kTf[:, :c])
        vp = tmp("a_vp", [128, 129], BF16)
        TS(vp[:c, 0:128], pbt[:c, 0:128], wcol[:c, 0:1], ALU.mult)
        CP(vp[:c, 128:129], wcol[:c, :], eng="dve")
        kt = T2("a_kt", BF16); CP(kt[:c, :], pbt[:c, 128:256])
        wbc = T2("a_wbc", BF16); TS(wbc[:c, :], ones_f[:c, :], wcol[:c, 0:1], ALU.mult)
        yield
        pc = pbank(); MM(pc[:c, :c], kT[:, :c], qT[:, :c])
        PT = T2("a_PT", BF16); TT(PT[:c, :c], pc[:c, :c], mask("U8" if smp else "U128", c), ALU.mult)
        yield
        Cb_, nb_ = (Cb, nbb) if smp else (LN["Cbp"], LN["nbp"])
        for g in range(G):
            ACT(Cb_[:, g, :], St[:, g, 0:128], AF.Copy, scale=ab[:, g:g + 1])
            ACT(nb_[:, g, :], St[:, g, 128:129].bc([128, 128]), AF.Copy, scale=ab[:, g:g + 1])
        pd_ = pbank()
        MM(pd_[:, :c], wbc[:c, :], PT[:c, :c], start=True, stop=False)
        for g in range(G):
            MM(pd_[:, g * gs:(g + 1) * gs], nb_[:, g, :], qT[:, g * gs:(g + 1) * gs], start=False, stop=(g == G - 1))
        denS = T2("a_denS"); CP(denS[:, :c], pd_[:, :c])
        dmax = T2("a_dmax"); STT(dmax[:, :c], denS[:, :c], -1.0, denS[:, :c], ALU.mult, ALU.max)
        TT(dmax[:, :c], dmax[:, :c], thrS[:, :c], ALU.max)
        yield
        RECIP(dmax[:, :c], dmax[:, :c])
        pn_ = pbank()
        MM(pn_[:, :c], vp[:c, 0:128], PT[:c, :c], start=True, stop=False)
        for g in range(G):
            MM(pn_[:, g * gs:(g + 1) * gs], Cb_[:, g, :], qT[:, g * gs:(g + 1) * gs], start=False, stop=(g == G - 1))
        hT = T2("a_hT"); TT(hT[:, :c], pn_[:, :c], dmax[:, :c], ALU.mult)
        RECIP1P(eo[:, :c], eo[:, :c])
        TT(hT[:, :c], hT[:, :c], eo[:, :c], ALU.mult)
        y_finalize(l, h, ti, tile, hT, zf[:, :c])
        yield
        if G > 1:
            vpm = tmp("a_vpm", [128, 16, 129], BF16, 1)
            o_, g_ = RM_OFF[gs]
            TT(vpm[:c, :G, :], vp[:c, :].un(1).bc([c, G, 129]), rmk[:c, o_:o_ + G].un(2).bc([c, G, 129]), ALU.mult)
        for g in range(G):
            pu = pbank()
            MM(pu[:, 0:129], kt[:c, :], vpm[:c, g, :] if G > 1 else vp[:c, :])
            STT(St[:, g, :], St[:, g, :], ab[:, g:g + 1], pu[:, 0:129], ALU.mult, ALU.add)
            yield
        if smp:
            S.D("sp", oC[l, :, h, :, :].rearrange("g d e -> d g e"), St[:, :, 0:128])
            S.D("sp", on[l, :, h, :].rearrange("g d -> d g").unsqueeze(2), St[:, :, 128:129])
            S.D("sp", om[l, :, h].unsqueeze(0), mrow)
        elif tile["last"]:
            S.D("sp", pC[l, h, :, :], St[:, 0, 0:128])
            S.D("sp", pn[l, h, :].unsqueeze(1), St[:, 0, 128:129])
            S.D("sp", pm[l, h:h + 1].unsqueeze(0), mrow[:, 0:1])
        yield

    import math

    def gdn_setup(l, h):
        LN = LANES[cur[0]]
        LN["wp"] = 0
        pBs = LN["pBs"]
        pBb = LN["pBb"]
        ue_p = LN["ue_p"]
        wbs = [wblock(l, B_Q + h * 128), wblock(l, B_K + h * 128), wblock(l, B_V + h * 128), wblock(l, Z0 + (6 + h) * 128)]
        wg = wgate(l, B_A + h, B_B + h)
        S.op("dve", lambda e: e.memset(pBs.ap, 0.0), [], [pBs.tk])
        S.op("dve", lambda e: e.memset(pBb.ap, 0.0), [], [pBb.tk])
        for i in range(3):
            S.op("dve", lambda e, i=i: e.memset(ue_p[i].ap, 0.0), [], [ue_p[i].tk])
        return wbs, wg

    def gdn_tile(l, h, ti, tile, pj, prefetch):
        LN = LANES[cur[0]]
        pBs = LN["pBs"]
        pBb = LN["pBb"]
        ue_p = LN["ue_p"]
        c, gs, t0 = tile["c"], tile["gB"], tile["t0"]
        G = c // gs
        smp = tile["kind"] == "s"
        nseq, L = (16, 8) if smp else (1, c)
        if smp:
            Sf, Sb_ = sB[0], sBb
            S.D("sp", Sf, sS[l, :, h, :, :].rearrange("g d e -> d g e"))
            CP(Sb_, Sf)
            for i in range(3):
                ch0 = i * 768 + h * 128
                for j_ in range(3):
                    S.D("sp", ue_s[i][:, :, j_], sconv[l, :, j_, ch0:ch0 + 128].rearrange("g p -> p g"))
        else:
            Sf, Sb_ = pBs, pBb
        (q_ps, k_ps, v_ps, z_ps), (ga_ps, gb_ps) = pj
        for i, p_ in enumerate((q_ps, k_ps, v_ps)):
            ue = ue_s[i] if smp else ue_p[i]
            CP(ue[:, :, 3:3 + L], p_.re("p (a b) -> p a b", a=nseq))
        zf = T2("b_zf"); CP(zf[:, :c], z_ps)
        xa = R1("b_xa"); TS(xa[:, :c], ga_ps, dtb_row[0:1, l * 6 + h:l * 6 + h + 1], ALU.add)
        beta = R1("b_beta"); ACT(beta[:, :c], gb_ps, AF.Exp, scale=-1.0)
        prefetch()
        yield
        cs = []
        for i, p_ in enumerate((q_ps, k_ps, v_ps)):
            ue = ue_s[i] if smp else ue_p[i]
            bidx = i * 6 + h
            cv = T2(f"b_cv{i}")
            cvv = cv[:, :c].re("p (a b) -> p a b", a=nseq)
            TS(cvv, ue[:, :, 0:L], cw[:, l, bidx, 0:1], ALU.mult)
            for j in range(1, 4):
                STT(cvv, ue[:, :, j:j + L], cw[:, l, bidx, j:j + 1], cvv, ALU.mult, ALU.add)
            ex = T2("b_ex")
            ACT(ex[:, :c], cv[:, :c], AF.Exp, scale=-1.0)
            RECIP1P(ex[:, :c], ex[:, :c])
            c_ = T2(f"b_cs{i}")
            TT(c_[:, :c], cv[:, :c], ex[:, :c], ALU.mult)
            cs.append(c_)
            yield
            if smp:
                ch0 = i * 768 + h * 128
                for j_ in range(3):
                    S.D("sp", oconv[l, :, j_, ch0:ch0 + 128].rearrange("g p -> p g"), ue[:, :, 8 + j_])
            else:
                CP(ue[:, :, 0:3], ue[:, :, L:L + 3])
                if tile["last"]:
                    ch0 = i * 768 + h * 128
                    S.D("sp", pconv[l, :, ch0:ch0 + 128].rearrange("j p -> p j"), ue[:, 0, 0:3])
        nf = []
        for i in range(2):
            sq = T2("b_sq", BF16); ACT(sq[:, :c], cs[i][:, :c], AF.Square)
            ps = pbank(); MM(ps[:, :c], ones_b, sq[:, :c])
            rn = T2("b_rn"); ACT(rn[:, :c], ps[:, :c], AF.Ln, bias=c_eps)
            ACT(rn[:, :c], rn[:, :c], AF.Exp, scale=-0.5)
            f_ = T2(f"b_nf{i}")
            if i == 0:
                STT(f_[:, :c], cs[0][:, :c], RS, rn[:, :c], ALU.mult, ALU.mult)
            else:
                TT(f_[:, :c], cs[1][:, :c], rn[:, :c], ALU.mult)
            nf.append(f_)
        qf, kf = nf
        yield
        ACT(xa[:, :c], xa[:, :c], AF.Exp)
        ACT(xa[:, :c], xa[:, :c], AF.Ln, bias=c_one[0:1, :])
        gn = R1("b_gn"); TS(gn[:, :c], xa[:, :c], al_row[0:1, l * 6 + h:l * 6 + h + 1], ALU.mult)
        Gn = R1("b_Gn"); scan(Gn[:, :c], resets[0:1, RESET_IDX[gs], :c], gn[:, :c])
        RECIP1P(beta[:, :c], beta[:, :c])
        eG = R1("b_eG"); ACT(eG[:, :c], Gn[:, :c], AF.Exp, scale=-1.0)
        eGe = R1("b_eGe", 16); ACT(eGe[:, :G], g3(Gn, c, G)[:, :, gs - 1], AF.Exp, scale=-1.0)
        df = R1("b_df"); TT(g3(df, c, G), g3(Gn, c, G), g3(Gn, c, G)[:, :, gs - 1:gs].bc([1, G, gs]), ALU.subtract)
        ACT(df[:, :c], df[:, :c], AF.Exp)
        bg = R1("b_bg"); TT(bg[:, :c], beta[:, :c], eG[:, :c], ALU.mult)
        Gr = R1("b_Gr"); TS(Gr[:, :c], Gn[:, :c], -1.0, ALU.mult)
        yield
        pa = pbank()
        MM(pa[:, 0:c], ones_f[0:1, :], beta[:, :c])
        MM(pa[:, 128:128 + c], ones_f[0:1, :], eG[:, :c])
        MM(pa[:, 256:256 + G], ones_f[0:1, :], eGe[:, :G])
        MM(pa[:c, 384:385], beta[:, :c], ones_f[0:1, 0:1])
        MM(pa[:c, 385:386], bg[:, :c], ones_f[0:1, 0:1])
        MM(pa[:c, 386:387], df[:, :c], ones_f[0:1, 0:1])
        bb = tmp("b_bb", [128, 512], F32)
        CP(bb[:, 0:128 + c], pa[:, 0:128 + c])
        CP(bb[:, 256:256 + G], pa[:, 256:256 + G], eng="dve")
        CP(bb[:c, 384:387], pa[:c, 384:387], eng="dve")
        beta_bc, eG_bc, eGe_bc = bb[:, 0:c], bb[:, 128:128 + c], bb[:, 256:256 + G]
        yield
        pd_ = pbank()
        MM(pd_[:c, :c], Gn[:, :c], ones_f[0:1, :c], start=True, stop=False)
        MM(pd_[:c, :c], ones_f[0:1, :c], Gr[:, :c], start=False, stop=True)
        dT = T2("b_dT"); TS(dT[:c, :c], pd_[:c, :c], 0.0, ALU.min)
        ACT(dT[:c, :c], dT[:c, :c], AF.Exp)
        mU, mSU = ("U8", "SU8") if smp else ("U64", "SU64")
        dTS = T2("b_dTS"); TT(dTS[:c, :c], dT[:c, :c], mask(mSU, c), ALU.mult)
        dTI = T2("b_dTI"); TT(dTI[:c, :c], dT[:c, :c], mask(mU, c), ALU.mult)
        yield
        khT = T2("b_khT", BF16); CP(khT[:, :c], kf[:, :c], eng="dve")
        kbT = T2("b_kbT", BF16); TT(kbT[:, :c], kf[:, :c], beta_bc, ALU.mult)
        qhT = T2("b_qhT", BF16); CP(qhT[:, :c], qf[:, :c], eng="dve")
        qgT = T2("b_qgT", BF16); TT(qgT[:, :c], qf[:, :c], eG_bc, ALU.mult)
        pe_ = pbank()
        MM(pe_[:c, 0:c], khT[:, :c], kbT[:, :c])
        MM(pe_[:c, 128:128 + c], khT[:, :c], qhT[:, :c])
        Bm = T2("b_B"); TT(Bm[:c, :c], pe_[:c, 0:c], dTS[:c, :c], ALU.mult)
        Aqk = T2("b_Aqk", BF16); TT(Aqk[:c, :c], pe_[:c, 128:128 + c], dTI[:c, :c], ALU.mult)
        X = T2("b_X"); TT(X[:c, :c], ident[:c, :c], Bm[:c, :c], ALU.subtract)
        pf = pbank(); TR(pf[:c, :c], Bm[:c, :c])
        Am = T2("b_A"); CP(Am[:c, :c], pf[:c, :c])
        yield
        nsq = int(math.log2(gs)) - 1
        for lev in range(nsq):
            pg = pbank()
            MM(pg[:c, 0:c], Bm[:c, :c], Am[:c, :c])
            if lev < nsq - 1:
                MM(pg[:c, 128:128 + c], Am[:c, :c], Bm[:c, :c])
            A2 = T2("b_A"); CP(A2[:c, :c], pg[:c, 0:c])
            if lev < nsq - 1:
                B2 = T2("b_B"); CP(B2[:c, :c], pg[:c, 128:128 + c], eng="dve")
            ph = pbank(); MM(ph[:c, :c], A2[:c, :c], X[:c, :c])
            Xn = T2("b_X"); TT(Xn[:c, :c], X[:c, :c], ph[:c, :c], ALU.add)
            yield
            X, Am = Xn, A2
            if lev < nsq - 1:
                Bm = B2
        pt = pbank()
        TR(pt[:c, 0:128], kf[:, :c])
        TR(pt[:c, 128:256], cs[2][:, :c])
        kbg = T2("b_kbg", BF16); TS(kbg[:c, :], pt[:c, 0:128], bb[:c, 385:386], ALU.mult)
        kd = T2("b_kd", BF16); TS(kd[:c, :], pt[:c, 0:128], bb[:c, 386:387], ALU.mult)
        bv = T2("b_bv", BF16); TS(bv[:c, :], pt[:c, 128:256], bb[:c, 384:385], ALU.mult)
        yield
        Xb = T2("b_Xb", BF16); CP(Xb[:c, :c], X[:c, :c], eng="dve")
        pw = pbank(); MM(pw[:, :c], kbg[:c, :], Xb[:c, :c])
        nWk = T2("b_nWk", BF16); S.I("act", "mul", out=nWk[:, :c], in_=pw[:, :c], mul=-1.0)
        yield
        if G > 1:
            kdm = tmp("b_kdm", [128, 16, 128], BF16, 1)
            o_, g_ = RM_OFF[gs]
            TT(kdm[:c, :G, :], kd[:c, :].un(1).bc([c, G, 128]), rmk[:c, o_:o_ + G].un(2).bc([c, G, 128]), ALU.mult)
        po = LN["X"]
        for g in range(G):
            gi_ = g if smp else 0
            cols = slice(g * gs, (g + 1) * gs)
            pv = pbank()
            MM(pv[:c, 0:128], Xb[:c, :c], bv[:c, :], start=True, stop=False)
            MM(pv[:c, 0:128], nWk[:, :c], Sb_[:, gi_, :], start=False, stop=True)
            Wg = T2("b_Wg", BF16); CP(Wg[:c, :], pv[:c, 0:128])
            yield
            MM(po[:, cols], Sb_[:, gi_, :], qgT[:, cols], start=True, stop=False)
            MM(po[:, cols], Wg[:c, :], Aqk[:c, cols], start=False, stop=True)
            pu = pbank()
            MM(pu[:, 0:128], kdm[:c, g, :] if G > 1 else kd[:c, :], Wg[:c, :])
            STT(Sf[:, gi_, :], Sf[:, gi_, :], eGe_bc[:, g:g + 1], pu[:, 0:128], ALU.mult, ALU.add)
            yield
            if not smp:
                CP(Sb_[:, 0, :], Sf[:, 0, :])
        hT = T2("b_hT"); CP(hT[:, :c], po[:, :c])
        y_finalize(l, 6 + h, ti, tile, hT, zf[:, :c])
        yield
        if smp:
            S.D("sp", oS[l, :, h, :, :].rearrange("g d e -> d g e"), Sf)
        elif tile["last"]:
            S.D("sp", pS[l, h, :, :], Sf[:, 0, :])
        yield

    def hgrn_setup(l, h):
        LN = LANES[cur[0]]
        LN["wp"] = 0
        pCs = LN["pCs"]
        pCb = LN["pCb"]
        wbs = [wblock(l, C_Q + h * 128), wblock(l, C_F + h * 128), wblock(l, C_I + h * 128), wblock(l, Z0 + (12 + h) * 128)]
        S.op("dve", lambda e: e.memset(pCs.ap, 0.0), [], [pCs.tk])
        S.op("dve", lambda e: e.memset(pCb.ap, 0.0), [], [pCb.tk])
        return wbs, None

    def hgrn_tile(l, h, ti, tile, pj, prefetch):
        LN = LANES[cur[0]]
        lbv, omlv = lb[:, l, h:h + 1], oml[:, l, h:h + 1]
        pCs = LN["pCs"]
        pCb = LN["pCb"]
        c, gs, t0 = tile["c"], tile["gC"], tile["t0"]
        G = c // gs
        smp = tile["kind"] == "s"
        if smp:
            Sf, Sb_ = sCc[0], sCb
            S.D("sp", Sf, sH[l, :, h, :, :].rearrange("g d e -> d g e"))
            CP(Sb_, Sf)
        else:
            Sf, Sb_ = pCs, pCb
        (q_ps, f_ps, i_ps, z_ps), _ = pj
        qraw = T2("c_qraw"); CP(qraw[:, :c], q_ps)
        e1 = T2("c_e1"); ACT(e1[:, :c], f_ps, AF.Exp, scale=-1.0)
        vf = T2("c_vf"); CP(vf[:, :c], i_ps)
        zf = T2("c_zf"); CP(zf[:, :c], z_ps)
        prefetch()
        yield
        eq = T2("c_eq"); ACT(eq[:, :c], qraw[:, :c], AF.Exp, scale=-1.0)
        RECIP1P(eq[:, :c], eq[:, :c])
        qf = T2("c_qf"); TT(qf[:, :c], qraw[:, :c], eq[:, :c], ALU.mult)
        yield
        TS(e1[:, :c], e1[:, :c], float(np.exp(60.0)), ALU.min)
        l1 = T2("c_l1"); ACT(l1[:, :c], e1[:, :c], AF.Ln, scale=lbv, bias=c_one)
        l2 = T2("c_l2"); ACT(l2[:, :c], e1[:, :c], AF.Ln, bias=c_one)
        nlf = T2("c_nlf"); TT(nlf[:, :c], l2[:, :c], l1[:, :c], ALU.subtract)
        r_ = T2("c_r"); ACT(r_[:, :c], l2[:, :c], AF.Exp, scale=-1.0)
        kf = T2("c_kf"); STT(kf[:, :c], e1[:, :c], omlv, r_[:, :c], ALU.mult, ALU.mult)
        yield
        bn = T2("c_bn"); scan(bn[:, :c], resets[:, RESET_IDX[gs], :c], nlf[:, :c])
        eb = T2("c_eb"); ACT(eb[:, :c], bn[:, :c], AF.Exp, scale=-1.0)
        enb = T2("c_enb"); ACT(enb[:, :c], bn[:, :c], AF.Exp)
        qeb = T2("c_qeb", BF16); TT(qeb[:, :c], qf[:, :c], eb[:, :c], ALU.mult)
        keb = T2("c_keb", BF16); TT(keb[:, :c], kf[:, :c], enb[:, :c], ALU.mult)
        yield
        ebe = tmp("c_ebe", [128, 16], F32); ACT(ebe[:, :G], g3(bn, c, G)[:, :, gs - 1], AF.Exp, scale=-1.0)
        kdT = T2("c_kdT"); TT(g3(kdT, c, G), g3(bn, c, G), g3(bn, c, G)[:, :, gs - 1:gs].bc([128, G, gs]), ALU.subtract)
        ACT(kdT[:, :c], kdT[:, :c], AF.Exp)
        TT(kdT[:, :c], kdT[:, :c], kf[:, :c], ALU.mult)
        yield
        pt = pbank()
        TR(pt[:c, 0:128], kdT[:, :c])
        TR(pt[:c, 128:256], vf[:, :c])
        kd = T2("c_kd", BF16); CP(kd[:c, :], pt[:c, 0:128])
        vb = T2("c_vb", BF16); CP(vb[:c, :], pt[:c, 128:256], eng="dve")
        yield
        if G > 1:
            kdm = tmp("c_kdm", [128, 16, 128], BF16, 1)
            o_, g_ = RM_OFF[gs]
            TT(kdm[:c, :G, :], kd[:c, :].un(1).bc([c, G, 128]), rmk[:c, o_:o_ + G].un(2).bc([c, G, 128]), ALU.mult)
        pa = pbank(); MM(pa[:c, :c], keb[:, :c], qeb[:, :c])
        mname = "U8" if smp else ("U16" if c == 128 else "U128")
        AT = T2("c_AT", BF16); TT(AT[:c, :c], pa[:c, :c], mask(mname, c), ALU.mult)
        yield
        po = LN["X"]
        MM(po[:, :c], vb[:c, :], AT[:c, :c], start=True, stop=False)
        for g in range(G):
            gi_ = g if smp else 0
            cols = slice(g * gs, (g + 1) * gs)
            MM(po[:, cols], Sb_[:, gi_, :], qeb[:, cols], start=False, stop=(g == G - 1))
            pu = pbank()
            MM(pu[:, 0:128], kdm[:c, g, :] if G > 1 else kd[:c, :], vb[:c, :])
            STT(Sf[:, gi_, :], Sf[:, gi_, :], ebe[:, g:g + 1], pu[:, 0:128], ALU.mult, ALU.add)
            yield
            if not smp:
                CP(Sb_[:, 0, :], Sf[:, 0, :])
        hT = T2("c_hT"); CP(hT[:, :c], po[:, :c])
        y_finalize(l, 12 + h, ti, tile, hT, zf[:, :c])
        yield
        if smp:
            S.D("sp", oH[l, :, h, :, :].rearrange("g d e -> d g e"), Sf)
        elif tile["last"]:
            S.D("sp", pH[l, h, :, :], Sf[:, 0, :])
        yield

    wo_c = []

    def out_stage(l):
        S.region("O")
        if not wo_c:
            wo_c.append(S.sb("wo", [128, 16, D], BF16))
        wo = wo_c[0]
        for q4 in range(4):
            S.D("pool", wo[:, q4 * 4:(q4 + 1) * 4, :], w_out[l, q4 * 512:(q4 + 1) * 512, :].rearrange("(h p) n -> p h n", p=128))
        for ti, tile in enumerate(TILES):
            c, t0 = tile["c"], tile["t0"]
            yt = tmp("op_y", [128, 16, 128], BF16, 2)
            S.D("sp", yt[:, :, :c], yTv[ti].re("h p t -> p h t"))
            xo = tmp("op_xo", [128, NCH, 128], F32, 2)
            S.D("sp", xo[:, :, :c], xrv[ti].re("j p t -> p j t"))
            xn = tmp("xT", [128, NCH, 128], F32, 2)
            for j in range(NCH):
                ps = pbank()
                for hh in range(16):
                    MM(ps[:, :c], wo[:, hh, j * 128:(j + 1) * 128], yt[:, hh, :c], start=(hh == 0), stop=(hh == 15))
                TT(xn[:, j, :c], xo[:, j, :c], ps[:, :c], ALU.add)
            finish_x(ti, tile, xn, l + 1)

    def run_heads(kind, l, hs):
        setup = {"a": mlstm_setup, "b": gdn_setup, "c": hgrn_setup}[kind]
        tilef = {"a": mlstm_tile, "b": gdn_tile, "c": hgrn_tile}[kind]
        S.region("H")
        ws = {}

        def head_gen(ln, h):
            pt_ = [(ti, t) for ti, t in enumerate(TILES) if t["kind"] == "p"]
            nxt = {0: proj(pt_[0][1], *ws[ln])}
            for k, (ti, tile) in enumerate(pt_):
                def prefetch(k=k):
                    if k + 1 < len(pt_):
                        nxt[k + 1] = proj(pt_[k + 1][1], *ws[ln])
                yield from tilef(l, h, ti, tile, nxt[k], prefetch)

        for ln, h in enumerate(hs):
            cur[0] = ln
            ws[ln] = setup(l, h)
        recs, n_smp = [], []
        for ln, h in enumerate(hs):
            cur[0] = ln
            S.rec = []
            for ti, tile in enumerate(TILES):
                if tile["kind"] == "s":
                    for _ in tilef(l, h, ti, tile, proj(tile, *ws[ln]), lambda: None):
                        pass
            n_smp.append(len(S.rec))
            for _ in head_gen(ln, h):
                pass
            recs.append(S.rec)
            S.rec = None
        S.replay(recs, offset=n_smp[0] + LANE_OFFSET)
        cur[0] = -1

    def full():
        cur[0] = -1
        stage_a()
        S.barrier()
        for l in range(n_layers):
            for hp in range(3):
                run_heads("a", l, [2 * hp, 2 * hp + 1])
            for hp in range(3):
                run_heads("b", l, [2 * hp, 2 * hp + 1])
            for hp in range(2):
                run_heads("c", l, [2 * hp, 2 * hp + 1])
            S.barrier()
            cur[0] = -1
            out_stage(l)
            S.barrier()

    return nc, S, locals()


def core_inputs(inp, c, consts):
    s = c // 2
    sl = slice(16 * c, 16 * c + 16)
    f = lambda a: np.ascontiguousarray(np.asarray(a, dtype=np.float32))
    m = {
        "xp": f(np.concatenate([inp["meta_tokens"], inp["x_prompt"][s]], axis=0)),
        "xs": f(np.asarray(inp["x_sample"])[sl].reshape(128, D)),
        "w_in": consts["_w_in_r"], "w_gc": consts["_w_gc"], "w_out": f(inp["w_out"]),
        "norm_w": f(inp["norm_w"]), "out_norm_w": f(inp["out_norm_w"]), "final_norm_w": f(inp["final_norm_w"]),
        "mlstm_gate_b": f(inp["mlstm_gate_b"]), "gdn_A_log": f(inp["gdn_A_log"]), "gdn_dt_bias": f(inp["gdn_dt_bias"]),
        "gdn_conv_w": f(inp["gdn_conv_w"]), "hgrn_lower_bounds": f(inp["hgrn_lower_bounds"]),
        "sC": f(np.asarray(inp["state_mlstm_C"])[:, sl]), "sn": f(np.asarray(inp["state_mlstm_n"])[:, sl]),
        "sm": f(np.asarray(inp["state_mlstm_m"])[:, sl]), "sS": f(np.asarray(inp["state_gdn_S"])[:, sl]),
        "sconv": f(np.asarray(inp["state_gdn_conv"])[:, sl]), "sH": f(np.asarray(inp["state_hgrn_S"])[:, sl]),
    }
    m.update({k: v for k, v in consts.items() if not k.startswith("_")})
    return m


def relayout_w_in(w_in):
    w = np.asarray(w_in, dtype=np.float32)
    out = np.empty((2, 70, 128, NCH * 128), np.float32)
    for i, c0 in enumerate(_blk_cols):
        blk = w[:, :, c0:c0 + 128].reshape(2, NCH, 128, 128)
        out[:, i] = blk.transpose(0, 2, 1, 3).reshape(2, 128, NCH * 128)
    g = w[:, :, _gate_cols].reshape(2, NCH, 128, 24).transpose(0, 2, 1, 3)
    return out, np.ascontiguousarray(g)


_CACHE = {}


def kernel(**inputs):
    if "nc" not in _CACHE:
        nc, S, L = build_nc(n_layers=2)
        L["full"]()
        S.finish()
        _CACHE["nc"] = nc
    nc = _CACHE["nc"]
    consts = host_consts()
    consts["_w_in_r"], consts["_w_gc"] = relayout_w_in(inputs["w_in"])
    in_maps = [core_inputs(inputs, c, consts) for c in range(8)]
    res = run_bass_kernel_spmd(nc, in_maps, core_ids=list(range(8)))
    R = res.results
    f = lambda a: np.asarray(a, dtype=np.float32)
    y_prompt = np.stack([f(R[2 * s]["yp"])[16:] for s in range(4)], axis=0)
    y_sample = np.concatenate([f(R[c]["ys"]).reshape(16, 8, D) for c in range(8)], axis=0)
    pst = lambda k: np.stack([f(R[2 * s][k]) for s in range(4)], axis=1)
    sst = lambda k: np.concatenate([f(R[c][k]) for c in range(8)], axis=1)
    return (y_prompt, y_sample,
            pst("pC"), pst("pn"), pst("pm"), pst("pS"), pst("pconv"), pst("pH"),
            sst("oC"), sst("on"), sst("om"), sst("oS"), sst("oconv"), sst("oH"))
```

```python
import contextlib
import numpy as np
import concourse.bass as bass
import concourse.mybir as mybir
from concourse.bass_utils import run_bass_kernel_spmd

F32 = mybir.dt.float32
BF16 = mybir.dt.bfloat16
I32 = mybir.dt.int32
AF = mybir.ActivationFunctionType
ALU = mybir.AluOpType
AX = mybir.AxisListType


class Tk:
    def __init__(self, name, psum=False):
        self.name = name
        self.psum = psum
        self.w = None
        self.r = []


class V:
    def __init__(self, tk, ap):
        self.tk = tk
        self.ap = ap

    def __getitem__(self, k):
        return V(self.tk, self.ap[k])

    def bc(self, shape):
        return V(self.tk, self.ap.broadcast_to(list(shape)))

    def un(self, axis):
        return V(self.tk, self.ap.unsqueeze(axis))

    def re(self, pat, **kw):
        return V(self.tk, self.ap.rearrange(pat, **kw))

    @property
    def shape(self):
        return self.ap.shape


class Sched:
    ENGS = ("pe", "act", "dve", "pool", "sp")
    ROT = 30000

    def __init__(self, nc):
        self.nc = nc
        self.es = contextlib.ExitStack()
        self.q = {e: [] for e in self.ENGS}
        self.cnt = {e: 0 for e in self.ENGS}
        self.nsem = 0
        self.sem = {e: self._newsem(e) for e in self.ENGS}
        self.waited = {e: {} for e in self.ENGS}
        self.dsem = {}
        self.n_ops = 0
        self.sb_off = 0
        self.sb_max = 0
        self.nalloc = 0
        self.rec = None
        self.offs = {"P": 16512}
        self.cur = "P"

    def split(self):
        r0 = self.offs["P"]
        self.offs["H"] = r0
        self.offs["O"] = r0

    def region(self, r):
        self.cur = r

    def _newsem(self, tag):
        self.nsem += 1
        return self.es.enter_context(self.nc.semaphore(f"s{self.nsem}_{tag}"))

    def sb(self, name, shape, dtype):
        nb = 2 if dtype == BF16 else 4
        n = 1
        for d in shape[1:]:
            n *= d
        size = (n * nb + 63) // 64 * 64
        off = self.offs[self.cur]
        self.offs[self.cur] = off + size
        self.sb_off = off + size
        self.sb_max = max(self.sb_max, self.sb_off)
        assert self.sb_off <= 229376, f"SBUF overflow at {name}: {self.sb_off} region {self.cur}"
        self.nalloc += 1
        t = self.nc.alloc_sbuf_tensor_at(f"{name}_{self.nalloc}", list(shape), dtype, offset=off)
        return V(Tk(name), t[:])

    def ps(self, name, shape, dtype=None):
        t = self.nc.alloc_psum_tensor(name, list(shape), F32)
        return V(Tk(name, psum=True), t[:])

    def I(self, eng, meth, **kw):
        reads, writes, args = [], [], {}
        for k, v in kw.items():
            if isinstance(v, V):
                if k in ("out", "accum_out") or v.tk.psum:
                    writes.append(v.tk)
                else:
                    reads.append(v.tk)
                args[k] = v.ap
            else:
                args[k] = v
        self.op(eng, lambda e: getattr(e, meth)(**args), reads, writes)

    def D(self, eng, out, in_, **kw):
        reads, writes = [], []
        names = []
        if isinstance(in_, V):
            reads.append(in_.tk)
            names.append(in_.tk.name)
            in_ = in_.ap
        if isinstance(out, V):
            writes.append(out.tk)
            names.append(out.tk.name)
            out = out.ap
        sbn = [n for n in names if not n.startswith("dram:")]
        key = sbn[0] if sbn else names[0]
        self.dma(eng, out, in_, reads, writes, key=key, **kw)

    def replay(self, lists, offset=0):
        self.rec = None
        idx = [0] * len(lists)
        step = 0
        while any(idx[k] < len(lists[k]) for k in range(len(lists))):
            for k, lst in enumerate(lists):
                if step < k * offset or idx[k] >= len(lst):
                    continue
                depth = 0
                while idx[k] < len(lst):
                    it = lst[idx[k]]
                    idx[k] += 1
                    if it[0] == "nop":
                        pass
                    elif it[0] == "gs":
                        depth += 1
                    elif it[0] == "ge":
                        depth -= 1
                    elif it[0] == "op":
                        self.op(*it[1:])
                    else:
                        self.dma(it[1], it[2], it[3], it[4], it[5], key=it[6], **it[7])
                    if depth == 0:
                        break
            step += 1

    def barrier(self):
        evs = [(self.sem[e], self.cnt[e]) for e in self.ENGS if self.cnt[e] > 0]
        evs += [(sem, val) for (sem, val) in self.dsem.values() if val > 0]
        for e in self.ENGS:
            wd = self.waited[e]
            for (sem, val) in evs:
                if sem is self.sem[e] and e == "pe":
                    continue
                if wd.get(id(sem), (None, 0))[1] >= val:
                    continue
                wd[id(sem)] = (sem, val)
                self.q[e].append(("wait", sem, val))

    def _deps(self, eng, reads, writes):
        deps = []
        for t in reads:
            if t.w is not None:
                deps.append(t.w)
        for t in writes:
            if t.w is not None:
                deps.append(t.w)
            deps.extend(t.r)
        wd = self.waited[eng]
        need = {}
        for (sem, val, e2) in deps:
            if e2 == "pe" and eng == "pe":
                continue
            k = id(sem)
            if wd.get(k, (None, 0))[1] >= val:
                continue
            if k not in need or need[k][1] < val:
                need[k] = (sem, val)
        for k, (sem, val) in need.items():
            wd[k] = (sem, val)
            self.q[eng].append(("wait", sem, val))

    def _record(self, ev, reads, writes):
        for t in reads:
            t.r.append(ev)
        for t in writes:
            t.w = ev
            t.r = []

    def op(self, eng, fn, reads, writes):
        if self.rec is not None:
            self.rec.append(("op", eng, fn, reads, writes))
            return
        self._deps(eng, reads, writes)
        if self.cnt[eng] >= self.ROT:
            self.sem[eng] = self._newsem(eng)
            self.cnt[eng] = 0
        self.cnt[eng] += 1
        ev = (self.sem[eng], self.cnt[eng], eng)
        self.q[eng].append(("op", fn, self.sem[eng], 1))
        self._record(ev, reads, writes)
        self.n_ops += 1

    def dma(self, eng, out, in_, reads, writes, key=None, **kw):
        if self.rec is not None:
            self.rec.append(("dma", eng, out, in_, reads, writes, key, kw))
            return
        self._deps(eng, reads, writes)
        if key is None:
            key = (writes[0] if writes else reads[0]).name
        if key not in self.dsem:
            self.dsem[key] = [self._newsem("d"), 0]
        ds = self.dsem[key]
        ds[1] += 16
        ev = (ds[0], ds[1], "dma")
        self.q[eng].append(("op", (lambda e, out=out, in_=in_, kw=kw: e.dma_start(out=out, in_=in_, **kw)), ds[0], 16))
        self._record(ev, reads, writes)
        self.n_ops += 1

    def finish(self):
        for key, (sem, val) in self.dsem.items():
            if val > 0:
                self.q["sp"].append(("wait", sem, val))
        for e in self.ENGS:
            if e != "sp" and self.cnt[e] > 0:
                self.q["sp"].append(("wait", self.sem[e], self.cnt[e]))
        nc = self.nc
        q = self.q

        def emit(engine, lst):
            for it in lst:
                if it[0] == "wait":
                    engine.wait_ge(it[1], it[2])
                else:
                    ins = it[1](engine)
                    ins.then_inc(it[2], it[3])

        with nc.allow_non_contiguous_dma(reason="small strided state/param transfers"), nc.Block() as block:
            @block.tensor
            def _(e):
                emit(e, q["pe"])

            @block.scalar
            def _(e):
                emit(e, q["act"])

            @block.vector
            def _(e):
                emit(e, q["dve"])

            @block.gpsimd
            def _(e):
                emit(e, q["pool"])

            @block.sync
            def _(e):
                emit(e, q["sp"])
        self.es.close()


D = 2048
NCH = 16
NP = 2064
NS = 128
NT = NP + NS
N_IN = 8984
A_Q, A_K, A_V, A_O, A_I, A_F = 0, 768, 1536, 2304, 3072, 3078
B_Q, B_K, B_V, B_A, B_B = 3084, 3852, 4620, 5388, 5394
C_Q, C_F, C_I, Z0 = 5400, 5912, 6424, 6936
EPS = 1e-6
RS = 128 ** -0.5
LANE_OFFSET = 110
_blk_cols = ([A_Q + i * 128 for i in range(24)] + [B_Q + i * 128 for i in range(18)]
             + [C_Q + i * 128 for i in range(12)] + [Z0 + i * 128 for i in range(16)])
BLK_OF = {c: i for i, c in enumerate(_blk_cols)}
_gate_cols = list(range(A_I, A_I + 12)) + list(range(B_A, B_A + 12))
GC_OF = {c: i for i, c in enumerate(_gate_cols)}
MASK_IDX = {"U128": 0, "U64": 1, "SU64": 2, "U16": 3, "U8": 4, "SU8": 5}
RESET_IDX = {128: 0, 64: 1, 16: 2, 8: 3}
RM_OFF = {64: (0, 2), 16: (2, 8), 8: (10, 16)}


def host_consts():
    idx = np.arange(128)
    masks = np.zeros((6, 128, 128), np.float32)
    for name, gs, strict in (("U128", 128, False), ("U64", 64, False), ("SU64", 64, True),
                             ("U16", 16, False), ("U8", 8, False), ("SU8", 8, True)):
        same = (idx[:, None] // gs) == (idx[None, :] // gs)
        tri = (idx[:, None] < idx[None, :]) if strict else (idx[:, None] <= idx[None, :])
        masks[MASK_IDX[name]] = (same & tri).astype(np.float32)
    resets = np.ones((4, 128, 128), np.float32)
    for gs, i in RESET_IDX.items():
        resets[i][:, (idx % gs) == 0] = 0.0
    rm = np.zeros((128, 26), np.float32)
    for gs, (o, g) in RM_OFF.items():
        for j in range(g):
            rm[(idx // gs) == j, o + j] = 1.0
    return {"c_ident": np.eye(128, dtype=np.float32), "c_masks": masks, "c_resets": resets, "c_rm": rm}


def make_tiles():
    tiles = [dict(kind="p", t0=0, c=16, gA=16, gB=16, gC=16, first=True, last=False)]
    for i in range(16):
        tiles.append(dict(kind="p", t0=16 + 128 * i, c=128, gA=128, gB=64, gC=16, first=False, last=(i == 15)))
    tiles.append(dict(kind="s", t0=NP, c=128, gA=8, gB=8, gC=8, first=True, last=True))
    return tiles


def build_nc(n_layers=2, heads=None, do_out=True, debug=False):
    nc = bass.Bass("TRN2", target_bir_lowering=False)
    S = Sched(nc)

    def din(name, shape):
        return nc.dram_tensor(name, list(shape), F32, kind="ExternalInput").ap()

    def dout(name, shape):
        return nc.dram_tensor(name, list(shape), F32, kind="ExternalOutput").ap()

    xp = din("xp", [NP, D]); xs = din("xs", [NS, D])
    w_in = din("w_in", [2, 70, 128, NCH * 128]); w_gc = din("w_gc", [2, 128, NCH, 24]); w_out = din("w_out", [2, D, D])
    norm_w = din("norm_w", [2, D]); out_norm_w = din("out_norm_w", [2, D]); final_norm_w = din("final_norm_w", [D])
    gate_b = din("mlstm_gate_b", [2, 2, 6]); A_log = din("gdn_A_log", [2, 6]); dt_bias = din("gdn_dt_bias", [2, 6])
    conv_w = din("gdn_conv_w", [2, 4, 2304]); lbp = din("hgrn_lower_bounds", [2, 512])
    sC = din("sC", [2, 16, 6, 128, 128]); sn = din("sn", [2, 16, 6, 128]); sm = din("sm", [2, 16, 6])
    sS = din("sS", [2, 16, 6, 128, 128]); sconv = din("sconv", [2, 16, 3, 2304]); sH = din("sH", [2, 16, 4, 128, 128])
    c_ident = din("c_ident", [128, 128]); c_masks = din("c_masks", [6, 128, 128])
    c_resets = din("c_resets", [4, 128, 128]); c_rm = din("c_rm", [128, 26])

    yp = dout("yp", [NP, D]); ys = dout("ys", [NS, D])
    pC = dout("pC", [2, 6, 128, 128]); pn = dout("pn", [2, 6, 128]); pm = dout("pm", [2, 6])
    pS = dout("pS", [2, 6, 128, 128]); pconv = dout("pconv", [2, 3, 2304]); pH = dout("pH", [2, 4, 128, 128])
    oC = dout("oC", [2, 16, 6, 128, 128]); on = dout("on", [2, 16, 6, 128]); om = dout("om", [2, 16, 6])
    oS = dout("oS", [2, 16, 6, 128, 128]); oconv = dout("oconv", [2, 16, 3, 2304]); oH = dout("oH", [2, 16, 4, 128, 128])

    skind = "ExternalOutput" if debug else "Internal"
    xres_t = nc.dram_tensor("xres", [NCH, 128, NT], F32, kind=skind).ap()
    yT_t = nc.dram_tensor("yTs", [16, 128, NT], BF16, kind=skind).ap()
    xres = V(Tk("dram:xres"), xres_t)
    yTd = V(Tk("dram:yT"), yT_t)

    TILES = make_tiles()

    def ACT(out, in_, func, scale=1.0, bias=None):
        kw = dict(out=out, in_=in_, func=func, scale=scale)
        if bias is not None:
            kw["bias"] = bias
        S.I("act", "activation", **kw)

    def TT(out, in0, in1, op, eng="dve"):
        S.I(eng, "tensor_tensor", out=out, in0=in0, in1=in1, op=op)

    def TS(out, in0, s1, op0, s2=None, op1=None, eng="dve"):
        if op1 is None:
            S.I(eng, "tensor_scalar", out=out, in0=in0, scalar1=s1, scalar2=None, op0=op0)
        else:
            S.I(eng, "tensor_scalar", out=out, in0=in0, scalar1=s1, scalar2=s2, op0=op0, op1=op1)

    def STT(out, in0, scalar, in1, op0, op1):
        S.I("dve", "scalar_tensor_tensor", out=out, in0=in0, scalar=scalar, in1=in1, op0=op0, op1=op1)

    def MM(out, lhsT, rhs, start=True, stop=True):
        S.I("pe", "matmul", out=out, lhsT=lhsT, rhs=rhs, start=start, stop=stop)

    def CP(out, in_, eng="act"):
        if eng == "act":
            S.I("act", "copy", out=out, in_=in_)
        else:
            S.I(eng, "tensor_copy", out=out, in_=in_)

    def RECIP(out, in_):
        ACT(out, in_, AF.Ln)
        ACT(out, out, AF.Exp, scale=-1.0)

    def RECIP1P(out, in_):
        ACT(out, in_, AF.Ln, bias=c_one[0:out.shape[0], :])
        ACT(out, out, AF.Exp, scale=-1.0)

    rings = {}
    cur = [-1]
    LANES = []

    slots = {}

    def tmp(tag, shape, dtype, n=2):
        lane_tag = tag[:2] in ("a_", "b_", "c_") or tag[:3] == "yf_"
        if lane_tag:
            n = 1
        if tag[:2] in ("a_", "b_", "c_"):
            k = (tag[:2], tuple(shape), str(dtype), n)
            d = slots.setdefault(k, {})
            if tag not in d:
                d[tag] = len(d)
            tag = f"mix_{tuple(shape)}_{dtype}_{n}_{d[tag]}"
        if lane_tag:
            tag = f"{tag}@{max(cur[0], 0)}"
        if tag not in rings:
            rings[tag] = [[S.sb(f"{tag}{i}", shape, dtype) for i in range(n)], 0]
        r = rings[tag]
        v = r[0][r[1] % n]
        r[1] += 1
        return v

    ident = S.sb("ident", [128, 128], F32)
    masks = S.sb("masks", [128, 6, 128], F32)
    resets = S.sb("resets", [128, 4, 128], F32)
    rmk = S.sb("rmk", [128, 26], F32)
    ones_f = S.sb("ones_f", [128, 128], F32)
    ones_b = S.sb("ones_b", [128, 128], BF16)
    c_one = S.sb("c_one", [128, 1], F32)
    c_eps = S.sb("c_eps", [128, 1], F32)
    S.D("sp", ident, c_ident)
    S.D("sp", masks, c_masks.rearrange("m p n -> p m n"))
    S.D("sp", resets, c_resets.rearrange("m p n -> p m n"))
    S.D("sp", rmk, c_rm)
    S.I("dve", "memset", ap=ones_f, constant=1.0) if False else S.op("dve", lambda e: e.memset(ones_f.ap, 1.0), [], [ones_f.tk])
    S.op("dve", lambda e: e.memset(ones_b.ap, 1.0), [], [ones_b.tk])
    S.op("dve", lambda e: e.memset(c_one.ap, 1.0), [], [c_one.tk])
    S.op("dve", lambda e: e.memset(c_eps.ap, EPS), [], [c_eps.tk])

    def mask(name, c):
        return masks[:c, MASK_IDX[name], :c]

    def TR(out, in_):
        k = in_.shape[0]
        S.I("pe", "transpose", out=out, in_=in_, identity=ident[:k, :k])

    pb = [S.ps(f"pb{i}", [128, 512]) for i in range(8)]
    pj_sets = [(pb[0], pb[1]), (pb[2], pb[3])]
    pring = [0]

    def pbank():
        if cur[0] < 0:
            b = pb[4 + pring[0] % 4]
            pring[0] += 1
            return b
        L = LANES[cur[0]]
        b = L["ring"][L["rp"] % 2]
        L["rp"] += 1
        return b

    gb_row = S.sb("gb_row", [1, 24], F32)
    ngb_row = S.sb("ngb_row", [1, 24], F32)
    al_row = S.sb("al_row", [1, 12], F32)
    dtb_row = S.sb("dtb_row", [1, 12], F32)
    S.D("sp", gb_row, gate_b.rearrange("l w h -> (l w h)").unsqueeze(0))
    S.D("sp", al_row, A_log.rearrange("l h -> (l h)").unsqueeze(0))
    S.D("sp", dtb_row, dt_bias.rearrange("l h -> (l h)").unsqueeze(0))
    TS(ngb_row, gb_row, -1.0, ALU.mult)
    ACT(al_row, al_row, AF.Exp)
    lbraw = S.sb("lbraw", [128, 2, 4], F32)
    S.D("sp", lbraw, lbp.rearrange("l (h p) -> p l h", p=128))
    lb = S.sb("lb", [128, 2, 4], F32)
    oml = S.sb("oml", [128, 2, 4], F32)
    S.op("dve", lambda e: e.memset(lb.ap, 0.0), [], [lb.tk])
    TT(lb[:, 1, :], lbraw[:, 0, :], lbraw[:, 1, :], ALU.subtract)
    ACT(lb[:, 1, :], lb[:, 1, :], AF.Exp)
    RECIP1P(lb[:, 1, :], lb[:, 1, :])
    TS(oml, lb, -1.0, ALU.mult, 1.0, ALU.add)
    nw = S.sb("nw", [128, 2, NCH], F32)
    onw = S.sb("onw", [128, 2, NCH], F32)
    fnw = S.sb("fnw", [128, NCH], F32)
    S.D("sp", nw, norm_w.rearrange("l (j p) -> p l j", p=128))
    S.D("sp", onw, out_norm_w.rearrange("l (j p) -> p l j", p=128))
    S.D("sp", fnw, final_norm_w.rearrange("(j p) -> p j", p=128))
    cw = S.sb("cw", [128, 2, 18, 4], F32)
    for l_ in range(2):
        for j_ in range(4):
            S.D("sp", cw[:, l_, :, j_], conv_w[l_, j_, :].rearrange("(b p) -> p b", p=128))

    xnT = S.sb("xnT", [128, NCH, NT], BF16)

    def finish_x(ti, tile, xT, layer_next):
        c, t0 = tile["c"], tile["t0"]
        if layer_next < n_layers:
            S.D("sp", xrv[ti].re("j p t -> p j t"), xT[:, :, :c])
        sq = tmp("fx_sq", [128, NCH, 128], BF16, 1)
        S.I("act", "activation", out=sq[:, :, :c], in_=xT[:, :, :c], func=AF.Square)
        ps = pbank()
        for j in range(NCH):
            MM(ps[:, :c], ones_b, sq[:, j, :c], start=(j == 0), stop=(j == NCH - 1))
        rstd = tmp("fx_rstd", [128, 128], F32, 2)
        ACT(rstd[:, :c], ps[:, :c], AF.Ln, scale=1.0 / D, bias=c_eps)
        ACT(rstd[:, :c], rstd[:, :c], AF.Exp, scale=-0.5)
        t1 = tmp("fx_t1", [128, NCH, 128], F32, 1)
        TT(t1[:, :, :c], xT[:, :, :c], rstd[:, :c].un(1).bc([128, NCH, c]), ALU.mult)
        if layer_next < n_layers:
            TT(xnT[:, :, t0:t0 + c], t1[:, :, :c], nw[:, layer_next, :].un(2).bc([128, NCH, c]), ALU.mult)
        else:
            TT(t1[:, :, :c], t1[:, :, :c], fnw.un(2).bc([128, NCH, c]), ALU.mult)
            ot = tmp("sa_x", [128, D], F32, 1)
            for q4 in range(4):
                pt = pbank()
                for jj in range(4):
                    j = q4 * 4 + jj
                    TR(pt[:c, jj * 128:(jj + 1) * 128], t1[:, j, :c])
                CP(ot[:c, q4 * 512:(q4 + 1) * 512], pt[:c, :])
            dst = yp[t0:t0 + c, :] if tile["kind"] == "p" else ys[:, :]
            S.D("sp", dst, ot[:c, :])

    def stage_a():
        S.region("O")
        for ti, tile in enumerate(TILES):
            c, t0 = tile["c"], tile["t0"]
            xt = tmp("sa_x", [128, D], F32, 1)
            src = xp[t0:t0 + c, :] if tile["kind"] == "p" else xs[:, :]
            S.D("sp", xt[:c, :], src)
            xT = tmp("xT", [128, NCH, 128], F32, 2)
            for q4 in range(4):
                pt = pbank()
                for jj in range(4):
                    j = q4 * 4 + jj
                    TR(pt[:, jj * 128:jj * 128 + c], xt[:c, j * 128:(j + 1) * 128])
                CP(xT[:, q4 * 4:(q4 + 1) * 4, :c], pt.re("p (a b) -> p a b", a=4)[:, :, :c])
            finish_x(ti, tile, xT, 0)


    S.split()
    S.region("H")
    sA = [S.sb("sA0", [128, 16, 129], F32)]
    sSb = S.sb("sSb", [128, 16, 128], BF16)
    sB = [sA[0][:, :, 0:128]]
    sBb = sSb
    sCc = [sA[0][:, :, 0:128]]
    sCb = sSb
    mrow_s = S.sb("mrow_s", [1, 16], F32)
    ue_s = [S.sb(f"ue_s{i}", [128, 16, 11], F32) for i in range(3)]
    Cb = S.sb("Cb", [128, 16, 128], BF16)
    nbb = S.sb("nbb", [128, 16, 128], BF16)
    for i_ in range(2):
        bk = pb[4 * i_:4 * i_ + 4]
        LANES.append(dict(
            i=i_, PJ=bk[0], X=bk[1], ring=[bk[2], bk[3]], rp=0,
            wring=[S.sb(f"wblk{i_}_{k}", [128, NCH, 128], BF16) for k in range(5)], wp=0,
            wg=S.sb(f"wg{i_}", [128, NCH, 2], BF16),
            pA=S.sb(f"pA{i_}", [128, 1, 129], F32), mrow_p=S.sb(f"mrow_p{i_}", [1, 16], F32),
            pBs=S.sb(f"pBs{i_}", [128, 1, 128], F32), pBb=S.sb(f"pBb{i_}", [128, 1, 128], BF16),
            pCs=S.sb(f"pCs{i_}", [128, 1, 128], F32), pCb=S.sb(f"pCb{i_}", [128, 1, 128], BF16),
            ue_p=[S.sb(f"ue_p{i_}_{k}", [128, 1, 131], F32) for k in range(3)],
            Cbp=S.sb(f"Cbp{i_}", [128, 1, 128], BF16), nbp=S.sb(f"nbp{i_}", [128, 1, 128], BF16),
        ))

    def wblock(l, col0):
        L = LANES[cur[0]]
        wb = L["wring"][L["wp"] % 5]
        L["wp"] += 1
        S.D("pool", wb, w_in[l, BLK_OF[col0], :, :].rearrange("p (j n) -> p j n", j=NCH))
        return wb

    def wgate(l, col_a, col_b):
        wg = LANES[cur[0]]["wg"]
        S.D("pool", wg[:, :, 0:1], w_gc[l, :, :, GC_OF[col_a]:GC_OF[col_a] + 1])
        S.D("pool", wg[:, :, 1:2], w_gc[l, :, :, GC_OF[col_b]:GC_OF[col_b] + 1])
        return wg

    def proj(tile, wbs, wg):
        if S.rec is not None:
            S.rec.append(("gs",))
        r = proj_(tile, wbs, wg)
        if S.rec is not None:
            S.rec.append(("ge",))
        return r

    def proj_(tile, wbs, wg):
        c, t0 = tile["c"], tile["t0"]
        L = LANES[cur[0]]
        outs = []
        for b, wb in enumerate(wbs):
            bank = L["PJ"] if b < 4 else L["X"]
            o = bank[:, (b % 4) * 128:(b % 4) * 128 + c]
            for j in range(NCH):
                MM(o, wb[:, j, :], xnT[:, j, t0:t0 + c], start=(j == 0), stop=(j == NCH - 1))
            outs.append(o)
        gr = []
        if wg is not None:
            for i in range(2):
                o = L["X"][0:1, 256 + i * 128:256 + i * 128 + c]
                for j in range(NCH):
                    MM(o, wg[:, j, i:i + 1], xnT[:, j, t0:t0 + c], start=(j == 0), stop=(j == NCH - 1))
                gr.append(o)
        return outs, gr

    def scan(out, msk, data):
        S.I("dve", "tensor_tensor_scan", out=out, data0=msk, data1=data, initial=0.0, op0=ALU.mult, op1=ALU.add)

    def T2(tag, dtype=F32, n=2):
        return tmp(tag, [128, 128], dtype, n)

    def R1(tag, w=128, n=2):
        return tmp(tag, [1, w], F32, n)

    def g3(v, c, G):
        return v[:, :c].re("p (a b) -> p a b", a=G)

    def y_finalize(l, hg, ti, tile, hT, z_ps):
        c, t0 = tile["c"], tile["t0"]
        ez = T2("yf_ez")
        ACT(ez[:, :c], z_ps, AF.Exp, scale=-1.0)
        RECIP1P(ez[:, :c], ez[:, :c])
        zs = T2("yf_zs")
        STT(zs[:, :c], z_ps, onw[:, l, hg:hg + 1], ez[:, :c], ALU.mult, ALU.mult)
        sq = T2("yf_sq", BF16)
        ACT(sq[:, :c], hT[:, :c], AF.Square)
        ps = pbank()
        MM(ps[:, :c], ones_b, sq[:, :c])
        rstd = T2("yf_rstd")
        ACT(rstd[:, :c], ps[:, :c], AF.Ln, scale=1.0 / 128, bias=c_eps)
        ACT(rstd[:, :c], rstd[:, :c], AF.Exp, scale=-0.5)
        y32 = T2("yf_y32")
        TT(y32[:, :c], hT[:, :c], rstd[:, :c], ALU.mult)
        yb = T2("yf_yb", BF16)
        TT(yb[:, :c], y32[:, :c], zs[:, :c], ALU.mult)
        S.D("sp", yTv[ti][hg, :, :], yb[:, :c])

    yTv = [V(Tk(f"dram:yT{ti}"), yT_t[:, :, t["t0"]:t["t0"] + t["c"]]) for ti, t in enumerate(TILES)]
    xrv = [V(Tk(f"dram:xr{ti}"), xres_t[:, :, t["t0"]:t["t0"] + t["c"]]) for ti, t in enumerate(TILES)]

    def mlstm_setup(l, h):
        LN = LANES[cur[0]]
        LN["wp"] = 0
        pA = LN["pA"]
        mrow_p = LN["mrow_p"]
        wbs = [wblock(l, A_Q + h * 128), wblock(l, A_K + h * 128), wblock(l, A_V + h * 128),
               wblock(l, A_O + h * 128), wblock(l, Z0 + h * 128)]
        wg = wgate(l, A_I + h, A_F + h)
        S.op("dve", lambda e: e.memset(pA.ap, 0.0), [], [pA.tk])
        S.op("dve", lambda e: e.memset(mrow_p.ap, 0.0), [], [mrow_p.tk])
        return wbs, wg

    def mlstm_tile(l, h, ti, tile, pj, prefetch):
        LN = LANES[cur[0]]
        pA = LN["pA"]
        mrow_p = LN["mrow_p"]
        c, gs, t0 = tile["c"], tile["gA"], tile["t0"]
        G = c // gs
        smp = tile["kind"] == "s"
        if smp:
            St = sA[0]
            S.D("sp", St[:, :, 0:128], sC[l, :, h, :, :].rearrange("g d e -> d g e"))
            S.D("sp", St[:, :, 128:129], sn[l, :, h, :].rearrange("g d -> d g").unsqueeze(2))
            mrow = mrow_s
            S.D("sp", mrow, sm[l, :, h].unsqueeze(0))
        else:
            St, mrow = pA, mrow_p
        (q_ps, k_ps, v_ps, o_ps, z_ps), (gi_ps, gf_ps) = pj
        qT = T2("a_qT", BF16); CP(qT[:, :c], q_ps)
        kTf = T2("a_kTf"); S.I("act", "mul", out=kTf[:, :c], in_=k_ps, mul=RS)
        kT = T2("a_kT", BF16); CP(kT[:, :c], kTf[:, :c], eng="dve")
        vTf = T2("a_vTf"); CP(vTf[:, :c], v_ps)
        eo = T2("a_eo"); ACT(eo[:, :c], o_ps, AF.Exp, scale=-1.0)
        zf = T2("a_zf"); CP(zf[:, :c], z_ps)
        li = R1("a_li"); TS(li[:, :c], gi_ps, gb_row[0:1, l * 12 + h:l * 12 + h + 1], ALU.add)
        e1 = R1("a_e1"); ACT(e1[:, :c], gf_ps, AF.Exp, scale=-1.0, bias=ngb_row[0:1, l * 12 + 6 + h:l * 12 + 7 + h])
        prefetch()
        yield
        sp_ = R1("a_sp"); ACT(sp_[:, :c], e1[:, :c], AF.Ln, bias=c_one[0:1, :])
        Fn = R1("a_Fn"); scan(Fn[:, :c], resets[0:1, RESET_IDX[gs], :c], sp_[:, :c])
        gg = R1("a_g"); TT(gg[:, :c], li[:, :c], Fn[:, :c], ALU.add)
        gmax = R1("a_gmax", 16)
        S.I("dve", "tensor_reduce", out=gmax[:, :G], in_=g3(gg, c, G), axis=AX.X, op=ALU.max)
        Mb = R1("a_Mb", 16); TT(Mb[:, :G], gmax[:, :G], mrow[:, :G], ALU.max)
        al = R1("a_al", 16); TT(al[:, :G], mrow[:, :G], Mb[:, :G], ALU.subtract)
        ACT(al[:, :G], al[:, :G], AF.Exp)
        TT(mrow[:, :G], Mb[:, :G], g3(Fn, c, G)[:, :, gs - 1], ALU.subtract)
        w = R1("a_w"); TT(g3(w, c, G), g3(gg, c, G), Mb[:, :G].un(2).bc([1, G, gs]), ALU.subtract)
        ACT(w[:, :c], w[:, :c], AF.Exp)
        thr = R1("a_thr"); TT(g3(thr, c, G), g3(Fn, c, G), Mb[:, :G].un(2).bc([1, G, gs]), ALU.subtract)
        yield
        pa = pbank()
        MM(pa[:, 0:c], ones_f[0:1, :], thr[:, :c])
        MM(pa[:, 128:128 + G], ones_f[0:1, :], al[:, :G])
        MM(pa[:c, 256:257], w[:, :c], ones_f[0:1, 0:1])
        thrS = T2("a_thrS"); ACT(thrS[:, :c], pa[:, 0:c], AF.Exp)
        ab = tmp("a_ab", [128, 16], F32); CP(ab[:, :G], pa[:, 128:128 + G])
        wcol = tmp("a_wcol", [128, 1], F32); CP(wcol[:c, :], pa[:c, 256:257])
        yield
        pbt = pbank()
        TR(pbt[:c, 0:128], vTf[:, :c])
        TR(pbt[:c, 128:256], kTf[:, :c])
        vp = tmp("a_vp", [128, 129], BF16)
        TS(vp[:c, 0:128], pbt[:c, 0:128], wcol[:c, 0:1], ALU.mult)
        CP(vp[:c, 128:129], wcol[:c, :], eng="dve")
        kt = T2("a_kt", BF16); CP(kt[:c, :], pbt[:c, 128:256])
        wbc = T2("a_wbc", BF16); TS(wbc[:c, :], ones_f[:c, :], wcol[:c, 0:1], ALU.mult)
        yield
        pc = pbank(); MM(pc[:c, :c], kT[:, :c], qT[:, :c])
        PT = T2("a_PT", BF16); TT(PT[:c, :c], pc[:c, :c], mask("U8" if smp else "U128", c), ALU.mult)
        yield
        Cb_, nb_ = (Cb, nbb) if smp else (LN["Cbp"], LN["nbp"])
        for g in range(G):
            ACT(Cb_[:, g, :], St[:, g, 0:128], AF.Copy, scale=ab[:, g:g + 1])
            ACT(nb_[:, g, :], St[:, g, 128:129].bc([128, 128]), AF.Copy, scale=ab[:, g:g + 1])
        pd_ = pbank()
        MM(pd_[:, :c], wbc[:c, :], PT[:c, :c], start=True, stop=False)
        for g in range(G):
            MM(pd_[:, g * gs:(g + 1) * gs], nb_[:, g, :], qT[:, g * gs:(g + 1) * gs], start=False, stop=(g == G - 1))
        denS = T2("a_denS"); CP(denS[:, :c], pd_[:, :c])
        dmax = T2("a_dmax"); STT(dmax[:, :c], denS[:, :c], -1.0, denS[:, :c], ALU.mult, ALU.max)
        TT(dmax[:, :c], dmax[:, :c], thrS[:, :c], ALU.max)
        yield
        RECIP(dmax[:, :c], dmax[:, :c])
        pn_ = pbank()
        MM(pn_[:, :c], vp[:c, 0:128], PT[:c, :c], start=True, stop=False)
        for g in range(G):
            MM(pn_[:, g * gs:(g + 1) * gs], Cb_[:, g, :], qT[:, g * gs:(g + 1) * gs], start=False, stop=(g == G - 1))
        hT = T2("a_hT"); TT(hT[:, :c], pn_[:, :c], dmax[:, :c], ALU.mult)
        RECIP1P(eo[:, :c], eo[:, :c])
        TT(hT[:, :c], hT[:, :c], eo[:, :c], ALU.mult)
        y_finalize(l, h, ti, tile, hT, zf[:, :c])
        yield
        if G > 1:
            vpm = tmp("a_vpm", [128, 16, 129], BF16, 1)
            o_, g_ = RM_OFF[gs]
            TT(vpm[:c, :G, :], vp[:c, :].un(1).bc([c, G, 129]), rmk[:c, o_:o_ + G].un(2).bc([c, G, 129]), ALU.mult)
        for g in range(G):
            pu = pbank()
            MM(pu[:, 0:129], kt[:c, :], vpm[:c, g, :] if G > 1 else vp[:c, :])
            STT(St[:, g, :], St[:, g, :], ab[:, g:g + 1], pu[:, 0:129], ALU.mult, ALU.add)
            yield
        if smp:
            S.D("sp", oC[l, :, h, :, :].rearrange("g d e -> d g e"), St[:, :, 0:128])
            S.D("sp", on[l, :, h, :].rearrange("g d -> d g").unsqueeze(2), St[:, :, 128:129])
            S.D("sp", om[l, :, h].unsqueeze(0), mrow)
        elif tile["last"]:
            S.D("sp", pC[l, h, :, :], St[:, 0, 0:128])
            S.D("sp", pn[l, h, :].unsqueeze(1), St[:, 0, 128:129])
            S.D("sp", pm[l, h:h + 1].unsqueeze(0), mrow[:, 0:1])
        yield

    import math

    def gdn_setup(l, h):
        LN = LANES[cur[0]]
        LN["wp"] = 0
        pBs = LN["pBs"]
        pBb = LN["pBb"]
        ue_p = LN["ue_p"]
        wbs = [wblock(l, B_Q + h * 128), wblock(l, B_K + h * 128), wblock(l, B_V + h * 128), wblock(l, Z0 + (6 + h) * 128)]
        wg = wgate(l, B_A + h, B_B + h)
        S.op("dve", lambda e: e.memset(pBs.ap, 0.0), [], [pBs.tk])
        S.op("dve", lambda e: e.memset(pBb.ap, 0.0), [], [pBb.tk])
        for i in range(3):
            S.op("dve", lambda e, i=i: e.memset(ue_p[i].ap, 0.0), [], [ue_p[i].tk])
        return wbs, wg

    def gdn_tile(l, h, ti, tile, pj, prefetch):
        LN = LANES[cur[0]]
        pBs = LN["pBs"]
        pBb = LN["pBb"]
        ue_p = LN["ue_p"]
        c, gs, t0 = tile["c"], tile["gB"], tile["t0"]
        G = c // gs
        smp = tile["kind"] == "s"
        nseq, L = (16, 8) if smp else (1, c)
        if smp:
            Sf, Sb_ = sB[0], sBb
            S.D("sp", Sf, sS[l, :, h, :, :].rearrange("g d e -> d g e"))
            CP(Sb_, Sf)
            for i in range(3):
                ch0 = i * 768 + h * 128
                for j_ in range(3):
                    S.D("sp", ue_s[i][:, :, j_], sconv[l, :, j_, ch0:ch0 + 128].rearrange("g p -> p g"))
        else:
            Sf, Sb_ = pBs, pBb
        (q_ps, k_ps, v_ps, z_ps), (ga_ps, gb_ps) = pj
        for i, p_ in enumerate((q_ps, k_ps, v_ps)):
            ue = ue_s[i] if smp else ue_p[i]
            CP(ue[:, :, 3:3 + L], p_.re("p (a b) -> p a b", a=nseq))
        zf = T2("b_zf"); CP(zf[:, :c], z_ps)
        xa = R1("b_xa"); TS(xa[:, :c], ga_ps, dtb_row[0:1, l * 6 + h:l * 6 + h + 1], ALU.add)
        beta = R1("b_beta"); ACT(beta[:, :c], gb_ps, AF.Exp, scale=-1.0)
        prefetch()
        yield
        cs = []
        for i, p_ in enumerate((q_ps, k_ps, v_ps)):
            ue = ue_s[i] if smp else ue_p[i]
            bidx = i * 6 + h
            cv = T2(f"b_cv{i}")
            cvv = cv[:, :c].re("p (a b) -> p a b", a=nseq)
            TS(cvv, ue[:, :, 0:L], cw[:, l, bidx, 0:1], ALU.mult)
            for j in range(1, 4):
                STT(cvv, ue[:, :, j:j + L], cw[:, l, bidx, j:j + 1], cvv, ALU.mult, ALU.add)
            ex = T2("b_ex")
            ACT(ex[:, :c], cv[:, :c], AF.Exp, scale=-1.0)
            RECIP1P(ex[:, :c], ex[:, :c])
            c_ = T2(f"b_cs{i}")
            TT(c_[:, :c], cv[:, :c], ex[:, :c], ALU.mult)
            cs.append(c_)
            yield
            if smp:
                ch0 = i * 768 + h * 128
                for j_ in range(3):
                    S.D("sp", oconv[l, :, j_, ch0:ch0 + 128].rearrange("g p -> p g"), ue[:, :, 8 + j_])
            else:
                CP(ue[:, :, 0:3], ue[:, :, L:L + 3])
                if tile["last"]:
                    ch0 = i * 768 + h * 128
                    S.D("sp", pconv[l, :, ch0:ch0 + 128].rearrange("j p -> p j"), ue[:, 0, 0:3])
        nf = []
        for i in range(2):
            sq = T2("b_sq", BF16); ACT(sq[:, :c], cs[i][:, :c], AF.Square)
            ps = pbank(); MM(ps[:, :c], ones_b, sq[:, :c])
            rn = T2("b_rn"); ACT(rn[:, :c], ps[:, :c], AF.Ln, bias=c_eps)
            ACT(rn[:, :c], rn[:, :c], AF.Exp, scale=-0.5)
            f_ = T2(f"b_nf{i}")
            if i == 0:
                STT(f_[:, :c], cs[0][:, :c], RS, rn[:, :c], ALU.mult, ALU.mult)
            else:
                TT(f_[:, :c], cs[1][:, :c], rn[:, :c], ALU.mult)
            nf.append(f_)
        qf, kf = nf
        yield
        ACT(xa[:, :c], xa[:, :c], AF.Exp)
        ACT(xa[:, :c], xa[:, :c], AF.Ln, bias=c_one[0:1, :])
        gn = R1("b_gn"); TS(gn[:, :c], xa[:, :c], al_row[0:1, l * 6 + h:l * 6 + h + 1], ALU.mult)
        Gn = R1("b_Gn"); scan(Gn[:, :c], resets[0:1, RESET_IDX[gs], :c], gn[:, :c])
        RECIP1P(beta[:, :c], beta[:, :c])
        eG = R1("b_eG"); ACT(eG[:, :c], Gn[:, :c], AF.Exp, scale=-1.0)
        eGe = R1("b_eGe", 16); ACT(eGe[:, :G], g3(Gn, c, G)[:, :, gs - 1], AF.Exp, scale=-1.0)
        df = R1("b_df"); TT(g3(df, c, G), g3(Gn, c, G), g3(Gn, c, G)[:, :, gs - 1:gs].bc([1, G, gs]), ALU.subtract)
        ACT(df[:, :c], df[:, :c], AF.Exp)
        bg = R1("b_bg"); TT(bg[:, :c], beta[:, :c], eG[:, :c], ALU.mult)
        Gr = R1("b_Gr"); TS(Gr[:, :c], Gn[:, :c], -1.0, ALU.mult)
        yield
        pa = pbank()
        MM(pa[:, 0:c], ones_f[0:1, :], beta[:, :c])
        MM(pa[:, 128:128 + c], ones_f[0:1, :], eG[:, :c])
        MM(pa[:, 256:256 + G], ones_f[0:1, :], eGe[:, :G])
        MM(pa[:c, 384:385], beta[:, :c], ones_f[0:1, 0:1])
        MM(pa[:c, 385:386], bg[:, :c], ones_f[0:1, 0:1])
        MM(pa[:c, 386:387], df[:, :c], ones_f[0:1, 0:1])
        bb = tmp("b_bb", [128, 512], F32)
        CP(bb[:, 0:128 + c], pa[:, 0:128 + c])
        CP(bb[:, 256:256 + G], pa[:, 256:256 + G], eng="dve")
        CP(bb[:c, 384:387], pa[:c, 384:387], eng="dve")
        beta_bc, eG_bc, eGe_bc = bb[:, 0:c], bb[:, 128:128 + c], bb[:, 256:256 + G]
        yield
        pd_ = pbank()
        MM(pd_[:c, :c], Gn[:, :c], ones_f[0:1, :c], start=True, stop=False)
        MM(pd_[:c, :c], ones_f[0:1, :c], Gr[:, :c], start=False, stop=True)
        dT = T2("b_dT"); TS(dT[:c, :c], pd_[:c, :c], 0.0, ALU.min)
        ACT(dT[:c, :c], dT[:c, :c], AF.Exp)
        mU, mSU = ("U8", "SU8") if smp else ("U64", "SU64")
        dTS = T2("b_dTS"); TT(dTS[:c, :c], dT[:c, :c], mask(mSU, c), ALU.mult)
        dTI = T2("b_dTI"); TT(dTI[:c, :c], dT[:c, :c], mask(mU, c), ALU.mult)
        yield
        khT = T2("b_khT", BF16); CP(khT[:, :c], kf[:, :c], eng="dve")
        kbT = T2("b_kbT", BF16); TT(kbT[:, :c], kf[:, :c], beta_bc, ALU.mult)
        qhT = T2("b_qhT", BF16); CP(qhT[:, :c], qf[:, :c], eng="dve")
        qgT = T2("b_qgT", BF16); TT(qgT[:, :c], qf[:, :c], eG_bc, ALU.mult)
        pe_ = pbank()
        MM(pe_[:c, 0:c], khT[:, :c], kbT[:, :c])
        MM(pe_[:c, 128:128 + c], khT[:, :c], qhT[:, :c])
        Bm = T2("b_B"); TT(Bm[:c, :c], pe_[:c, 0:c], dTS[:c, :c], ALU.mult)
        Aqk = T2("b_Aqk", BF16); TT(Aqk[:c, :c], pe_[:c, 128:128 + c], dTI[:c, :c], ALU.mult)
        X = T2("b_X"); TT(X[:c, :c], ident[:c, :c], Bm[:c, :c], ALU.subtract)
        pf = pbank(); TR(pf[:c, :c], Bm[:c, :c])
        Am = T2("b_A"); CP(Am[:c, :c], pf[:c, :c])
        yield
        nsq = int(math.log2(gs)) - 1
        for lev in range(nsq):
            pg = pbank()
            MM(pg[:c, 0:c], Bm[:c, :c], Am[:c, :c])
            if lev < nsq - 1:
                MM(pg[:c, 128:128 + c], Am[:c, :c], Bm[:c, :c])
            A2 = T2("b_A"); CP(A2[:c, :c], pg[:c, 0:c])
            if lev < nsq - 1:
                B2 = T2("b_B"); CP(B2[:c, :c], pg[:c, 128:128 + c], eng="dve")
            ph = pbank(); MM(ph[:c, :c], A2[:c, :c], X[:c, :c])
            Xn = T2("b_X"); TT(Xn[:c, :c], X[:c, :c], ph[:c, :c], ALU.add)
            yield
            X, Am = Xn, A2
            if lev < nsq - 1:
                Bm = B2
        pt = pbank()
        TR(pt[:c, 0:128], kf[:, :c])
        TR(pt[:c, 128:256], cs[2][:, :c])
        kbg = T2("b_kbg", BF16); TS(kbg[:c, :], pt[:c, 0:128], bb[:c, 385:386], ALU.mult)
        kd = T2("b_kd", BF16); TS(kd[:c, :], pt[:c, 0:128], bb[:c, 386:387], ALU.mult)
        bv = T2("b_bv", BF16); TS(bv[:c, :], pt[:c, 128:256], bb[:c, 384:385], ALU.mult)
        yield
        Xb = T2("b_Xb", BF16); CP(Xb[:c, :c], X[:c, :c], eng="dve")
        pw = pbank(); MM(pw[:, :c], kbg[:c, :], Xb[:c, :c])
        nWk = T2("b_nWk", BF16); S.I("act", "mul", out=nWk[:, :c], in_=pw[:, :c], mul=-1.0)
        yield
        if G > 1:
            kdm = tmp("b_kdm", [128, 16, 128], BF16, 1)
            o_, g_ = RM_OFF[gs]
            TT(kdm[:c, :G, :], kd[:c, :].un(1).bc([c, G, 128]), rmk[:c, o_:o_ + G].un(2).bc([c, G, 128]), ALU.mult)
        po = LN["X"]
        for g in range(G):
            gi_ = g if smp else 0
            cols = slice(g * gs, (g + 1) * gs)
            pv = pbank()
            MM(pv[:c, 0:128], Xb[:c, :c], bv[:c, :], start=True, stop=False)
            MM(pv[:c, 0:128], nWk[:, :c], Sb_[:, gi_, :], start=False, stop=True)
            Wg = T2("b_Wg", BF16); CP(Wg[:c, :], pv[:c, 0:128])
            yield
            MM(po[:, cols], Sb_[:, gi_, :], qgT[:, cols], start=True, stop=False)
            MM(po[:, cols], Wg[:c, :], Aqk[:c, cols], start=False, stop=True)
            pu = pbank()
            MM(pu[:, 0:128], kdm[:c, g, :] if G > 1 else kd[:c, :], Wg[:c, :])
            STT(Sf[:, gi_, :], Sf[:, gi_, :], eGe_bc[:, g:g + 1], pu[:, 0:128], ALU.mult, ALU.add)
            yield
            if not smp:
                CP(Sb_[:, 0, :], Sf[:, 0, :])
        hT = T2("b_hT"); CP(hT[:, :c], po[:, :c])
        y_finalize(l, 6 + h, ti, tile, hT, zf[:, :c])
        yield
        if smp:
            S.D("sp", oS[l, :, h, :, :].rearrange("g d e -> d g e"), Sf)
        elif tile["last"]:
            S.D("sp", pS[l, h, :, :], Sf[:, 0, :])
        yield

    def hgrn_setup(l, h):
        LN = LANES[cur[0]]
        LN["wp"] = 0
        pCs = LN["pCs"]
        pCb = LN["pCb"]
        wbs = [wblock(l, C_Q + h * 128), wblock(l, C_F + h * 128), wblock(l, C_I + h * 128), wblock(l, Z0 + (12 + h) * 128)]
        S.op("dve", lambda e: e.memset(pCs.ap, 0.0), [], [pCs.tk])
        S.op("dve", lambda e: e.memset(pCb.ap, 0.0), [], [pCb.tk])
        return wbs, None

    def hgrn_tile(l, h, ti, tile, pj, prefetch):
        LN = LANES[cur[0]]
        lbv, omlv = lb[:, l, h:h + 1], oml[:, l, h:h + 1]
        pCs = LN["pCs"]
        pCb = LN["pCb"]
        c, gs, t0 = tile["c"], tile["gC"], tile["t0"]
        G = c // gs
        smp = tile["kind"] == "s"
        if smp:
            Sf, Sb_ = sCc[0], sCb
            S.D("sp", Sf, sH[l, :, h, :, :].rearrange("g d e -> d g e"))
            CP(Sb_, Sf)
        else:
            Sf, Sb_ = pCs, pCb
        (q_ps, f_ps, i_ps, z_ps), _ = pj
        qraw = T2("c_qraw"); CP(qraw[:, :c], q_ps)
        e1 = T2("c_e1"); ACT(e1[:, :c], f_ps, AF.Exp, scale=-1.0)
        vf = T2("c_vf"); CP(vf[:, :c], i_ps)
        zf = T2("c_zf"); CP(zf[:, :c], z_ps)
        prefetch()
        yield
        eq = T2("c_eq"); ACT(eq[:, :c], qraw[:, :c], AF.Exp, scale=-1.0)
        RECIP1P(eq[:, :c], eq[:, :c])
        qf = T2("c_qf"); TT(qf[:, :c], qraw[:, :c], eq[:, :c], ALU.mult)
        yield
        TS(e1[:, :c], e1[:, :c], float(np.exp(60.0)), ALU.min)
        l1 = T2("c_l1"); ACT(l1[:, :c], e1[:, :c], AF.Ln, scale=lbv, bias=c_one)
        l2 = T2("c_l2"); ACT(l2[:, :c], e1[:, :c], AF.Ln, bias=c_one)
        nlf = T2("c_nlf"); TT(nlf[:, :c], l2[:, :c], l1[:, :c], ALU.subtract)
        r_ = T2("c_r"); ACT(r_[:, :c], l2[:, :c], AF.Exp, scale=-1.0)
        kf = T2("c_kf"); STT(kf[:, :c], e1[:, :c], omlv, r_[:, :c], ALU.mult, ALU.mult)
        yield
        bn = T2("c_bn"); scan(bn[:, :c], resets[:, RESET_IDX[gs], :c], nlf[:, :c])
        eb = T2("c_eb"); ACT(eb[:, :c], bn[:, :c], AF.Exp, scale=-1.0)
        enb = T2("c_enb"); ACT(enb[:, :c], bn[:, :c], AF.Exp)
        qeb = T2("c_qeb", BF16); TT(qeb[:, :c], qf[:, :c], eb[:, :c], ALU.mult)
        keb = T2("c_keb", BF16); TT(keb[:, :c], kf[:, :c], enb[:, :c], ALU.mult)
        yield
        ebe = tmp("c_ebe", [128, 16], F32); ACT(ebe[:, :G], g3(bn, c, G)[:, :, gs - 1], AF.Exp, scale=-1.0)
        kdT = T2("c_kdT"); TT(g3(kdT, c, G), g3(bn, c, G), g3(bn, c, G)[:, :, gs - 1:gs].bc([128, G, gs]), ALU.subtract)
        ACT(kdT[:, :c], kdT[:, :c], AF.Exp)
        TT(kdT[:, :c], kdT[:, :c], kf[:, :c], ALU.mult)
        yield
        pt = pbank()
        TR(pt[:c, 0:128], kdT[:, :c])
        TR(pt[:c, 128:256], vf[:, :c])
        kd = T2("c_kd", BF16); CP(kd[:c, :], pt[:c, 0:128])
        vb = T2("c_vb", BF16); CP(vb[:c, :], pt[:c, 128:256], eng="dve")
        yield
        if G > 1:
            kdm = tmp("c_kdm", [128, 16, 128], BF16, 1)
            o_, g_ = RM_OFF[gs]
            TT(kdm[:c, :G, :], kd[:c, :].un(1).bc([c, G, 128]), rmk[:c, o_:o_ + G].un(2).bc([c, G, 128]), ALU.mult)
        pa = pbank(); MM(pa[:c, :c], keb[:, :c], qeb[:, :c])
        mname = "U8" if smp else ("U16" if c == 128 else "U128")
        AT = T2("c_AT", BF16); TT(AT[:c, :c], pa[:c, :c], mask(mname, c), ALU.mult)
        yield
        po = LN["X"]
        MM(po[:, :c], vb[:c, :], AT[:c, :c], start=True, stop=False)
        for g in range(G):
            gi_ = g if smp else 0
            cols = slice(g * gs, (g + 1) * gs)
            MM(po[:, cols], Sb_[:, gi_, :], qeb[:, cols], start=False, stop=(g == G - 1))
            pu = pbank()
            MM(pu[:, 0:128], kdm[:c, g, :] if G > 1 else kd[:c, :], vb[:c, :])
            STT(Sf[:, gi_, :], Sf[:, gi_, :], ebe[:, g:g + 1], pu[:, 0:128], ALU.mult, ALU.add)
            yield
            if not smp:
                CP(Sb_[:, 0, :], Sf[:, 0, :])
        hT = T2("c_hT"); CP(hT[:, :c], po[:, :c])
        y_finalize(l, 12 + h, ti, tile, hT, zf[:, :c])
        yield
        if smp:
            S.D("sp", oH[l, :, h, :, :].rearrange("g d e -> d g e"), Sf)
        elif tile["last"]:
            S.D("sp", pH[l, h, :, :], Sf[:, 0, :])
        yield

    wo_c = []

    def out_stage(l):
        S.region("O")
        if not wo_c:
            wo_c.append(S.sb("wo", [128, 16, D], BF16))
        wo = wo_c[0]
        for q4 in range(4):
            S.D("pool", wo[:, q4 * 4:(q4 + 1) * 4, :], w_out[l, q4 * 512:(q4 + 1) * 512, :].rearrange("(h p) n -> p h n", p=128))
        for ti, tile in enumerate(TILES):
            c, t0 = tile["c"], tile["t0"]
            yt = tmp("op_y", [128, 16, 128], BF16, 2)
            S.D("sp", yt[:, :, :c], yTv[ti].re("h p t -> p h t"))
            xo = tmp("op_xo", [128, NCH, 128], F32, 2)
            S.D("sp", xo[:, :, :c], xrv[ti].re("j p t -> p j t"))
            xn = tmp("xT", [128, NCH, 128], F32, 2)
            for j in range(NCH):
                ps = pbank()
                for hh in range(16):
                    MM(ps[:, :c], wo[:, hh, j * 128:(j + 1) * 128], yt[:, hh, :c], start=(hh == 0), stop=(hh == 15))
                TT(xn[:, j, :c], xo[:, j, :c], ps[:, :c], ALU.add)
            finish_x(ti, tile, xn, l + 1)

    def run_heads(kind, l, hs):
        setup = {"a": mlstm_setup, "b": gdn_setup, "c": hgrn_setup}[kind]
        tilef = {"a": mlstm_tile, "b": gdn_tile, "c": hgrn_tile}[kind]
        S.region("H")
        ws = {}
        bounds = [0]

        def head_gen(ln, h):
            pt_ = [(ti, t) for ti, t in enumerate(TILES) if t["kind"] == "p"]
            nxt = {0: proj(pt_[0][1], *ws[ln])}
            for k, (ti, tile) in enumerate(pt_):
                if k > 0:
                    bounds.append(len(S.rec))
                def prefetch(k=k):
                    if k + 1 < len(pt_):
                        nxt[k + 1] = proj(pt_[k + 1][1], *ws[ln])
                yield from tilef(l, h, ti, tile, nxt[k], prefetch)

        for ln, h in enumerate(hs):
            cur[0] = ln
            ws[ln] = setup(l, h)
        def nsteps(items):
            n, depth = 0, 0
            for it in items:
                if it[0] == "gs":
                    if depth == 0:
                        n += 1
                    depth += 1
                elif it[0] == "ge":
                    depth -= 1
                elif depth == 0:
                    n += 1
            return n

        lanes_segs = []
        for ln, h in enumerate(hs):
            cur[0] = ln
            S.rec = []
            for ti, tile in enumerate(TILES):
                if tile["kind"] == "s":
                    for _ in tilef(l, h, ti, tile, proj(tile, *ws[ln]), lambda: None):
                        pass
            segs = [S.rec]
            bounds.clear()
            S.rec = []
            for _ in head_gen(ln, h):
                pass
            full_ = S.rec
            S.rec = None
            bb_ = [0] + list(bounds) + [len(full_)]
            segs += [full_[bb_[i]:bb_[i + 1]] for i in range(len(bb_) - 1)]
            lanes_segs.append(segs)
        T = nsteps(lanes_segs[0][2])
        streams = []
        for segs in lanes_segs:
            st = list(segs[0])
            st += [("nop",)] * ((-nsteps(st)) % T)
            t0seg = list(segs[1])
            t0seg += [("nop",)] * max(0, T - nsteps(t0seg))
            st += t0seg
            for sg in segs[2:]:
                st += sg
            streams.append(st)
        off = nsteps(lanes_segs[0][0])
        off += (-off) % T
        S.replay(streams, offset=off)
        cur[0] = -1

    def full():
        cur[0] = -1
        stage_a()
        S.barrier()
        for l in range(n_layers):
            for hp in range(3):
                run_heads("a", l, [2 * hp, 2 * hp + 1])
            for hp in range(3):
                run_heads("b", l, [2 * hp, 2 * hp + 1])
            for hp in range(2):
                run_heads("c", l, [2 * hp, 2 * hp + 1])
            S.barrier()
            cur[0] = -1
            out_stage(l)
            S.barrier()

    return nc, S, locals()


def core_inputs(inp, c, consts):
    s = c // 2
    sl = slice(16 * c, 16 * c + 16)
    f = lambda a: np.ascontiguousarray(np.asarray(a, dtype=np.float32))
    m = {
        "xp": f(np.concatenate([inp["meta_tokens"], inp["x_prompt"][s]], axis=0)),
        "xs": f(np.asarray(inp["x_sample"])[sl].reshape(128, D)),
        "w_in": consts["_w_in_r"], "w_gc": consts["_w_gc"], "w_out": f(inp["w_out"]),
        "norm_w": f(inp["norm_w"]), "out_norm_w": f(inp["out_norm_w"]), "final_norm_w": f(inp["final_norm_w"]),
        "mlstm_gate_b": f(inp["mlstm_gate_b"]), "gdn_A_log": f(inp["gdn_A_log"]), "gdn_dt_bias": f(inp["gdn_dt_bias"]),
        "gdn_conv_w": f(inp["gdn_conv_w"]), "hgrn_lower_bounds": f(inp["hgrn_lower_bounds"]),
        "sC": f(np.asarray(inp["state_mlstm_C"])[:, sl]), "sn": f(np.asarray(inp["state_mlstm_n"])[:, sl]),
        "sm": f(np.asarray(inp["state_mlstm_m"])[:, sl]), "sS": f(np.asarray(inp["state_gdn_S"])[:, sl]),
        "sconv": f(np.asarray(inp["state_gdn_conv"])[:, sl]), "sH": f(np.asarray(inp["state_hgrn_S"])[:, sl]),
    }
    m.update({k: v for k, v in consts.items() if not k.startswith("_")})
    return m


def relayout_w_in(w_in):
    w = np.asarray(w_in, dtype=np.float32)
    out = np.empty((2, 70, 128, NCH * 128), np.float32)
    for i, c0 in enumerate(_blk_cols):
        blk = w[:, :, c0:c0 + 128].reshape(2, NCH, 128, 128)
        out[:, i] = blk.transpose(0, 2, 1, 3).reshape(2, 128, NCH * 128)
    g = w[:, :, _gate_cols].reshape(2, NCH, 128, 24).transpose(0, 2, 1, 3)
    return out, np.ascontiguousarray(g)


_CACHE = {}


def kernel(**inputs):
    if "nc" not in _CACHE:
        nc, S, L = build_nc(n_layers=2)
        L["full"]()
        S.finish()
        _CACHE["nc"] = nc
    nc = _CACHE["nc"]
    consts = host_consts()
    consts["_w_in_r"], consts["_w_gc"] = relayout_w_in(inputs["w_in"])
    in_maps = [core_inputs(inputs, c, consts) for c in range(8)]
    res = run_bass_kernel_spmd(nc, in_maps, core_ids=list(range(8)))
    R = res.results
    f = lambda a: np.asarray(a, dtype=np.float32)
    y_prompt = np.stack([f(R[2 * s]["yp"])[16:] for s in range(4)], axis=0)
    y_sample = np.concatenate([f(R[c]["ys"]).reshape(16, 8, D) for c in range(8)], axis=0)
    pst = lambda k: np.stack([f(R[2 * s][k]) for s in range(4)], axis=1)
    sst = lambda k: np.concatenate([f(R[c][k]) for c in range(8)], axis=1)
    return (y_prompt, y_sample,
            pst("pC"), pst("pn"), pst("pm"), pst("pS"), pst("pconv"), pst("pH"),
            sst("oC"), sst("on"), sst("om"), sst("oS"), sst("oconv"), sst("oH"))
```

```python
import contextlib
import numpy as np
import concourse.bass as bass
import concourse.mybir as mybir
from concourse.bass_utils import run_bass_kernel_spmd

F32 = mybir.dt.float32
BF16 = mybir.dt.bfloat16
I32 = mybir.dt.int32
AF = mybir.ActivationFunctionType
ALU = mybir.AluOpType
AX = mybir.AxisListType


class Tk:
    def __init__(self, name, psum=False):
        self.name = name
        self.psum = psum
        self.w = None
        self.r = []


class V:
    def __init__(self, tk, ap):
        self.tk = tk
        self.ap = ap

    def __getitem__(self, k):
        return V(self.tk, self.ap[k])

    def bc(self, shape):
        return V(self.tk, self.ap.broadcast_to(list(shape)))

    def un(self, axis):
        return V(self.tk, self.ap.unsqueeze(axis))

    def re(self, pat, **kw):
        return V(self.tk, self.ap.rearrange(pat, **kw))

    @property
    def shape(self):
        return self.ap.shape


class Sched:
    ENGS = ("pe", "act", "dve", "pool", "sp")
    ROT = 30000

    def __init__(self, nc):
        self.nc = nc
        self.es = contextlib.ExitStack()
        self.q = {e: [] for e in self.ENGS}
        self.cnt = {e: 0 for e in self.ENGS}
        self.nsem = 0
        self.sem = {e: self._newsem(e) for e in self.ENGS}
        self.waited = {e: {} for e in self.ENGS}
        self.dsem = {}
        self.n_ops = 0
        self.sb_off = 0
        self.sb_max = 0
        self.nalloc = 0
        self.rec = None
        self.offs = {"P": 16512}
        self.cur = "P"

    def split(self):
        r0 = self.offs["P"]
        self.offs["H"] = r0
        self.offs["O"] = r0

    def region(self, r):
        self.cur = r

    def _newsem(self, tag):
        self.nsem += 1
        return self.es.enter_context(self.nc.semaphore(f"s{self.nsem}_{tag}"))

    def sb(self, name, shape, dtype):
        nb = 2 if dtype == BF16 else 4
        n = 1
        for d in shape[1:]:
            n *= d
        size = (n * nb + 63) // 64 * 64
        off = self.offs[self.cur]
        self.offs[self.cur] = off + size
        self.sb_off = off + size
        self.sb_max = max(self.sb_max, self.sb_off)
        assert self.sb_off <= 229376, f"SBUF overflow at {name}: {self.sb_off} region {self.cur}"
        self.nalloc += 1
        t = self.nc.alloc_sbuf_tensor_at(f"{name}_{self.nalloc}", list(shape), dtype, offset=off)
        return V(Tk(name), t[:])

    def ps(self, name, shape, dtype=None):
        t = self.nc.alloc_psum_tensor(name, list(shape), F32)
        return V(Tk(name, psum=True), t[:])

    def I(self, eng, meth, **kw):
        reads, writes, args = [], [], {}
        for k, v in kw.items():
            if isinstance(v, V):
                if k in ("out", "accum_out") or v.tk.psum:
                    writes.append(v.tk)
                else:
                    reads.append(v.tk)
                args[k] = v.ap
            else:
                args[k] = v
        self.op(eng, lambda e: getattr(e, meth)(**args), reads, writes)

    def D(self, eng, out, in_, **kw):
        reads, writes = [], []
        names = []
        if isinstance(in_, V):
            reads.append(in_.tk)
            names.append(in_.tk.name)
            in_ = in_.ap
        if isinstance(out, V):
            writes.append(out.tk)
            names.append(out.tk.name)
            out = out.ap
        sbn = [n for n in names if not n.startswith("dram:")]
        key = sbn[0] if sbn else names[0]
        self.dma(eng, out, in_, reads, writes, key=key, **kw)

    def replay(self, lists, offset=0):
        self.rec = None
        idx = [0] * len(lists)
        step = 0
        while any(idx[k] < len(lists[k]) for k in range(len(lists))):
            for k, lst in enumerate(lists):
                if step < k * offset or idx[k] >= len(lst):
                    continue
                depth = 0
                while idx[k] < len(lst):
                    it = lst[idx[k]]
                    idx[k] += 1
                    if it[0] == "nop":
                        pass
                    elif it[0] == "gs":
                        depth += 1
                    elif it[0] == "ge":
                        depth -= 1
                    elif it[0] == "op":
                        self.op(*it[1:])
                    else:
                        self.dma(it[1], it[2], it[3], it[4], it[5], key=it[6], **it[7])
                    if depth == 0:
                        break
            step += 1

    def barrier(self):
        evs = [(self.sem[e], self.cnt[e]) for e in self.ENGS if self.cnt[e] > 0]
        evs += [(sem, val) for (sem, val) in self.dsem.values() if val > 0]
        for e in self.ENGS:
            wd = self.waited[e]
            for (sem, val) in evs:
                if sem is self.sem[e] and e == "pe":
                    continue
                if wd.get(id(sem), (None, 0))[1] >= val:
                    continue
                wd[id(sem)] = (sem, val)
                self.q[e].append(("wait", sem, val))

    def _deps(self, eng, reads, writes):
        deps = []
        for t in reads:
            if t.w is not None:
                deps.append(t.w)
        for t in writes:
            if t.w is not None:
                deps.append(t.w)
            deps.extend(t.r)
        wd = self.waited[eng]
        need = {}
        for (sem, val, e2) in deps:
            if e2 == "pe" and eng == "pe":
                continue
            k = id(sem)
            if wd.get(k, (None, 0))[1] >= val:
                continue
            if k not in need or need[k][1] < val:
                need[k] = (sem, val)
        for k, (sem, val) in need.items():
            wd[k] = (sem, val)
            self.q[eng].append(("wait", sem, val))

    def _record(self, ev, reads, writes):
        for t in reads:
            t.r.append(ev)
        for t in writes:
            t.w = ev
            t.r = []

    def op(self, eng, fn, reads, writes):
        if self.rec is not None:
            self.rec.append(("op", eng, fn, reads, writes))
            return
        self._deps(eng, reads, writes)
        if self.cnt[eng] >= self.ROT:
            self.sem[eng] = self._newsem(eng)
            self.cnt[eng] = 0
        self.cnt[eng] += 1
        ev = (self.sem[eng], self.cnt[eng], eng)
        self.q[eng].append(("op", fn, self.sem[eng], 1))
        self._record(ev, reads, writes)
        self.n_ops += 1

    def dma(self, eng, out, in_, reads, writes, key=None, **kw):
        if self.rec is not None:
            self.rec.append(("dma", eng, out, in_, reads, writes, key, kw))
            return
        self._deps(eng, reads, writes)
        if key is None:
            key = (writes[0] if writes else reads[0]).name
        if key not in self.dsem:
            self.dsem[key] = [self._newsem("d"), 0]
        ds = self.dsem[key]
        ds[1] += 16
        ev = (ds[0], ds[1], "dma")
        self.q[eng].append(("op", (lambda e, out=out, in_=in_, kw=kw: e.dma_start(out=out, in_=in_, **kw)), ds[0], 16))
        self._record(ev, reads, writes)
        self.n_ops += 1

    def finish(self):
        for key, (sem, val) in self.dsem.items():
            if val > 0:
                self.q["sp"].append(("wait", sem, val))
        for e in self.ENGS:
            if e != "sp" and self.cnt[e] > 0:
                self.q["sp"].append(("wait", self.sem[e], self.cnt[e]))
        nc = self.nc
        q = self.q

        def emit(engine, lst):
            for it in lst:
                if it[0] == "wait":
                    engine.wait_ge(it[1], it[2])
                else:
                    ins = it[1](engine)
                    ins.then_inc(it[2], it[3])

        with nc.allow_non_contiguous_dma(reason="small strided state/param transfers"), nc.Block() as block:
            @block.tensor
            def _(e):
                emit(e, q["pe"])

            @block.scalar
            def _(e):
                emit(e, q["act"])

            @block.vector
            def _(e):
                emit(e, q["dve"])

            @block.gpsimd
            def _(e):
                emit(e, q["pool"])

            @block.sync
            def _(e):
                emit(e, q["sp"])
        self.es.close()


D = 2048
NCH = 16
NP = 2064
NS = 128
NT = NP + NS
N_IN = 8984
A_Q, A_K, A_V, A_O, A_I, A_F = 0, 768, 1536, 2304, 3072, 3078
B_Q, B_K, B_V, B_A, B_B = 3084, 3852, 4620, 5388, 5394
C_Q, C_F, C_I, Z0 = 5400, 5912, 6424, 6936
EPS = 1e-6
RS = 128 ** -0.5
LANE_OFFSET = 110
_blk_cols = ([A_Q + i * 128 for i in range(24)] + [B_Q + i * 128 for i in range(18)]
             + [C_Q + i * 128 for i in range(12)] + [Z0 + i * 128 for i in range(16)])
BLK_OF = {c: i for i, c in enumerate(_blk_cols)}
_gate_cols = list(range(A_I, A_I + 12)) + list(range(B_A, B_A + 12))
GC_OF = {c: i for i, c in enumerate(_gate_cols)}
MASK_IDX = {"U128": 0, "U64": 1, "SU64": 2, "U16": 3, "U8": 4, "SU8": 5}
RESET_IDX = {128: 0, 64: 1, 16: 2, 8: 3}
RM_OFF = {64: (0, 2), 16: (2, 8), 8: (10, 16)}


def host_consts():
    idx = np.arange(128)
    masks = np.zeros((6, 128, 128), np.float32)
    for name, gs, strict in (("U128", 128, False), ("U64", 64, False), ("SU64", 64, True),
                             ("U16", 16, False), ("U8", 8, False), ("SU8", 8, True)):
        same = (idx[:, None] // gs) == (idx[None, :] // gs)
        tri = (idx[:, None] < idx[None, :]) if strict else (idx[:, None] <= idx[None, :])
        masks[MASK_IDX[name]] = (same & tri).astype(np.float32)
    resets = np.ones((4, 128, 128), np.float32)
    for gs, i in RESET_IDX.items():
        resets[i][:, (idx % gs) == 0] = 0.0
    rm = np.zeros((128, 26), np.float32)
    for gs, (o, g) in RM_OFF.items():
        for j in range(g):
            rm[(idx // gs) == j, o + j] = 1.0
    return {"c_ident": np.eye(128, dtype=np.float32), "c_masks": masks, "c_resets": resets, "c_rm": rm}


def make_tiles():
    tiles = [dict(kind="p", t0=0, c=16, gA=16, gB=16, gC=16, first=True, last=False)]
    for i in range(16):
        tiles.append(dict(kind="p", t0=16 + 128 * i, c=128, gA=128, gB=64, gC=16, first=False, last=(i == 15)))
    tiles.append(dict(kind="s", t0=NP, c=128, gA=8, gB=8, gC=8, first=True, last=True))
    return tiles


def build_nc(n_layers=2, heads=None, do_out=True, debug=False):
    nc = bass.Bass("TRN2", target_bir_lowering=False)
    S = Sched(nc)

    def din(name, shape):
        return nc.dram_tensor(name, list(shape), F32, kind="ExternalInput").ap()

    def dout(name, shape):
        return nc.dram_tensor(name, list(shape), F32, kind="ExternalOutput").ap()

    xp = din("xp", [NP, D]); xs = din("xs", [NS, D])
    w_in = din("w_in", [2, 70, 128, NCH * 128]); w_gc = din("w_gc", [2, 128, NCH, 24]); w_out = din("w_out", [2, D, D])
    norm_w = din("norm_w", [2, D]); out_norm_w = din("out_norm_w", [2, D]); final_norm_w = din("final_norm_w", [D])
    gate_b = din("mlstm_gate_b", [2, 2, 6]); A_log = din("gdn_A_log", [2, 6]); dt_bias = din("gdn_dt_bias", [2, 6])
    conv_w = din("gdn_conv_w", [2, 4, 2304]); lbp = din("hgrn_lower_bounds", [2, 512])
    sC = din("sC", [2, 16, 6, 128, 128]); sn = din("sn", [2, 16, 6, 128]); sm = din("sm", [2, 16, 6])
    sS = din("sS", [2, 16, 6, 128, 128]); sconv = din("sconv", [2, 16, 3, 2304]); sH = din("sH", [2, 16, 4, 128, 128])
    c_ident = din("c_ident", [128, 128]); c_masks = din("c_masks", [6, 128, 128])
    c_resets = din("c_resets", [4, 128, 128]); c_rm = din("c_rm", [128, 26])

    yp = dout("yp", [NP, D]); ys = dout("ys", [NS, D])
    pC = dout("pC", [2, 6, 128, 128]); pn = dout("pn", [2, 6, 128]); pm = dout("pm", [2, 6])
    pS = dout("pS", [2, 6, 128, 128]); pconv = dout("pconv", [2, 3, 2304]); pH = dout("pH", [2, 4, 128, 128])
    oC = dout("oC", [2, 16, 6, 128, 128]); on = dout("on", [2, 16, 6, 128]); om = dout("om", [2, 16, 6])
    oS = dout("oS", [2, 16, 6, 128, 128]); oconv = dout("oconv", [2, 16, 3, 2304]); oH = dout("oH", [2, 16, 4, 128, 128])

    skind = "ExternalOutput" if debug else "Internal"
    xres_t = nc.dram_tensor("xres", [NCH, 128, NT], F32, kind=skind).ap()
    yT_t = nc.dram_tensor("yTs", [16, 128, NT], BF16, kind=skind).ap()
    xres = V(Tk("dram:xres"), xres_t)
    yTd = V(Tk("dram:yT"), yT_t)

    TILES = make_tiles()

    def ACT(out, in_, func, scale=1.0, bias=None):
        kw = dict(out=out, in_=in_, func=func, scale=scale)
        if bias is not None:
            kw["bias"] = bias
        S.I("act", "activation", **kw)

    def TT(out, in0, in1, op, eng="dve"):
        S.I(eng, "tensor_tensor", out=out, in0=in0, in1=in1, op=op)

    def TS(out, in0, s1, op0, s2=None, op1=None, eng="dve"):
        if op1 is None:
            S.I(eng, "tensor_scalar", out=out, in0=in0, scalar1=s1, scalar2=None, op0=op0)
        else:
            S.I(eng, "tensor_scalar", out=out, in0=in0, scalar1=s1, scalar2=s2, op0=op0, op1=op1)

    def STT(out, in0, scalar, in1, op0, op1):
        S.I("dve", "scalar_tensor_tensor", out=out, in0=in0, scalar=scalar, in1=in1, op0=op0, op1=op1)

    def MM(out, lhsT, rhs, start=True, stop=True):
        S.I("pe", "matmul", out=out, lhsT=lhsT, rhs=rhs, start=start, stop=stop)

    def CP(out, in_, eng="act"):
        if eng == "act":
            S.I("act", "copy", out=out, in_=in_)
        else:
            S.I(eng, "tensor_copy", out=out, in_=in_)

    def RECIP(out, in_):
        ACT(out, in_, AF.Ln)
        ACT(out, out, AF.Exp, scale=-1.0)

    def RECIP1P(out, in_):
        ACT(out, in_, AF.Ln, bias=c_one[0:out.shape[0], :])
        ACT(out, out, AF.Exp, scale=-1.0)

    rings = {}
    cur = [-1]
    LANES = []

    slots = {}

    def tmp(tag, shape, dtype, n=2):
        lane_tag = tag[:2] in ("a_", "b_", "c_") or tag[:3] == "yf_"
        if lane_tag:
            n = 1
        if tag[:2] in ("a_", "b_", "c_"):
            k = (tag[:2], tuple(shape), str(dtype), n)
            d = slots.setdefault(k, {})
            if tag not in d:
                d[tag] = len(d)
            tag = f"mix_{tuple(shape)}_{dtype}_{n}_{d[tag]}"
        if lane_tag:
            tag = f"{tag}@{max(cur[0], 0)}"
        if tag not in rings:
            rings[tag] = [[S.sb(f"{tag}{i}", shape, dtype) for i in range(n)], 0]
        r = rings[tag]
        v = r[0][r[1] % n]
        r[1] += 1
        return v

    ident = S.sb("ident", [128, 128], F32)
    masks = S.sb("masks", [128, 6, 128], F32)
    resets = S.sb("resets", [128, 4, 128], F32)
    rmk = S.sb("rmk", [128, 26], F32)
    ones_f = S.sb("ones_f", [128, 128], F32)
    ones_b = S.sb("ones_b", [128, 128], BF16)
    c_one = S.sb("c_one", [128, 1], F32)
    c_eps = S.sb("c_eps", [128, 1], F32)
    S.D("sp", ident, c_ident)
    S.D("sp", masks, c_masks.rearrange("m p n -> p m n"))
    S.D("sp", resets, c_resets.rearrange("m p n -> p m n"))
    S.D("sp", rmk, c_rm)
    S.I("dve", "memset", ap=ones_f, constant=1.0) if False else S.op("dve", lambda e: e.memset(ones_f.ap, 1.0), [], [ones_f.tk])
    S.op("dve", lambda e: e.memset(ones_b.ap, 1.0), [], [ones_b.tk])
    S.op("dve", lambda e: e.memset(c_one.ap, 1.0), [], [c_one.tk])
    S.op("dve", lambda e: e.memset(c_eps.ap, EPS), [], [c_eps.tk])

    def mask(name, c):
        return masks[:c, MASK_IDX[name], :c]

    def TR(out, in_):
        k = in_.shape[0]
        S.I("pe", "transpose", out=out, in_=in_, identity=ident[:k, :k])

    pb = [S.ps(f"pb{i}", [128, 512]) for i in range(8)]
    pj_sets = [(pb[0], pb[1]), (pb[2], pb[3])]
    pring = [0]

    def pbank():
        if cur[0] < 0:
            b = pb[4 + pring[0] % 4]
            pring[0] += 1
            return b
        L = LANES[cur[0]]
        b = L["ring"][L["rp"] % 2]
        L["rp"] += 1
        return b

    gb_row = S.sb("gb_row", [1, 24], F32)
    ngb_row = S.sb("ngb_row", [1, 24], F32)
    al_row = S.sb("al_row", [1, 12], F32)
    dtb_row = S.sb("dtb_row", [1, 12], F32)
    S.D("sp", gb_row, gate_b.rearrange("l w h -> (l w h)").unsqueeze(0))
    S.D("sp", al_row, A_log.rearrange("l h -> (l h)").unsqueeze(0))
    S.D("sp", dtb_row, dt_bias.rearrange("l h -> (l h)").unsqueeze(0))
    TS(ngb_row, gb_row, -1.0, ALU.mult)
    ACT(al_row, al_row, AF.Exp)
    lbraw = S.sb("lbraw", [128, 2, 4], F32)
    S.D("sp", lbraw, lbp.rearrange("l (h p) -> p l h", p=128))
    lb = S.sb("lb", [128, 2, 4], F32)
    oml = S.sb("oml", [128, 2, 4], F32)
    S.op("dve", lambda e: e.memset(lb.ap, 0.0), [], [lb.tk])
    TT(lb[:, 1, :], lbraw[:, 0, :], lbraw[:, 1, :], ALU.subtract)
    ACT(lb[:, 1, :], lb[:, 1, :], AF.Exp)
    RECIP1P(lb[:, 1, :], lb[:, 1, :])
    TS(oml, lb, -1.0, ALU.mult, 1.0, ALU.add)
    nw = S.sb("nw", [128, 2, NCH], F32)
    onw = S.sb("onw", [128, 2, NCH], F32)
    fnw = S.sb("fnw", [128, NCH], F32)
    S.D("sp", nw, norm_w.rearrange("l (j p) -> p l j", p=128))
    S.D("sp", onw, out_norm_w.rearrange("l (j p) -> p l j", p=128))
    S.D("sp", fnw, final_norm_w.rearrange("(j p) -> p j", p=128))
    cw = S.sb("cw", [128, 2, 18, 4], F32)
    for l_ in range(2):
        for j_ in range(4):
            S.D("sp", cw[:, l_, :, j_], conv_w[l_, j_, :].rearrange("(b p) -> p b", p=128))

    xnT = S.sb("xnT", [128, NCH, NT], BF16)

    def finish_x(ti, tile, xT, layer_next):
        c, t0 = tile["c"], tile["t0"]
        if layer_next < n_layers:
            S.D("sp", xrv[ti].re("j p t -> p j t"), xT[:, :, :c])
        sq = tmp("fx_sq", [128, NCH, 128], BF16, 1)
        S.I("act", "activation", out=sq[:, :, :c], in_=xT[:, :, :c], func=AF.Square)
        ps = pbank()
        for j in range(NCH):
            MM(ps[:, :c], ones_b, sq[:, j, :c], start=(j == 0), stop=(j == NCH - 1))
        rstd = tmp("fx_rstd", [128, 128], F32, 2)
        ACT(rstd[:, :c], ps[:, :c], AF.Ln, scale=1.0 / D, bias=c_eps)
        ACT(rstd[:, :c], rstd[:, :c], AF.Exp, scale=-0.5)
        t1 = tmp("fx_t1", [128, NCH, 128], F32, 1)
        TT(t1[:, :, :c], xT[:, :, :c], rstd[:, :c].un(1).bc([128, NCH, c]), ALU.mult)
        if layer_next < n_layers:
            TT(xnT[:, :, t0:t0 + c], t1[:, :, :c], nw[:, layer_next, :].un(2).bc([128, NCH, c]), ALU.mult)
        else:
            TT(t1[:, :, :c], t1[:, :, :c], fnw.un(2).bc([128, NCH, c]), ALU.mult)
            ot = tmp("sa_x", [128, D], F32, 1)
            for q4 in range(4):
                pt = pbank()
                for jj in range(4):
                    j = q4 * 4 + jj
                    TR(pt[:c, jj * 128:(jj + 1) * 128], t1[:, j, :c])
                CP(ot[:c, q4 * 512:(q4 + 1) * 512], pt[:c, :])
            dst = yp[t0:t0 + c, :] if tile["kind"] == "p" else ys[:, :]
            S.D("sp", dst, ot[:c, :])

    def stage_a():
        S.region("O")
        for ti, tile in enumerate(TILES):
            c, t0 = tile["c"], tile["t0"]
            xt = tmp("sa_x", [128, D], F32, 1)
            src = xp[t0:t0 + c, :] if tile["kind"] == "p" else xs[:, :]
            S.D("sp", xt[:c, :], src)
            xT = tmp("xT", [128, NCH, 128], F32, 2)
            for q4 in range(4):
                pt = pbank()
                for jj in range(4):
                    j = q4 * 4 + jj
                    TR(pt[:, jj * 128:jj * 128 + c], xt[:c, j * 128:(j + 1) * 128])
                CP(xT[:, q4 * 4:(q4 + 1) * 4, :c], pt.re("p (a b) -> p a b", a=4)[:, :, :c])
            finish_x(ti, tile, xT, 0)


    S.split()
    S.region("H")
    sA = [S.sb("sA0", [128, 16, 129], F32)]
    sSb = S.sb("sSb", [128, 16, 128], BF16)
    sB = [sA[0][:, :, 0:128]]
    sBb = sSb
    sCc = [sA[0][:, :, 0:128]]
    sCb = sSb
    mrow_s = S.sb("mrow_s", [1, 16], F32)
    ue_s = [S.sb(f"ue_s{i}", [128, 16, 11], F32) for i in range(3)]
    Cb = S.sb("Cb", [128, 16, 128], BF16)
    nbb = S.sb("nbb", [128, 16, 128], BF16)
    for i_ in range(2):
        bk = pb[4 * i_:4 * i_ + 4]
        LANES.append(dict(
            i=i_, PJ=bk[0], X=bk[1], ring=[bk[2], bk[3]], rp=0,
            wring=[S.sb(f"wblk{i_}_{k}", [128, NCH, 128], BF16) for k in range(5)], wp=0,
            wg=S.sb(f"wg{i_}", [128, NCH, 2], BF16),
            pA=S.sb(f"pA{i_}", [128, 1, 129], F32), mrow_p=S.sb(f"mrow_p{i_}", [1, 16], F32),
            pBs=S.sb(f"pBs{i_}", [128, 1, 128], F32), pBb=S.sb(f"pBb{i_}", [128, 1, 128], BF16),
            pCs=S.sb(f"pCs{i_}", [128, 1, 128], F32), pCb=S.sb(f"pCb{i_}", [128, 1, 128], BF16),
            ue_p=[S.sb(f"ue_p{i_}_{k}", [128, 1, 131], F32) for k in range(3)],
            Cbp=S.sb(f"Cbp{i_}", [128, 1, 128], BF16), nbp=S.sb(f"nbp{i_}", [128, 1, 128], BF16),
        ))

    def wblock(l, col0):
        L = LANES[cur[0]]
        wb = L["wring"][L["wp"] % 5]
        L["wp"] += 1
        S.D("pool", wb, w_in[l, BLK_OF[col0], :, :].rearrange("p (j n) -> p j n", j=NCH))
        return wb

    def wgate(l, col_a, col_b):
        wg = LANES[cur[0]]["wg"]
        S.D("pool", wg[:, :, 0:1], w_gc[l, :, :, GC_OF[col_a]:GC_OF[col_a] + 1])
        S.D("pool", wg[:, :, 1:2], w_gc[l, :, :, GC_OF[col_b]:GC_OF[col_b] + 1])
        return wg

    def proj(tile, wbs, wg):
        if S.rec is not None:
            S.rec.append(("gs",))
        r = proj_(tile, wbs, wg)
        if S.rec is not None:
            S.rec.append(("ge",))
        return r

    def proj_(tile, wbs, wg):
        c, t0 = tile["c"], tile["t0"]
        L = LANES[cur[0]]
        outs = []
        for b, wb in enumerate(wbs):
            bank = L["PJ"] if b < 4 else L["X"]
            o = bank[:, (b % 4) * 128:(b % 4) * 128 + c]
            for j in range(NCH):
                MM(o, wb[:, j, :], xnT[:, j, t0:t0 + c], start=(j == 0), stop=(j == NCH - 1))
            outs.append(o)
        gr = []
        if wg is not None:
            for i in range(2):
                o = L["X"][0:1, 256 + i * 128:256 + i * 128 + c]
                for j in range(NCH):
                    MM(o, wg[:, j, i:i + 1], xnT[:, j, t0:t0 + c], start=(j == 0), stop=(j == NCH - 1))
                gr.append(o)
        return outs, gr

    def par(*fns):
        outer = S.rec
        lists = []
        for f in fns:
            S.rec = []
            f()
            lists.append(S.rec)
        S.rec = outer
        idx = [0] * len(lists)
        while any(idx[k] < len(lists[k]) for k in range(len(lists))):
            for k, lst in enumerate(lists):
                if idx[k] < len(lst):
                    it = lst[idx[k]]
                    idx[k] += 1
                    if outer is not None:
                        outer.append(it)
                    elif it[0] == "op":
                        S.op(*it[1:])
                    else:
                        S.dma(it[1], it[2], it[3], it[4], it[5], key=it[6], **it[7])

    def scan(out, msk, data):
        S.I("dve", "tensor_tensor_scan", out=out, data0=msk, data1=data, initial=0.0, op0=ALU.mult, op1=ALU.add)

    def T2(tag, dtype=F32, n=2):
        return tmp(tag, [128, 128], dtype, n)

    def R1(tag, w=128, n=2):
        return tmp(tag, [1, w], F32, n)

    def g3(v, c, G):
        return v[:, :c].re("p (a b) -> p a b", a=G)

    def y_finalize(l, hg, ti, tile, hT, z_ps):
        c, t0 = tile["c"], tile["t0"]
        zs = T2("yf_zs")
        y32 = T2("yf_y32")

        def chain_z():
            ez = T2("yf_ez")
            ACT(ez[:, :c], z_ps, AF.Exp, scale=-1.0)
            RECIP1P(ez[:, :c], ez[:, :c])
            STT(zs[:, :c], z_ps, onw[:, l, hg:hg + 1], ez[:, :c], ALU.mult, ALU.mult)

        def chain_h():
            sq = T2("yf_sq", BF16)
            ACT(sq[:, :c], hT[:, :c], AF.Square)
            ps = pbank()
            MM(ps[:, :c], ones_b, sq[:, :c])
            rstd = T2("yf_rstd")
            ACT(rstd[:, :c], ps[:, :c], AF.Ln, scale=1.0 / 128, bias=c_eps)
            ACT(rstd[:, :c], rstd[:, :c], AF.Exp, scale=-0.5)
            TT(y32[:, :c], hT[:, :c], rstd[:, :c], ALU.mult)

        par(chain_h, chain_z)
        yb = T2("yf_yb", BF16)
        TT(yb[:, :c], y32[:, :c], zs[:, :c], ALU.mult)
        S.D("sp", yTv[ti][hg, :, :], yb[:, :c])

    yTv = [V(Tk(f"dram:yT{ti}"), yT_t[:, :, t["t0"]:t["t0"] + t["c"]]) for ti, t in enumerate(TILES)]
    xrv = [V(Tk(f"dram:xr{ti}"), xres_t[:, :, t["t0"]:t["t0"] + t["c"]]) for ti, t in enumerate(TILES)]

    def mlstm_setup(l, h):
        LN = LANES[cur[0]]
        LN["wp"] = 0
        pA = LN["pA"]
        mrow_p = LN["mrow_p"]
        wbs = [wblock(l, A_Q + h * 128), wblock(l, A_K + h * 128), wblock(l, A_V + h * 128),
               wblock(l, A_O + h * 128), wblock(l, Z0 + h * 128)]
        wg = wgate(l, A_I + h, A_F + h)
        S.op("dve", lambda e: e.memset(pA.ap, 0.0), [], [pA.tk])
        S.op("dve", lambda e: e.memset(mrow_p.ap, 0.0), [], [mrow_p.tk])
        return wbs, wg

    def mlstm_tile(l, h, ti, tile, pj, prefetch):
        LN = LANES[cur[0]]
        pA = LN["pA"]
        mrow_p = LN["mrow_p"]
        c, gs, t0 = tile["c"], tile["gA"], tile["t0"]
        G = c // gs
        smp = tile["kind"] == "s"
        if smp:
            St = sA[0]
            S.D("sp", St[:, :, 0:128], sC[l, :, h, :, :].rearrange("g d e -> d g e"))
            S.D("sp", St[:, :, 128:129], sn[l, :, h, :].rearrange("g d -> d g").unsqueeze(2))
            mrow = mrow_s
            S.D("sp", mrow, sm[l, :, h].unsqueeze(0))
        else:
            St, mrow = pA, mrow_p
        (q_ps, k_ps, v_ps, o_ps, z_ps), (gi_ps, gf_ps) = pj
        qT = T2("a_qT", BF16); CP(qT[:, :c], q_ps)
        kTf = T2("a_kTf"); S.I("act", "mul", out=kTf[:, :c], in_=k_ps, mul=RS)
        kT = T2("a_kT", BF16); CP(kT[:, :c], kTf[:, :c], eng="dve")
        vTf = T2("a_vTf"); CP(vTf[:, :c], v_ps)
        eo = T2("a_eo"); ACT(eo[:, :c], o_ps, AF.Exp, scale=-1.0)
        zf = T2("a_zf"); CP(zf[:, :c], z_ps)
        li = R1("a_li"); TS(li[:, :c], gi_ps, gb_row[0:1, l * 12 + h:l * 12 + h + 1], ALU.add)
        e1 = R1("a_e1"); ACT(e1[:, :c], gf_ps, AF.Exp, scale=-1.0, bias=ngb_row[0:1, l * 12 + 6 + h:l * 12 + 7 + h])
        prefetch()
        yield
        sp_ = R1("a_sp"); ACT(sp_[:, :c], e1[:, :c], AF.Ln, bias=c_one[0:1, :])
        Fn = R1("a_Fn"); scan(Fn[:, :c], resets[0:1, RESET_IDX[gs], :c], sp_[:, :c])
        gg = R1("a_g"); TT(gg[:, :c], li[:, :c], Fn[:, :c], ALU.add)
        gmax = R1("a_gmax", 16)
        S.I("dve", "tensor_reduce", out=gmax[:, :G], in_=g3(gg, c, G), axis=AX.X, op=ALU.max)
        Mb = R1("a_Mb", 16); TT(Mb[:, :G], gmax[:, :G], mrow[:, :G], ALU.max)
        al = R1("a_al", 16); TT(al[:, :G], mrow[:, :G], Mb[:, :G], ALU.subtract)
        ACT(al[:, :G], al[:, :G], AF.Exp)
        TT(mrow[:, :G], Mb[:, :G], g3(Fn, c, G)[:, :, gs - 1], ALU.subtract)
        w = R1("a_w"); TT(g3(w, c, G), g3(gg, c, G), Mb[:, :G].un(2).bc([1, G, gs]), ALU.subtract)
        ACT(w[:, :c], w[:, :c], AF.Exp)
        thr = R1("a_thr"); TT(g3(thr, c, G), g3(Fn, c, G), Mb[:, :G].un(2).bc([1, G, gs]), ALU.subtract)
        yield
        pa = pbank()
        MM(pa[:, 0:c], ones_f[0:1, :], thr[:, :c])
        MM(pa[:, 128:128 + G], ones_f[0:1, :], al[:, :G])
        MM(pa[:c, 256:257], w[:, :c], ones_f[0:1, 0:1])
        thrS = T2("a_thrS"); ACT(thrS[:, :c], pa[:, 0:c], AF.Exp)
        ab = tmp("a_ab", [128, 16], F32); CP(ab[:, :G], pa[:, 128:128 + G])
        wcol = tmp("a_wcol", [128, 1], F32); CP(wcol[:c, :], pa[:c, 256:257])
        yield
        pbt = pbank()
        TR(pbt[:c, 0:128], vTf[:, :c])
        TR(pbt[:c, 128:256], kTf[:, :c])
        vp = tmp("a_vp", [128, 129], BF16)
        TS(vp[:c, 0:128], pbt[:c, 0:128], wcol[:c, 0:1], ALU.mult)
        CP(vp[:c, 128:129], wcol[:c, :], eng="dve")
        kt = T2("a_kt", BF16); CP(kt[:c, :], pbt[:c, 128:256])
        wbc = T2("a_wbc", BF16); TS(wbc[:c, :], ones_f[:c, :], wcol[:c, 0:1], ALU.mult)
        yield
        pc = pbank(); MM(pc[:c, :c], kT[:, :c], qT[:, :c])
        PT = T2("a_PT", BF16); TT(PT[:c, :c], pc[:c, :c], mask("U8" if smp else "U128", c), ALU.mult)
        yield
        Cb_, nb_ = (Cb, nbb) if smp else (LN["Cbp"], LN["nbp"])
        for g in range(G):
            ACT(Cb_[:, g, :], St[:, g, 0:128], AF.Copy, scale=ab[:, g:g + 1])
            ACT(nb_[:, g, :], St[:, g, 128:129].bc([128, 128]), AF.Copy, scale=ab[:, g:g + 1])
        pd_ = pbank()
        MM(pd_[:, :c], wbc[:c, :], PT[:c, :c], start=True, stop=False)
        for g in range(G):
            MM(pd_[:, g * gs:(g + 1) * gs], nb_[:, g, :], qT[:, g * gs:(g + 1) * gs], start=False, stop=(g == G - 1))
        denS = T2("a_denS"); CP(denS[:, :c], pd_[:, :c])
        dmax = T2("a_dmax"); STT(dmax[:, :c], denS[:, :c], -1.0, denS[:, :c], ALU.mult, ALU.max)
        TT(dmax[:, :c], dmax[:, :c], thrS[:, :c], ALU.max)
        yield
        RECIP(dmax[:, :c], dmax[:, :c])
        pn_ = pbank()
        MM(pn_[:, :c], vp[:c, 0:128], PT[:c, :c], start=True, stop=False)
        for g in range(G):
            MM(pn_[:, g * gs:(g + 1) * gs], Cb_[:, g, :], qT[:, g * gs:(g + 1) * gs], start=False, stop=(g == G - 1))
        hT = T2("a_hT"); TT(hT[:, :c], pn_[:, :c], dmax[:, :c], ALU.mult)
        RECIP1P(eo[:, :c], eo[:, :c])
        TT(hT[:, :c], hT[:, :c], eo[:, :c], ALU.mult)
        y_finalize(l, h, ti, tile, hT, zf[:, :c])
        yield
        if G > 1:
            vpm = tmp("a_vpm", [128, 16, 129], BF16, 1)
            o_, g_ = RM_OFF[gs]
            TT(vpm[:c, :G, :], vp[:c, :].un(1).bc([c, G, 129]), rmk[:c, o_:o_ + G].un(2).bc([c, G, 129]), ALU.mult)
        for g in range(G):
            pu = pbank()
            MM(pu[:, 0:129], kt[:c, :], vpm[:c, g, :] if G > 1 else vp[:c, :])
            STT(St[:, g, :], St[:, g, :], ab[:, g:g + 1], pu[:, 0:129], ALU.mult, ALU.add)
            yield
        if smp:
            S.D("sp", oC[l, :, h, :, :].rearrange("g d e -> d g e"), St[:, :, 0:128])
            S.D("sp", on[l, :, h, :].rearrange("g d -> d g").unsqueeze(2), St[:, :, 128:129])
            S.D("sp", om[l, :, h].unsqueeze(0), mrow)
        elif tile["last"]:
            S.D("sp", pC[l, h, :, :], St[:, 0, 0:128])
            S.D("sp", pn[l, h, :].unsqueeze(1), St[:, 0, 128:129])
            S.D("sp", pm[l, h:h + 1].unsqueeze(0), mrow[:, 0:1])
        yield

    import math

    def gdn_setup(l, h):
        LN = LANES[cur[0]]
        LN["wp"] = 0
        pBs = LN["pBs"]
        pBb = LN["pBb"]
        ue_p = LN["ue_p"]
        wbs = [wblock(l, B_Q + h * 128), wblock(l, B_K + h * 128), wblock(l, B_V + h * 128), wblock(l, Z0 + (6 + h) * 128)]
        wg = wgate(l, B_A + h, B_B + h)
        S.op("dve", lambda e: e.memset(pBs.ap, 0.0), [], [pBs.tk])
        S.op("dve", lambda e: e.memset(pBb.ap, 0.0), [], [pBb.tk])
        for i in range(3):
            S.op("dve", lambda e, i=i: e.memset(ue_p[i].ap, 0.0), [], [ue_p[i].tk])
        return wbs, wg

    def gdn_tile(l, h, ti, tile, pj, prefetch):
        LN = LANES[cur[0]]
        pBs = LN["pBs"]
        pBb = LN["pBb"]
        ue_p = LN["ue_p"]
        c, gs, t0 = tile["c"], tile["gB"], tile["t0"]
        G = c // gs
        smp = tile["kind"] == "s"
        nseq, L = (16, 8) if smp else (1, c)
        if smp:
            Sf, Sb_ = sB[0], sBb
            S.D("sp", Sf, sS[l, :, h, :, :].rearrange("g d e -> d g e"))
            CP(Sb_, Sf)
            for i in range(3):
                ch0 = i * 768 + h * 128
                for j_ in range(3):
                    S.D("sp", ue_s[i][:, :, j_], sconv[l, :, j_, ch0:ch0 + 128].rearrange("g p -> p g"))
        else:
            Sf, Sb_ = pBs, pBb
        (q_ps, k_ps, v_ps, z_ps), (ga_ps, gb_ps) = pj
        for i, p_ in enumerate((q_ps, k_ps, v_ps)):
            ue = ue_s[i] if smp else ue_p[i]
            CP(ue[:, :, 3:3 + L], p_.re("p (a b) -> p a b", a=nseq))
        zf = T2("b_zf"); CP(zf[:, :c], z_ps)
        xa = R1("b_xa"); TS(xa[:, :c], ga_ps, dtb_row[0:1, l * 6 + h:l * 6 + h + 1], ALU.add)
        beta = R1("b_beta"); ACT(beta[:, :c], gb_ps, AF.Exp, scale=-1.0)
        prefetch()
        yield
        cs = [None, None, None]

        def conv_chain(i):
            def f():
                ue = ue_s[i] if smp else ue_p[i]
                bidx = i * 6 + h
                cv = T2(f"b_cv{i}")
                cvv = cv[:, :c].re("p (a b) -> p a b", a=nseq)
                TS(cvv, ue[:, :, 0:L], cw[:, l, bidx, 0:1], ALU.mult)
                for j in range(1, 4):
                    STT(cvv, ue[:, :, j:j + L], cw[:, l, bidx, j:j + 1], cvv, ALU.mult, ALU.add)
                ex = T2(f"b_ex{i}")
                ACT(ex[:, :c], cv[:, :c], AF.Exp, scale=-1.0)
                RECIP1P(ex[:, :c], ex[:, :c])
                c_ = T2(f"b_cs{i}")
                TT(c_[:, :c], cv[:, :c], ex[:, :c], ALU.mult)
                cs[i] = c_
                if smp:
                    ch0 = i * 768 + h * 128
                    for j_ in range(3):
                        S.D("sp", oconv[l, :, j_, ch0:ch0 + 128].rearrange("g p -> p g"), ue[:, :, 8 + j_])
                else:
                    CP(ue[:, :, 0:3], ue[:, :, L:L + 3])
                    if tile["last"]:
                        ch0 = i * 768 + h * 128
                        S.D("sp", pconv[l, :, ch0:ch0 + 128].rearrange("j p -> p j"), ue[:, 0, 0:3])
            return f

        GR = {}

        def gate_chain():
            ACT(xa[:, :c], xa[:, :c], AF.Exp)
            ACT(xa[:, :c], xa[:, :c], AF.Ln, bias=c_one[0:1, :])
            gn = R1("b_gn"); TS(gn[:, :c], xa[:, :c], al_row[0:1, l * 6 + h:l * 6 + h + 1], ALU.mult)
            Gn = R1("b_Gn"); scan(Gn[:, :c], resets[0:1, RESET_IDX[gs], :c], gn[:, :c])
            RECIP1P(beta[:, :c], beta[:, :c])
            eG = R1("b_eG"); ACT(eG[:, :c], Gn[:, :c], AF.Exp, scale=-1.0)
            eGe = R1("b_eGe", 16); ACT(eGe[:, :G], g3(Gn, c, G)[:, :, gs - 1], AF.Exp, scale=-1.0)
            df = R1("b_df"); TT(g3(df, c, G), g3(Gn, c, G), g3(Gn, c, G)[:, :, gs - 1:gs].bc([1, G, gs]), ALU.subtract)
            ACT(df[:, :c], df[:, :c], AF.Exp)
            bg = R1("b_bg"); TT(bg[:, :c], beta[:, :c], eG[:, :c], ALU.mult)
            Gr = R1("b_Gr"); TS(Gr[:, :c], Gn[:, :c], -1.0, ALU.mult)
            GR.update(Gn=Gn, eG=eG, eGe=eGe, df=df, bg=bg, Gr=Gr)

        par(conv_chain(0), conv_chain(1), conv_chain(2), gate_chain)
        Gn, eG, eGe, df, bg, Gr = GR["Gn"], GR["eG"], GR["eGe"], GR["df"], GR["bg"], GR["Gr"]
        yield
        nf = []
        for i in range(2):
            sq = T2("b_sq", BF16); ACT(sq[:, :c], cs[i][:, :c], AF.Square)
            ps = pbank(); MM(ps[:, :c], ones_b, sq[:, :c])
            rn = T2("b_rn"); ACT(rn[:, :c], ps[:, :c], AF.Ln, bias=c_eps)
            ACT(rn[:, :c], rn[:, :c], AF.Exp, scale=-0.5)
            f_ = T2(f"b_nf{i}")
            if i == 0:
                STT(f_[:, :c], cs[0][:, :c], RS, rn[:, :c], ALU.mult, ALU.mult)
            else:
                TT(f_[:, :c], cs[1][:, :c], rn[:, :c], ALU.mult)
            nf.append(f_)
        qf, kf = nf
        yield
        pa = pbank()
        MM(pa[:, 0:c], ones_f[0:1, :], beta[:, :c])
        MM(pa[:, 128:128 + c], ones_f[0:1, :], eG[:, :c])
        MM(pa[:, 256:256 + G], ones_f[0:1, :], eGe[:, :G])
        MM(pa[:c, 384:385], beta[:, :c], ones_f[0:1, 0:1])
        MM(pa[:c, 385:386], bg[:, :c], ones_f[0:1, 0:1])
        MM(pa[:c, 386:387], df[:, :c], ones_f[0:1, 0:1])
        bb = tmp("b_bb", [128, 512], F32)
        CP(bb[:, 0:128 + c], pa[:, 0:128 + c])
        CP(bb[:, 256:256 + G], pa[:, 256:256 + G], eng="dve")
        CP(bb[:c, 384:387], pa[:c, 384:387], eng="dve")
        beta_bc, eG_bc, eGe_bc = bb[:, 0:c], bb[:, 128:128 + c], bb[:, 256:256 + G]
        yield
        pd_ = pbank()
        MM(pd_[:c, :c], Gn[:, :c], ones_f[0:1, :c], start=True, stop=False)
        MM(pd_[:c, :c], ones_f[0:1, :c], Gr[:, :c], start=False, stop=True)
        dT = T2("b_dT"); TS(dT[:c, :c], pd_[:c, :c], 0.0, ALU.min)
        ACT(dT[:c, :c], dT[:c, :c], AF.Exp)
        mU, mSU = ("U8", "SU8") if smp else ("U64", "SU64")
        dTS = T2("b_dTS"); TT(dTS[:c, :c], dT[:c, :c], mask(mSU, c), ALU.mult)
        dTI = T2("b_dTI"); TT(dTI[:c, :c], dT[:c, :c], mask(mU, c), ALU.mult)
        yield
        khT = T2("b_khT", BF16); CP(khT[:, :c], kf[:, :c], eng="dve")
        kbT = T2("b_kbT", BF16); TT(kbT[:, :c], kf[:, :c], beta_bc, ALU.mult)
        qhT = T2("b_qhT", BF16); CP(qhT[:, :c], qf[:, :c], eng="dve")
        qgT = T2("b_qgT", BF16); TT(qgT[:, :c], qf[:, :c], eG_bc, ALU.mult)
        pe_ = pbank()
        MM(pe_[:c, 0:c], khT[:, :c], kbT[:, :c])
        MM(pe_[:c, 128:128 + c], khT[:, :c], qhT[:, :c])
        Bm = T2("b_B"); TT(Bm[:c, :c], pe_[:c, 0:c], dTS[:c, :c], ALU.mult)
        Aqk = T2("b_Aqk", BF16); TT(Aqk[:c, :c], pe_[:c, 128:128 + c], dTI[:c, :c], ALU.mult)
        X = T2("b_X"); TT(X[:c, :c], ident[:c, :c], Bm[:c, :c], ALU.subtract)
        pf = pbank(); TR(pf[:c, :c], Bm[:c, :c])
        Am = T2("b_A"); CP(Am[:c, :c], pf[:c, :c])
        yield
        nsq = int(math.log2(gs)) - 1
        for lev in range(nsq):
            pg = pbank()
            MM(pg[:c, 0:c], Bm[:c, :c], Am[:c, :c])
            if lev < nsq - 1:
                MM(pg[:c, 128:128 + c], Am[:c, :c], Bm[:c, :c])
            A2 = T2("b_A"); CP(A2[:c, :c], pg[:c, 0:c])
            if lev < nsq - 1:
                B2 = T2("b_B"); CP(B2[:c, :c], pg[:c, 128:128 + c], eng="dve")
            ph = pbank(); MM(ph[:c, :c], A2[:c, :c], X[:c, :c])
            Xn = T2("b_X"); TT(Xn[:c, :c], X[:c, :c], ph[:c, :c], ALU.add)
            yield
            X, Am = Xn, A2
            if lev < nsq - 1:
                Bm = B2
        pt = pbank()
        TR(pt[:c, 0:128], kf[:, :c])
        TR(pt[:c, 128:256], cs[2][:, :c])
        kbg = T2("b_kbg", BF16); TS(kbg[:c, :], pt[:c, 0:128], bb[:c, 385:386], ALU.mult)
        kd = T2("b_kd", BF16); TS(kd[:c, :], pt[:c, 0:128], bb[:c, 386:387], ALU.mult)
        bv = T2("b_bv", BF16); TS(bv[:c, :], pt[:c, 128:256], bb[:c, 384:385], ALU.mult)
        yield
        Xb = T2("b_Xb", BF16); CP(Xb[:c, :c], X[:c, :c], eng="dve")
        pw = pbank(); MM(pw[:, :c], kbg[:c, :], Xb[:c, :c])
        nWk = T2("b_nWk", BF16); S.I("act", "mul", out=nWk[:, :c], in_=pw[:, :c], mul=-1.0)
        yield
        if G > 1:
            kdm = tmp("b_kdm", [128, 16, 128], BF16, 1)
            o_, g_ = RM_OFF[gs]
            TT(kdm[:c, :G, :], kd[:c, :].un(1).bc([c, G, 128]), rmk[:c, o_:o_ + G].un(2).bc([c, G, 128]), ALU.mult)
        po = LN["X"]
        for g in range(G):
            gi_ = g if smp else 0
            cols = slice(g * gs, (g + 1) * gs)
            pv = pbank()
            MM(pv[:c, 0:128], Xb[:c, :c], bv[:c, :], start=True, stop=False)
            MM(pv[:c, 0:128], nWk[:, :c], Sb_[:, gi_, :], start=False, stop=True)
            Wg = T2("b_Wg", BF16); CP(Wg[:c, :], pv[:c, 0:128])
            yield
            MM(po[:, cols], Sb_[:, gi_, :], qgT[:, cols], start=True, stop=False)
            MM(po[:, cols], Wg[:c, :], Aqk[:c, cols], start=False, stop=True)
            pu = pbank()
            MM(pu[:, 0:128], kdm[:c, g, :] if G > 1 else kd[:c, :], Wg[:c, :])
            STT(Sf[:, gi_, :], Sf[:, gi_, :], eGe_bc[:, g:g + 1], pu[:, 0:128], ALU.mult, ALU.add)
            yield
            if not smp:
                CP(Sb_[:, 0, :], Sf[:, 0, :])
        hT = T2("b_hT"); CP(hT[:, :c], po[:, :c])
        y_finalize(l, 6 + h, ti, tile, hT, zf[:, :c])
        yield
        if smp:
            S.D("sp", oS[l, :, h, :, :].rearrange("g d e -> d g e"), Sf)
        elif tile["last"]:
            S.D("sp", pS[l, h, :, :], Sf[:, 0, :])
        yield

    def hgrn_setup(l, h):
        LN = LANES[cur[0]]
        LN["wp"] = 0
        pCs = LN["pCs"]
        pCb = LN["pCb"]
        wbs = [wblock(l, C_Q + h * 128), wblock(l, C_F + h * 128), wblock(l, C_I + h * 128), wblock(l, Z0 + (12 + h) * 128)]
        S.op("dve", lambda e: e.memset(pCs.ap, 0.0), [], [pCs.tk])
        S.op("dve", lambda e: e.memset(pCb.ap, 0.0), [], [pCb.tk])
        return wbs, None

    def hgrn_tile(l, h, ti, tile, pj, prefetch):
        LN = LANES[cur[0]]
        lbv, omlv = lb[:, l, h:h + 1], oml[:, l, h:h + 1]
        pCs = LN["pCs"]
        pCb = LN["pCb"]
        c, gs, t0 = tile["c"], tile["gC"], tile["t0"]
        G = c // gs
        smp = tile["kind"] == "s"
        if smp:
            Sf, Sb_ = sCc[0], sCb
            S.D("sp", Sf, sH[l, :, h, :, :].rearrange("g d e -> d g e"))
            CP(Sb_, Sf)
        else:
            Sf, Sb_ = pCs, pCb
        (q_ps, f_ps, i_ps, z_ps), _ = pj
        qraw = T2("c_qraw"); CP(qraw[:, :c], q_ps)
        e1 = T2("c_e1"); ACT(e1[:, :c], f_ps, AF.Exp, scale=-1.0)
        vf = T2("c_vf"); CP(vf[:, :c], i_ps)
        zf = T2("c_zf"); CP(zf[:, :c], z_ps)
        prefetch()
        yield
        eq = T2("c_eq"); ACT(eq[:, :c], qraw[:, :c], AF.Exp, scale=-1.0)
        RECIP1P(eq[:, :c], eq[:, :c])
        qf = T2("c_qf"); TT(qf[:, :c], qraw[:, :c], eq[:, :c], ALU.mult)
        yield
        TS(e1[:, :c], e1[:, :c], float(np.exp(60.0)), ALU.min)
        l1 = T2("c_l1"); ACT(l1[:, :c], e1[:, :c], AF.Ln, scale=lbv, bias=c_one)
        l2 = T2("c_l2"); ACT(l2[:, :c], e1[:, :c], AF.Ln, bias=c_one)
        nlf = T2("c_nlf"); TT(nlf[:, :c], l2[:, :c], l1[:, :c], ALU.subtract)
        r_ = T2("c_r"); ACT(r_[:, :c], l2[:, :c], AF.Exp, scale=-1.0)
        kf = T2("c_kf"); STT(kf[:, :c], e1[:, :c], omlv, r_[:, :c], ALU.mult, ALU.mult)
        yield
        bn = T2("c_bn"); scan(bn[:, :c], resets[:, RESET_IDX[gs], :c], nlf[:, :c])
        eb = T2("c_eb"); ACT(eb[:, :c], bn[:, :c], AF.Exp, scale=-1.0)
        enb = T2("c_enb"); ACT(enb[:, :c], bn[:, :c], AF.Exp)
        qeb = T2("c_qeb", BF16); TT(qeb[:, :c], qf[:, :c], eb[:, :c], ALU.mult)
        keb = T2("c_keb", BF16); TT(keb[:, :c], kf[:, :c], enb[:, :c], ALU.mult)
        yield
        ebe = tmp("c_ebe", [128, 16], F32); ACT(ebe[:, :G], g3(bn, c, G)[:, :, gs - 1], AF.Exp, scale=-1.0)
        kdT = T2("c_kdT"); TT(g3(kdT, c, G), g3(bn, c, G), g3(bn, c, G)[:, :, gs - 1:gs].bc([128, G, gs]), ALU.subtract)
        ACT(kdT[:, :c], kdT[:, :c], AF.Exp)
        TT(kdT[:, :c], kdT[:, :c], kf[:, :c], ALU.mult)
        yield
        pt = pbank()
        TR(pt[:c, 0:128], kdT[:, :c])
        TR(pt[:c, 128:256], vf[:, :c])
        kd = T2("c_kd", BF16); CP(kd[:c, :], pt[:c, 0:128])
        vb = T2("c_vb", BF16); CP(vb[:c, :], pt[:c, 128:256], eng="dve")
        yield
        if G > 1:
            kdm = tmp("c_kdm", [128, 16, 128], BF16, 1)
            o_, g_ = RM_OFF[gs]
            TT(kdm[:c, :G, :], kd[:c, :].un(1).bc([c, G, 128]), rmk[:c, o_:o_ + G].un(2).bc([c, G, 128]), ALU.mult)
        pa = pbank(); MM(pa[:c, :c], keb[:, :c], qeb[:, :c])
        mname = "U8" if smp else ("U16" if c == 128 else "U128")
        AT = T2("c_AT", BF16); TT(AT[:c, :c], pa[:c, :c], mask(mname, c), ALU.mult)
        yield
        po = LN["X"]
        MM(po[:, :c], vb[:c, :], AT[:c, :c], start=True, stop=False)
        for g in range(G):
            gi_ = g if smp else 0
            cols = slice(g * gs, (g + 1) * gs)
            MM(po[:, cols], Sb_[:, gi_, :], qeb[:, cols], start=False, stop=(g == G - 1))
            pu = pbank()
            MM(pu[:, 0:128], kdm[:c, g, :] if G > 1 else kd[:c, :], vb[:c, :])
            STT(Sf[:, gi_, :], Sf[:, gi_, :], ebe[:, g:g + 1], pu[:, 0:128], ALU.mult, ALU.add)
            yield
            if not smp:
                CP(Sb_[:, 0, :], Sf[:, 0, :])
        hT = T2("c_hT"); CP(hT[:, :c], po[:, :c])
        y_finalize(l, 12 + h, ti, tile, hT, zf[:, :c])
        yield
        if smp:
            S.D("sp", oH[l, :, h, :, :].rearrange("g d e -> d g e"), Sf)
        elif tile["last"]:
            S.D("sp", pH[l, h, :, :], Sf[:, 0, :])
        yield

    wo_c = []

    def out_stage(l):
        S.region("O")
        if not wo_c:
            wo_c.append(S.sb("wo", [128, 16, D], BF16))
        wo = wo_c[0]
        for q4 in range(4):
            S.D("pool", wo[:, q4 * 4:(q4 + 1) * 4, :], w_out[l, q4 * 512:(q4 + 1) * 512, :].rearrange("(h p) n -> p h n", p=128))
        for ti, tile in enumerate(TILES):
            c, t0 = tile["c"], tile["t0"]
            yt = tmp("op_y", [128, 16, 128], BF16, 2)
            S.D("sp", yt[:, :, :c], yTv[ti].re("h p t -> p h t"))
            xo = tmp("op_xo", [128, NCH, 128], F32, 2)
            S.D("sp", xo[:, :, :c], xrv[ti].re("j p t -> p j t"))
            xn = tmp("xT", [128, NCH, 128], F32, 2)
            for j in range(NCH):
                ps = pbank()
                for hh in range(16):
                    MM(ps[:, :c], wo[:, hh, j * 128:(j + 1) * 128], yt[:, hh, :c], start=(hh == 0), stop=(hh == 15))
                TT(xn[:, j, :c], xo[:, j, :c], ps[:, :c], ALU.add)
            finish_x(ti, tile, xn, l + 1)

    def run_heads(kind, l, hs):
        setup = {"a": mlstm_setup, "b": gdn_setup, "c": hgrn_setup}[kind]
        tilef = {"a": mlstm_tile, "b": gdn_tile, "c": hgrn_tile}[kind]
        S.region("H")
        ws = {}
        bounds = [0]

        def head_gen(ln, h):
            pt_ = [(ti, t) for ti, t in enumerate(TILES) if t["kind"] == "p"]
            nxt = {0: proj(pt_[0][1], *ws[ln])}
            for k, (ti, tile) in enumerate(pt_):
                if k > 0:
                    bounds.append(len(S.rec))
                def prefetch(k=k):
                    if k + 1 < len(pt_):
                        nxt[k + 1] = proj(pt_[k + 1][1], *ws[ln])
                yield from tilef(l, h, ti, tile, nxt[k], prefetch)

        for ln, h in enumerate(hs):
            cur[0] = ln
            ws[ln] = setup(l, h)
        def nsteps(items):
            n, depth = 0, 0
            for it in items:
                if it[0] == "gs":
                    if depth == 0:
                        n += 1
                    depth += 1
                elif it[0] == "ge":
                    depth -= 1
                elif depth == 0:
                    n += 1
            return n

        lanes_segs = []
        for ln, h in enumerate(hs):
            cur[0] = ln
            S.rec = []
            for ti, tile in enumerate(TILES):
                if tile["kind"] == "s":
                    for _ in tilef(l, h, ti, tile, proj(tile, *ws[ln]), lambda: None):
                        pass
            segs = [S.rec]
            bounds.clear()
            S.rec = []
            for _ in head_gen(ln, h):
                pass
            full_ = S.rec
            S.rec = None
            bb_ = [0] + list(bounds) + [len(full_)]
            segs += [full_[bb_[i]:bb_[i + 1]] for i in range(len(bb_) - 1)]
            lanes_segs.append(segs)
        T = nsteps(lanes_segs[0][2])
        streams = []
        for segs in lanes_segs:
            st = list(segs[0])
            st += [("nop",)] * ((-nsteps(st)) % T)
            t0seg = list(segs[1])
            t0seg += [("nop",)] * max(0, T - nsteps(t0seg))
            st += t0seg
            for sg in segs[2:]:
                st += sg
            streams.append(st)
        off = nsteps(lanes_segs[0][0])
        off += (-off) % T
        S.replay(streams, offset=off)
        cur[0] = -1

    def full():
        cur[0] = -1
        stage_a()
        S.barrier()
        for l in range(n_layers):
            for hp in range(3):
                run_heads("a", l, [2 * hp, 2 * hp + 1])
            for hp in range(3):
                run_heads("b", l, [2 * hp, 2 * hp + 1])
            for hp in range(2):
                run_heads("c", l, [2 * hp, 2 * hp + 1])
            S.barrier()
            cur[0] = -1
            out_stage(l)
            S.barrier()

    return nc, S, locals()


def core_inputs(inp, c, consts):
    s = c // 2
    sl = slice(16 * c, 16 * c + 16)
    f = lambda a: np.ascontiguousarray(np.asarray(a, dtype=np.float32))
    m = {
        "xp": f(np.concatenate([inp["meta_tokens"], inp["x_prompt"][s]], axis=0)),
        "xs": f(np.asarray(inp["x_sample"])[sl].reshape(128, D)),
        "w_in": consts["_w_in_r"], "w_gc": consts["_w_gc"], "w_out": f(inp["w_out"]),
        "norm_w": f(inp["norm_w"]), "out_norm_w": f(inp["out_norm_w"]), "final_norm_w": f(inp["final_norm_w"]),
        "mlstm_gate_b": f(inp["mlstm_gate_b"]), "gdn_A_log": f(inp["gdn_A_log"]), "gdn_dt_bias": f(inp["gdn_dt_bias"]),
        "gdn_conv_w": f(inp["gdn_conv_w"]), "hgrn_lower_bounds": f(inp["hgrn_lower_bounds"]),
        "sC": f(np.asarray(inp["state_mlstm_C"])[:, sl]), "sn": f(np.asarray(inp["state_mlstm_n"])[:, sl]),
        "sm": f(np.asarray(inp["state_mlstm_m"])[:, sl]), "sS": f(np.asarray(inp["state_gdn_S"])[:, sl]),
        "sconv": f(np.asarray(inp["state_gdn_conv"])[:, sl]), "sH": f(np.asarray(inp["state_hgrn_S"])[:, sl]),
    }
    m.update({k: v for k, v in consts.items() if not k.startswith("_")})
    return m


def relayout_w_in(w_in):
    w = np.asarray(w_in, dtype=np.float32)
    out = np.empty((2, 70, 128, NCH * 128), np.float32)
    for i, c0 in enumerate(_blk_cols):
        blk = w[:, :, c0:c0 + 128].reshape(2, NCH, 128, 128)
        out[:, i] = blk.transpose(0, 2, 1, 3).reshape(2, 128, NCH * 128)
    g = w[:, :, _gate_cols].reshape(2, NCH, 128, 24).transpose(0, 2, 1, 3)
    return out, np.ascontiguousarray(g)


_CACHE = {}


def kernel(**inputs):
    if "nc" not in _CACHE:
        nc, S, L = build_nc(n_layers=2)
        L["full"]()
        S.finish()
        _CACHE["nc"] = nc
    nc = _CACHE["nc"]
    consts = host_consts()
    consts["_w_in_r"], consts["_w_gc"] = relayout_w_in(inputs["w_in"])
    in_maps = [core_inputs(inputs, c, consts) for c in range(8)]
    res = run_bass_kernel_spmd(nc, in_maps, core_ids=list(range(8)))
    R = res.results
    f = lambda a: np.asarray(a, dtype=np.float32)
    y_prompt = np.stack([f(R[2 * s]["yp"])[16:] for s in range(4)], axis=0)
    y_sample = np.concatenate([f(R[c]["ys"]).reshape(16, 8, D) for c in range(8)], axis=0)
    pst = lambda k: np.stack([f(R[2 * s][k]) for s in range(4)], axis=1)
    sst = lambda k: np.concatenate([f(R[c][k]) for c in range(8)], axis=1)
    return (y_prompt, y_sample,
            pst("pC"), pst("pn"), pst("pm"), pst("pS"), pst("pconv"), pst("pH"),
            sst("oC"), sst("on"), sst("om"), sst("oS"), sst("oconv"), sst("oH"))
```

```python
import contextlib
import numpy as np
import concourse.bass as bass
import concourse.mybir as mybir
from concourse.bass_utils import run_bass_kernel_spmd

F32 = mybir.dt.float32
BF16 = mybir.dt.bfloat16
I32 = mybir.dt.int32
AF = mybir.ActivationFunctionType
ALU = mybir.AluOpType
AX = mybir.AxisListType


class Tk:
    def __init__(self, name, psum=False):
        self.name = name
        self.psum = psum
        self.w = None
        self.r = []


class V:
    def __init__(self, tk, ap):
        self.tk = tk
        self.ap = ap

    def __getitem__(self, k):
        return V(self.tk, self.ap[k])

    def bc(self, shape):
        return V(self.tk, self.ap.broadcast_to(list(shape)))

    def un(self, axis):
        return V(self.tk, self.ap.unsqueeze(axis))

    def re(self, pat, **kw):
        return V(self.tk, self.ap.rearrange(pat, **kw))

    @property
    def shape(self):
        return self.ap.shape


class Sched:
    ENGS = ("pe", "act", "dve", "pool", "sp")
    ROT = 30000

    def __init__(self, nc):
        self.nc = nc
        self.es = contextlib.ExitStack()
        self.q = {e: [] for e in self.ENGS}
        self.cnt = {e: 0 for e in self.ENGS}
        self.nsem = 0
        self.sem = {e: self._newsem(e) for e in self.ENGS}
        self.waited = {e: {} for e in self.ENGS}
        self.dsem = {}
        self.n_ops = 0
        self.sb_off = 0
        self.sb_max = 0
        self.nalloc = 0
        self.rec = None
        self.offs = {"P": 16512}
        self.cur = "P"

    def split(self):
        r0 = self.offs["P"]
        self.offs["H"] = r0
        self.offs["O"] = r0

    def region(self, r):
        self.cur = r

    def _newsem(self, tag):
        self.nsem += 1
        return self.es.enter_context(self.nc.semaphore(f"s{self.nsem}_{tag}"))

    def sb(self, name, shape, dtype):
        nb = 2 if dtype == BF16 else 4
        n = 1
        for d in shape[1:]:
            n *= d
        size = (n * nb + 63) // 64 * 64
        off = self.offs[self.cur]
        self.offs[self.cur] = off + size
        self.sb_off = off + size
        self.sb_max = max(self.sb_max, self.sb_off)
        assert self.sb_off <= 229376, f"SBUF overflow at {name}: {self.sb_off} region {self.cur}"
        self.nalloc += 1
        t = self.nc.alloc_sbuf_tensor_at(f"{name}_{self.nalloc}", list(shape), dtype, offset=off)
        return V(Tk(name), t[:])

    def ps(self, name, shape, dtype=None):
        t = self.nc.alloc_psum_tensor(name, list(shape), F32)
        return V(Tk(name, psum=True), t[:])

    def I(self, eng, meth, **kw):
        reads, writes, args = [], [], {}
        for k, v in kw.items():
            if isinstance(v, V):
                if k in ("out", "accum_out") or v.tk.psum:
                    writes.append(v.tk)
                else:
                    reads.append(v.tk)
                args[k] = v.ap
            else:
                args[k] = v
        self.op(eng, lambda e: getattr(e, meth)(**args), reads, writes)

    def D(self, eng, out, in_, **kw):
        reads, writes = [], []
        names = []
        if isinstance(in_, V):
            reads.append(in_.tk)
            names.append(in_.tk.name)
            in_ = in_.ap
        if isinstance(out, V):
            writes.append(out.tk)
            names.append(out.tk.name)
            out = out.ap
        sbn = [n for n in names if not n.startswith("dram:")]
        key = sbn[0] if sbn else names[0]
        self.dma(eng, out, in_, reads, writes, key=key, **kw)

    def replay(self, lists, offset=0):
        self.rec = None
        idx = [0] * len(lists)
        step = 0
        while any(idx[k] < len(lists[k]) for k in range(len(lists))):
            for k, lst in enumerate(lists):
                if step < k * offset or idx[k] >= len(lst):
                    continue
                depth = 0
                while idx[k] < len(lst):
                    it = lst[idx[k]]
                    idx[k] += 1
                    if it[0] == "nop":
                        pass
                    elif it[0] == "gs":
                        depth += 1
                    elif it[0] == "ge":
                        depth -= 1
                    elif it[0] == "op":
                        self.op(*it[1:])
                    else:
                        self.dma(it[1], it[2], it[3], it[4], it[5], key=it[6], **it[7])
                    if depth == 0:
                        break
            step += 1

    def barrier(self):
        evs = [(self.sem[e], self.cnt[e]) for e in self.ENGS if self.cnt[e] > 0]
        evs += [(sem, val) for (sem, val) in self.dsem.values() if val > 0]
        for e in self.ENGS:
            wd = self.waited[e]
            for (sem, val) in evs:
                if sem is self.sem[e] and e == "pe":
                    continue
                if wd.get(id(sem), (None, 0))[1] >= val:
                    continue
                wd[id(sem)] = (sem, val)
                self.q[e].append(("wait", sem, val))

    def _deps(self, eng, reads, writes):
        deps = []
        for t in reads:
            if t.w is not None:
                deps.append(t.w)
        for t in writes:
            if t.w is not None:
                deps.append(t.w)
            deps.extend(t.r)
        wd = self.waited[eng]
        need = {}
        for (sem, val, e2) in deps:
            if e2 == "pe" and eng == "pe":
                continue
            k = id(sem)
            if wd.get(k, (None, 0))[1] >= val:
                continue
            if k not in need or need[k][1] < val:
                need[k] = (sem, val)
        for k, (sem, val) in need.items():
            wd[k] = (sem, val)
            self.q[eng].append(("wait", sem, val))

    def _record(self, ev, reads, writes):
        for t in reads:
            t.r.append(ev)
        for t in writes:
            t.w = ev
            t.r = []

    def op(self, eng, fn, reads, writes):
        if self.rec is not None:
            self.rec.append(("op", eng, fn, reads, writes))
            return
        self._deps(eng, reads, writes)
        if self.cnt[eng] >= self.ROT:
            self.sem[eng] = self._newsem(eng)
            self.cnt[eng] = 0
        self.cnt[eng] += 1
        ev = (self.sem[eng], self.cnt[eng], eng)
        self.q[eng].append(("op", fn, self.sem[eng], 1))
        self._record(ev, reads, writes)
        self.n_ops += 1

    def dma(self, eng, out, in_, reads, writes, key=None, **kw):
        if self.rec is not None:
            self.rec.append(("dma", eng, out, in_, reads, writes, key, kw))
            return
        self._deps(eng, reads, writes)
        if key is None:
            key = (writes[0] if writes else reads[0]).name
        if key not in self.dsem:
            self.dsem[key] = [self._newsem("d"), 0]
        ds = self.dsem[key]
        ds[1] += 16
        ev = (ds[0], ds[1], "dma")
        self.q[eng].append(("op", (lambda e, out=out, in_=in_, kw=kw: e.dma_start(out=out, in_=in_, **kw)), ds[0], 16))
        self._record(ev, reads, writes)
        self.n_ops += 1

    def finish(self):
        for key, (sem, val) in self.dsem.items():
            if val > 0:
                self.q["sp"].append(("wait", sem, val))
        for e in self.ENGS:
            if e != "sp" and self.cnt[e] > 0:
                self.q["sp"].append(("wait", self.sem[e], self.cnt[e]))
        nc = self.nc
        q = self.q

        def emit(engine, lst):
            for it in lst:
                if it[0] == "wait":
                    engine.wait_ge(it[1], it[2])
                else:
                    ins = it[1](engine)
                    ins.then_inc(it[2], it[3])

        with nc.allow_non_contiguous_dma(reason="small strided state/param transfers"), nc.Block() as block:
            @block.tensor
            def _(e):
                emit(e, q["pe"])

            @block.scalar
            def _(e):
                emit(e, q["act"])

            @block.vector
            def _(e):
                emit(e, q["dve"])

            @block.gpsimd
            def _(e):
                emit(e, q["pool"])

            @block.sync
            def _(e):
                emit(e, q["sp"])
        self.es.close()


D = 2048
NCH = 16
NP = 2064
NS = 128
NT = NP + NS
N_IN = 8984
A_Q, A_K, A_V, A_O, A_I, A_F = 0, 768, 1536, 2304, 3072, 3078
B_Q, B_K, B_V, B_A, B_B = 3084, 3852, 4620, 5388, 5394
C_Q, C_F, C_I, Z0 = 5400, 5912, 6424, 6936
EPS = 1e-6
RS = 128 ** -0.5
LANE_OFFSET = 110
_blk_cols = ([A_Q + i * 128 for i in range(24)] + [B_Q + i * 128 for i in range(18)]
             + [C_Q + i * 128 for i in range(12)] + [Z0 + i * 128 for i in range(16)])
BLK_OF = {c: i for i, c in enumerate(_blk_cols)}
_gate_cols = list(range(A_I, A_I + 12)) + list(range(B_A, B_A + 12))
GC_OF = {c: i for i, c in enumerate(_gate_cols)}
MASK_IDX = {"U128": 0, "U64": 1, "SU64": 2, "U16": 3, "U8": 4, "SU8": 5}
RESET_IDX = {128: 0, 64: 1, 16: 2, 8: 3}
RM_OFF = {64: (0, 2), 16: (2, 8), 8: (10, 16)}


def host_consts():
    idx = np.arange(128)
    masks = np.zeros((6, 128, 128), np.float32)
    for name, gs, strict in (("U128", 128, False), ("U64", 64, False), ("SU64", 64, True),
                             ("U16", 16, False), ("U8", 8, False), ("SU8", 8, True)):
        same = (idx[:, None] // gs) == (idx[None, :] // gs)
        tri = (idx[:, None] < idx[None, :]) if strict else (idx[:, None] <= idx[None, :])
        masks[MASK_IDX[name]] = (same & tri).astype(np.float32)
    resets = np.ones((4, 128, 128), np.float32)
    for gs, i in RESET_IDX.items():
        resets[i][:, (idx % gs) == 0] = 0.0
    rm = np.zeros((128, 26), np.float32)
    for gs, (o, g) in RM_OFF.items():
        for j in range(g):
            rm[(idx // gs) == j, o + j] = 1.0
    return {"c_ident": np.eye(128, dtype=np.float32), "c_masks": masks, "c_resets": resets, "c_rm": rm}


def make_tiles():
    tiles = [dict(kind="p", t0=0, c=16, gA=16, gB=16, gC=16, first=True, last=False)]
    for i in range(16):
        tiles.append(dict(kind="p", t0=16 + 128 * i, c=128, gA=128, gB=64, gC=16, first=False, last=(i == 15)))
    tiles.append(dict(kind="s", t0=NP, c=128, gA=8, gB=8, gC=8, first=True, last=True))
    return tiles


def build_nc(n_layers=2, heads=None, do_out=True, debug=False):
    nc = bass.Bass("TRN2", target_bir_lowering=False)
    S = Sched(nc)

    def din(name, shape):
        return nc.dram_tensor(name, list(shape), F32, kind="ExternalInput").ap()

    def dout(name, shape):
        return nc.dram_tensor(name, list(shape), F32, kind="ExternalOutput").ap()

    xp = din("xp", [NP, D]); xs = din("xs", [NS, D])
    w_in = din("w_in", [2, 70, 128, NCH * 128]); w_gc = din("w_gc", [2, 128, NCH, 24]); w_out = din("w_out", [2, D, D])
    norm_w = din("norm_w", [2, D]); out_norm_w = din("out_norm_w", [2, D]); final_norm_w = din("final_norm_w", [D])
    gate_b = din("mlstm_gate_b", [2, 2, 6]); A_log = din("gdn_A_log", [2, 6]); dt_bias = din("gdn_dt_bias", [2, 6])
    conv_w = din("gdn_conv_w", [2, 4, 2304]); lbp = din("hgrn_lower_bounds", [2, 512])
    sC = din("sC", [2, 16, 6, 128, 128]); sn = din("sn", [2, 16, 6, 128]); sm = din("sm", [2, 16, 6])
    sS = din("sS", [2, 16, 6, 128, 128]); sconv = din("sconv", [2, 16, 3, 2304]); sH = din("sH", [2, 16, 4, 128, 128])
    c_ident = din("c_ident", [128, 128]); c_masks = din("c_masks", [6, 128, 128])
    c_resets = din("c_resets", [4, 128, 128]); c_rm = din("c_rm", [128, 26])

    yp = dout("yp", [NP, D]); ys = dout("ys", [NS, D])
    pC = dout("pC", [2, 6, 128, 128]); pn = dout("pn", [2, 6, 128]); pm = dout("pm", [2, 6])
    pS = dout("pS", [2, 6, 128, 128]); pconv = dout("pconv", [2, 3, 2304]); pH = dout("pH", [2, 4, 128, 128])
    oC = dout("oC", [2, 16, 6, 128, 128]); on = dout("on", [2, 16, 6, 128]); om = dout("om", [2, 16, 6])
    oS = dout("oS", [2, 16, 6, 128, 128]); oconv = dout("oconv", [2, 16, 3, 2304]); oH = dout("oH", [2, 16, 4, 128, 128])

    skind = "ExternalOutput" if debug else "Internal"
    xres_t = nc.dram_tensor("xres", [NCH, 128, NT], F32, kind=skind).ap()
    yT_t = nc.dram_tensor("yTs", [16, 128, NT], BF16, kind=skind).ap()
    xres = V(Tk("dram:xres"), xres_t)
    yTd = V(Tk("dram:yT"), yT_t)

    TILES = make_tiles()

    def ACT(out, in_, func, scale=1.0, bias=None):
        kw = dict(out=out, in_=in_, func=func, scale=scale)
        if bias is not None:
            kw["bias"] = bias
        S.I("act", "activation", **kw)

    def TT(out, in0, in1, op, eng="dve"):
        S.I(eng, "tensor_tensor", out=out, in0=in0, in1=in1, op=op)

    def TS(out, in0, s1, op0, s2=None, op1=None, eng="dve"):
        if op1 is None:
            S.I(eng, "tensor_scalar", out=out, in0=in0, scalar1=s1, scalar2=None, op0=op0)
        else:
            S.I(eng, "tensor_scalar", out=out, in0=in0, scalar1=s1, scalar2=s2, op0=op0, op1=op1)

    def STT(out, in0, scalar, in1, op0, op1):
        S.I("dve", "scalar_tensor_tensor", out=out, in0=in0, scalar=scalar, in1=in1, op0=op0, op1=op1)

    def MM(out, lhsT, rhs, start=True, stop=True):
        S.I("pe", "matmul", out=out, lhsT=lhsT, rhs=rhs, start=start, stop=stop)

    def CP(out, in_, eng="act"):
        if eng == "act":
            S.I("act", "copy", out=out, in_=in_)
        else:
            S.I(eng, "tensor_copy", out=out, in_=in_)

    def RECIP(out, in_):
        ACT(out, in_, AF.Ln)
        ACT(out, out, AF.Exp, scale=-1.0)

    def RECIP1P(out, in_):
        ACT(out, in_, AF.Ln, bias=c_one[0:out.shape[0], :])
        ACT(out, out, AF.Exp, scale=-1.0)

    rings = {}
    cur = [-1]
    LANES = []

    slots = {}

    def tmp(tag, shape, dtype, n=2):
        lane_tag = tag[:2] in ("a_", "b_", "c_") or tag[:3] == "yf_"
        if lane_tag:
            n = 1
        if tag[:2] in ("a_", "b_", "c_"):
            k = (tag[:2], tuple(shape), str(dtype), n)
            d = slots.setdefault(k, {})
            if tag not in d:
                d[tag] = len(d)
            tag = f"mix_{tuple(shape)}_{dtype}_{n}_{d[tag]}"
        if lane_tag:
            tag = f"{tag}@{max(cur[0], 0)}"
        if tag not in rings:
            rings[tag] = [[S.sb(f"{tag}{i}", shape, dtype) for i in range(n)], 0]
        r = rings[tag]
        v = r[0][r[1] % n]
        r[1] += 1
        return v

    ident = S.sb("ident", [128, 128], F32)
    masks = S.sb("masks", [128, 6, 128], F32)
    resets = S.sb("resets", [128, 4, 128], F32)
    rmk = S.sb("rmk", [128, 26], F32)
    ones_f = S.sb("ones_f", [128, 128], F32)
    ones_b = S.sb("ones_b", [128, 128], BF16)
    c_one = S.sb("c_one", [128, 1], F32)
    c_eps = S.sb("c_eps", [128, 1], F32)
    S.D("sp", ident, c_ident)
    S.D("sp", masks, c_masks.rearrange("m p n -> p m n"))
    S.D("sp", resets, c_resets.rearrange("m p n -> p m n"))
    S.D("sp", rmk, c_rm)
    S.I("dve", "memset", ap=ones_f, constant=1.0) if False else S.op("dve", lambda e: e.memset(ones_f.ap, 1.0), [], [ones_f.tk])
    S.op("dve", lambda e: e.memset(ones_b.ap, 1.0), [], [ones_b.tk])
    S.op("dve", lambda e: e.memset(c_one.ap, 1.0), [], [c_one.tk])
    S.op("dve", lambda e: e.memset(c_eps.ap, EPS), [], [c_eps.tk])

    def mask(name, c):
        return masks[:c, MASK_IDX[name], :c]

    def TR(out, in_):
        k = in_.shape[0]
        S.I("pe", "transpose", out=out, in_=in_, identity=ident[:k, :k])

    pb = [S.ps(f"pb{i}", [128, 512]) for i in range(8)]
    pj_sets = [(pb[0], pb[1]), (pb[2], pb[3])]
    pring = [0]

    def pbank():
        if cur[0] < 0:
            b = pb[4 + pring[0] % 4]
            pring[0] += 1
            return b
        L = LANES[cur[0]]
        b = L["ring"][L["rp"] % 2]
        L["rp"] += 1
        return b

    gb_row = S.sb("gb_row", [1, 24], F32)
    ngb_row = S.sb("ngb_row", [1, 24], F32)
    al_row = S.sb("al_row", [1, 12], F32)
    dtb_row = S.sb("dtb_row", [1, 12], F32)
    S.D("sp", gb_row, gate_b.rearrange("l w h -> (l w h)").unsqueeze(0))
    S.D("sp", al_row, A_log.rearrange("l h -> (l h)").unsqueeze(0))
    S.D("sp", dtb_row, dt_bias.rearrange("l h -> (l h)").unsqueeze(0))
    TS(ngb_row, gb_row, -1.0, ALU.mult)
    ACT(al_row, al_row, AF.Exp)
    lbraw = S.sb("lbraw", [128, 2, 4], F32)
    S.D("sp", lbraw, lbp.rearrange("l (h p) -> p l h", p=128))
    lb = S.sb("lb", [128, 2, 4], F32)
    oml = S.sb("oml", [128, 2, 4], F32)
    S.op("dve", lambda e: e.memset(lb.ap, 0.0), [], [lb.tk])
    TT(lb[:, 1, :], lbraw[:, 0, :], lbraw[:, 1, :], ALU.subtract)
    ACT(lb[:, 1, :], lb[:, 1, :], AF.Exp)
    RECIP1P(lb[:, 1, :], lb[:, 1, :])
    TS(oml, lb, -1.0, ALU.mult, 1.0, ALU.add)
    nw = S.sb("nw", [128, 2, NCH], F32)
    onw = S.sb("onw", [128, 2, NCH], F32)
    fnw = S.sb("fnw", [128, NCH], F32)
    S.D("sp", nw, norm_w.rearrange("l (j p) -> p l j", p=128))
    S.D("sp", onw, out_norm_w.rearrange("l (j p) -> p l j", p=128))
    S.D("sp", fnw, final_norm_w.rearrange("(j p) -> p j", p=128))
    cw = S.sb("cw", [128, 2, 18, 4], F32)
    for l_ in range(2):
        for j_ in range(4):
            S.D("sp", cw[:, l_, :, j_], conv_w[l_, j_, :].rearrange("(b p) -> p b", p=128))

    xnT = S.sb("xnT", [128, NCH, NT], BF16)

    def finish_x(ti, tile, xT, layer_next):
        c, t0 = tile["c"], tile["t0"]
        if layer_next < n_layers:
            S.D("sp", xrv[ti].re("j p t -> p j t"), xT[:, :, :c])
        sq = tmp("fx_sq", [128, NCH, 128], BF16, 1)
        S.I("act", "activation", out=sq[:, :, :c], in_=xT[:, :, :c], func=AF.Square)
        ps = pbank()
        for j in range(NCH):
            MM(ps[:, :c], ones_b, sq[:, j, :c], start=(j == 0), stop=(j == NCH - 1))
        rstd = tmp("fx_rstd", [128, 128], F32, 2)
        ACT(rstd[:, :c], ps[:, :c], AF.Ln, scale=1.0 / D, bias=c_eps)
        ACT(rstd[:, :c], rstd[:, :c], AF.Exp, scale=-0.5)
        t1 = tmp("fx_t1", [128, NCH, 128], F32, 1)
        TT(t1[:, :, :c], xT[:, :, :c], rstd[:, :c].un(1).bc([128, NCH, c]), ALU.mult)
        if layer_next < n_layers:
            TT(xnT[:, :, t0:t0 + c], t1[:, :, :c], nw[:, layer_next, :].un(2).bc([128, NCH, c]), ALU.mult)
        else:
            TT(t1[:, :, :c], t1[:, :, :c], fnw.un(2).bc([128, NCH, c]), ALU.mult)
            ot = tmp("sa_x", [128, D], F32, 1)
            for q4 in range(4):
                pt = pbank()
                for jj in range(4):
                    j = q4 * 4 + jj
                    TR(pt[:c, jj * 128:(jj + 1) * 128], t1[:, j, :c])
                CP(ot[:c, q4 * 512:(q4 + 1) * 512], pt[:c, :])
            dst = yp[t0:t0 + c, :] if tile["kind"] == "p" else ys[:, :]
            S.D("sp", dst, ot[:c, :])

    def stage_a():
        S.region("O")
        for ti, tile in enumerate(TILES):
            c, t0 = tile["c"], tile["t0"]
            xt = tmp("sa_x", [128, D], F32, 1)
            src = xp[t0:t0 + c, :] if tile["kind"] == "p" else xs[:, :]
            S.D("sp", xt[:c, :], src)
            xT = tmp("xT", [128, NCH, 128], F32, 2)
            for q4 in range(4):
                pt = pbank()
                for jj in range(4):
                    j = q4 * 4 + jj
                    TR(pt[:, jj * 128:jj * 128 + c], xt[:c, j * 128:(j + 1) * 128])
                CP(xT[:, q4 * 4:(q4 + 1) * 4, :c], pt.re("p (a b) -> p a b", a=4)[:, :, :c])
            finish_x(ti, tile, xT, 0)


    S.split()
    S.region("H")
    sA = [S.sb("sA0", [128, 16, 129], F32)]
    sSb = S.sb("sSb", [128, 16, 128], BF16)
    sB = [sA[0][:, :, 0:128]]
    sBb = sSb
    sCc = [sA[0][:, :, 0:128]]
    sCb = sSb
    mrow_s = S.sb("mrow_s", [1, 16], F32)
    ue_s = [S.sb(f"ue_s{i}", [128, 16, 11], F32) for i in range(3)]
    Cb = S.sb("Cb", [128, 16, 128], BF16)
    nbb = S.sb("nbb", [128, 16, 128], BF16)
    for i_ in range(2):
        bk = pb[4 * i_:4 * i_ + 4]
        LANES.append(dict(
            i=i_, PJ=bk[0], X=bk[1], ring=[bk[2], bk[3]], rp=0,
            wring=[S.sb(f"wblk{i_}_{k}", [128, NCH, 128], BF16) for k in range(5)], wp=0,
            wg=S.sb(f"wg{i_}", [128, NCH, 2], BF16),
            pA=S.sb(f"pA{i_}", [128, 1, 129], F32), mrow_p=S.sb(f"mrow_p{i_}", [1, 16], F32),
            pBs=S.sb(f"pBs{i_}", [128, 1, 128], F32), pBb=S.sb(f"pBb{i_}", [128, 1, 128], BF16),
            pCs=S.sb(f"pCs{i_}", [128, 1, 128], F32), pCb=S.sb(f"pCb{i_}", [128, 1, 128], BF16),
            ue_p=[S.sb(f"ue_p{i_}_{k}", [128, 1, 131], F32) for k in range(3)],
            Cbp=S.sb(f"Cbp{i_}", [128, 1, 128], BF16), nbp=S.sb(f"nbp{i_}", [128, 1, 128], BF16),
        ))

    def wblock(l, col0):
        L = LANES[cur[0]]
        wb = L["wring"][L["wp"] % 5]
        L["wp"] += 1
        S.D("pool", wb, w_in[l, BLK_OF[col0], :, :].rearrange("p (j n) -> p j n", j=NCH))
        return wb

    def wgate(l, col_a, col_b):
        wg = LANES[cur[0]]["wg"]
        S.D("pool", wg[:, :, 0:1], w_gc[l, :, :, GC_OF[col_a]:GC_OF[col_a] + 1])
        S.D("pool", wg[:, :, 1:2], w_gc[l, :, :, GC_OF[col_b]:GC_OF[col_b] + 1])
        return wg

    def proj(tile, wbs, wg):
        if S.rec is not None:
            S.rec.append(("gs",))
        r = proj_(tile, wbs, wg)
        if S.rec is not None:
            S.rec.append(("ge",))
        return r

    def proj_(tile, wbs, wg):
        c, t0 = tile["c"], tile["t0"]
        L = LANES[cur[0]]
        outs = []
        for b, wb in enumerate(wbs):
            bank = L["PJ"] if b < 4 else L["X"]
            o = bank[:, (b % 4) * 128:(b % 4) * 128 + c]
            for j in range(NCH):
                MM(o, wb[:, j, :], xnT[:, j, t0:t0 + c], start=(j == 0), stop=(j == NCH - 1))
            outs.append(o)
        gr = []
        if wg is not None:
            for i in range(2):
                o = L["X"][0:1, 256 + i * 128:256 + i * 128 + c]
                for j in range(NCH):
                    MM(o, wg[:, j, i:i + 1], xnT[:, j, t0:t0 + c], start=(j == 0), stop=(j == NCH - 1))
                gr.append(o)
        return outs, gr

    def par(*fns):
        outer = S.rec
        lists = []
        for f in fns:
            S.rec = []
            f()
            lists.append(S.rec)
        S.rec = outer
        idx = [0] * len(lists)
        while any(idx[k] < len(lists[k]) for k in range(len(lists))):
            for k, lst in enumerate(lists):
                if idx[k] < len(lst):
                    it = lst[idx[k]]
                    idx[k] += 1
                    if outer is not None:
                        outer.append(it)
                    elif it[0] == "op":
                        S.op(*it[1:])
                    else:
                        S.dma(it[1], it[2], it[3], it[4], it[5], key=it[6], **it[7])

    def scan(out, msk, data):
        S.I("dve", "tensor_tensor_scan", out=out, data0=msk, data1=data, initial=0.0, op0=ALU.mult, op1=ALU.add)

    def T2(tag, dtype=F32, n=2):
        return tmp(tag, [128, 128], dtype, n)

    def R1(tag, w=128, n=2):
        return tmp(tag, [1, w], F32, n)

    def g3(v, c, G):
        return v[:, :c].re("p (a b) -> p a b", a=G)

    def y_finalize(l, hg, ti, tile, hT, z_ps):
        c, t0 = tile["c"], tile["t0"]
        zs = T2("yf_zs")
        y32 = T2("yf_y32")

        def chain_z():
            ez = T2("yf_ez")
            ACT(ez[:, :c], z_ps, AF.Exp, scale=-1.0)
            RECIP1P(ez[:, :c], ez[:, :c])
            STT(zs[:, :c], z_ps, onw[:, l, hg:hg + 1], ez[:, :c], ALU.mult, ALU.mult)

        def chain_h():
            sq = T2("yf_sq", BF16)
            ACT(sq[:, :c], hT[:, :c], AF.Square)
            ps = pbank()
            MM(ps[:, :c], ones_b, sq[:, :c])
            rstd = T2("yf_rstd")
            ACT(rstd[:, :c], ps[:, :c], AF.Ln, scale=1.0 / 128, bias=c_eps)
            ACT(rstd[:, :c], rstd[:, :c], AF.Exp, scale=-0.5)
            TT(y32[:, :c], hT[:, :c], rstd[:, :c], ALU.mult)

        par(chain_h, chain_z)
        yb = T2("yf_yb", BF16)
        TT(yb[:, :c], y32[:, :c], zs[:, :c], ALU.mult)
        S.D("sp", yTv[ti][hg, :, :], yb[:, :c])

    yTv = [V(Tk(f"dram:yT{ti}"), yT_t[:, :, t["t0"]:t["t0"] + t["c"]]) for ti, t in enumerate(TILES)]
    xrv = [V(Tk(f"dram:xr{ti}"), xres_t[:, :, t["t0"]:t["t0"] + t["c"]]) for ti, t in enumerate(TILES)]

    def mlstm_setup(l, h):
        LN = LANES[cur[0]]
        LN["wp"] = 0
        pA = LN["pA"]
        mrow_p = LN["mrow_p"]
        wbs = [wblock(l, A_Q + h * 128), wblock(l, A_K + h * 128), wblock(l, A_V + h * 128),
               wblock(l, A_O + h * 128), wblock(l, Z0 + h * 128)]
        wg = wgate(l, A_I + h, A_F + h)
        S.op("dve", lambda e: e.memset(pA.ap, 0.0), [], [pA.tk])
        S.op("dve", lambda e: e.memset(mrow_p.ap, 0.0), [], [mrow_p.tk])
        return wbs, wg

    def mlstm_tile(l, h, ti, tile, pj, prefetch):
        LN = LANES[cur[0]]
        pA = LN["pA"]
        mrow_p = LN["mrow_p"]
        c, gs, t0 = tile["c"], tile["gA"], tile["t0"]
        G = c // gs
        smp = tile["kind"] == "s"
        if smp:
            St = sA[0]
            S.D("sp", St[:, :, 0:128], sC[l, :, h, :, :].rearrange("g d e -> d g e"))
            S.D("sp", St[:, :, 128:129], sn[l, :, h, :].rearrange("g d -> d g").unsqueeze(2))
            mrow = mrow_s
            S.D("sp", mrow, sm[l, :, h].unsqueeze(0))
        else:
            St, mrow = pA, mrow_p
        (q_ps, k_ps, v_ps, o_ps, z_ps), (gi_ps, gf_ps) = pj
        qT = T2("a_qT", BF16); CP(qT[:, :c], q_ps)
        kTf = T2("a_kTf"); S.I("act", "mul", out=kTf[:, :c], in_=k_ps, mul=RS)
        kT = T2("a_kT", BF16); CP(kT[:, :c], kTf[:, :c], eng="dve")
        vTf = T2("a_vTf"); CP(vTf[:, :c], v_ps)
        eo = T2("a_eo"); ACT(eo[:, :c], o_ps, AF.Exp, scale=-1.0)
        zf = T2("a_zf"); CP(zf[:, :c], z_ps)
        li = R1("a_li"); TS(li[:, :c], gi_ps, gb_row[0:1, l * 12 + h:l * 12 + h + 1], ALU.add)
        e1 = R1("a_e1"); ACT(e1[:, :c], gf_ps, AF.Exp, scale=-1.0, bias=ngb_row[0:1, l * 12 + 6 + h:l * 12 + 7 + h])
        prefetch()
        yield
        sp_ = R1("a_sp"); ACT(sp_[:, :c], e1[:, :c], AF.Ln, bias=c_one[0:1, :])
        Fn = R1("a_Fn"); scan(Fn[:, :c], resets[0:1, RESET_IDX[gs], :c], sp_[:, :c])
        gg = R1("a_g"); TT(gg[:, :c], li[:, :c], Fn[:, :c], ALU.add)
        gmax = R1("a_gmax", 16)
        S.I("dve", "tensor_reduce", out=gmax[:, :G], in_=g3(gg, c, G), axis=AX.X, op=ALU.max)
        Mb = R1("a_Mb", 16); TT(Mb[:, :G], gmax[:, :G], mrow[:, :G], ALU.max)
        al = R1("a_al", 16); TT(al[:, :G], mrow[:, :G], Mb[:, :G], ALU.subtract)
        ACT(al[:, :G], al[:, :G], AF.Exp)
        TT(mrow[:, :G], Mb[:, :G], g3(Fn, c, G)[:, :, gs - 1], ALU.subtract)
        w = R1("a_w"); TT(g3(w, c, G), g3(gg, c, G), Mb[:, :G].un(2).bc([1, G, gs]), ALU.subtract)
        ACT(w[:, :c], w[:, :c], AF.Exp)
        thr = R1("a_thr"); TT(g3(thr, c, G), g3(Fn, c, G), Mb[:, :G].un(2).bc([1, G, gs]), ALU.subtract)
        yield
        pa = pbank()
        MM(pa[:, 0:c], ones_f[0:1, :], thr[:, :c])
        MM(pa[:, 128:128 + G], ones_f[0:1, :], al[:, :G])
        MM(pa[:c, 256:257], w[:, :c], ones_f[0:1, 0:1])
        thrS = T2("a_thrS"); ACT(thrS[:, :c], pa[:, 0:c], AF.Exp)
        ab = tmp("a_ab", [128, 16], F32); CP(ab[:, :G], pa[:, 128:128 + G])
        wcol = tmp("a_wcol", [128, 1], F32); CP(wcol[:c, :], pa[:c, 256:257])
        yield
        pbt = pbank()
        TR(pbt[:c, 0:128], vTf[:, :c])
        TR(pbt[:c, 128:256], kTf[:, :c])
        vp = tmp("a_vp", [128, 129], BF16)
        TS(vp[:c, 0:128], pbt[:c, 0:128], wcol[:c, 0:1], ALU.mult)
        CP(vp[:c, 128:129], wcol[:c, :], eng="dve")
        kt = T2("a_kt", BF16); CP(kt[:c, :], pbt[:c, 128:256])
        wbc = T2("a_wbc", BF16); TS(wbc[:c, :], ones_f[:c, :], wcol[:c, 0:1], ALU.mult)
        yield
        pc = pbank(); MM(pc[:c, :c], kT[:, :c], qT[:, :c])
        PT = T2("a_PT", BF16); TT(PT[:c, :c], pc[:c, :c], mask("U8" if smp else "U128", c), ALU.mult)
        yield
        Cb_, nb_ = (Cb, nbb) if smp else (LN["Cbp"], LN["nbp"])
        for g in range(G):
            ACT(Cb_[:, g, :], St[:, g, 0:128], AF.Copy, scale=ab[:, g:g + 1])
            ACT(nb_[:, g, :], St[:, g, 128:129].bc([128, 128]), AF.Copy, scale=ab[:, g:g + 1])
        pd_ = pbank()
        MM(pd_[:, :c], wbc[:c, :], PT[:c, :c], start=True, stop=False)
        for g in range(G):
            MM(pd_[:, g * gs:(g + 1) * gs], nb_[:, g, :], qT[:, g * gs:(g + 1) * gs], start=False, stop=(g == G - 1))
        denS = T2("a_denS"); CP(denS[:, :c], pd_[:, :c])
        dmax = T2("a_dmax"); STT(dmax[:, :c], denS[:, :c], -1.0, denS[:, :c], ALU.mult, ALU.max)
        TT(dmax[:, :c], dmax[:, :c], thrS[:, :c], ALU.max)
        yield
        RECIP(dmax[:, :c], dmax[:, :c])
        pn_ = pbank()
        MM(pn_[:, :c], vp[:c, 0:128], PT[:c, :c], start=True, stop=False)
        for g in range(G):
            MM(pn_[:, g * gs:(g + 1) * gs], Cb_[:, g, :], qT[:, g * gs:(g + 1) * gs], start=False, stop=(g == G - 1))
        hT = T2("a_hT"); TT(hT[:, :c], pn_[:, :c], dmax[:, :c], ALU.mult)
        RECIP1P(eo[:, :c], eo[:, :c])
        TT(hT[:, :c], hT[:, :c], eo[:, :c], ALU.mult)
        y_finalize(l, h, ti, tile, hT, zf[:, :c])
        yield
        if G > 1:
            vpm = tmp("a_vpm", [128, 16, 129], BF16, 1)
            o_, g_ = RM_OFF[gs]
            TT(vpm[:c, :G, :], vp[:c, :].un(1).bc([c, G, 129]), rmk[:c, o_:o_ + G].un(2).bc([c, G, 129]), ALU.mult)
        for g in range(G):
            pu = pbank()
            MM(pu[:, 0:129], kt[:c, :], vpm[:c, g, :] if G > 1 else vp[:c, :])
            STT(St[:, g, :], St[:, g, :], ab[:, g:g + 1], pu[:, 0:129], ALU.mult, ALU.add)
            yield
        if smp:
            S.D("sp", oC[l, :, h, :, :].rearrange("g d e -> d g e"), St[:, :, 0:128])
            S.D("sp", on[l, :, h, :].rearrange("g d -> d g").unsqueeze(2), St[:, :, 128:129])
            S.D("sp", om[l, :, h].unsqueeze(0), mrow)
        elif tile["last"]:
            S.D("sp", pC[l, h, :, :], St[:, 0, 0:128])
            S.D("sp", pn[l, h, :].unsqueeze(1), St[:, 0, 128:129])
            S.D("sp", pm[l, h:h + 1].unsqueeze(0), mrow[:, 0:1])
        yield

    import math

    def gdn_setup(l, h):
        LN = LANES[cur[0]]
        LN["wp"] = 0
        pBs = LN["pBs"]
        pBb = LN["pBb"]
        ue_p = LN["ue_p"]
        wbs = [wblock(l, B_Q + h * 128), wblock(l, B_K + h * 128), wblock(l, B_V + h * 128), wblock(l, Z0 + (6 + h) * 128)]
        wg = wgate(l, B_A + h, B_B + h)
        S.op("dve", lambda e: e.memset(pBs.ap, 0.0), [], [pBs.tk])
        S.op("dve", lambda e: e.memset(pBb.ap, 0.0), [], [pBb.tk])
        for i in range(3):
            S.op("dve", lambda e, i=i: e.memset(ue_p[i].ap, 0.0), [], [ue_p[i].tk])
        return wbs, wg

    def gdn_tile(l, h, ti, tile, pj, prefetch):
        LN = LANES[cur[0]]
        pBs = LN["pBs"]
        pBb = LN["pBb"]
        ue_p = LN["ue_p"]
        c, gs, t0 = tile["c"], tile["gB"], tile["t0"]
        G = c // gs
        smp = tile["kind"] == "s"
        nseq, L = (16, 8) if smp else (1, c)
        if smp:
            Sf, Sb_ = sB[0], sBb
            S.D("sp", Sf, sS[l, :, h, :, :].rearrange("g d e -> d g e"))
            CP(Sb_, Sf)
            for i in range(3):
                ch0 = i * 768 + h * 128
                for j_ in range(3):
                    S.D("sp", ue_s[i][:, :, j_], sconv[l, :, j_, ch0:ch0 + 128].rearrange("g p -> p g"))
        else:
            Sf, Sb_ = pBs, pBb
        (q_ps, k_ps, v_ps, z_ps), (ga_ps, gb_ps) = pj
        for i, p_ in enumerate((q_ps, k_ps, v_ps)):
            ue = ue_s[i] if smp else ue_p[i]
            CP(ue[:, :, 3:3 + L], p_.re("p (a b) -> p a b", a=nseq))
        zf = T2("b_zf"); CP(zf[:, :c], z_ps)
        xa = R1("b_xa"); TS(xa[:, :c], ga_ps, dtb_row[0:1, l * 6 + h:l * 6 + h + 1], ALU.add)
        beta = R1("b_beta"); ACT(beta[:, :c], gb_ps, AF.Exp, scale=-1.0)
        prefetch()
        yield
        cs = [None, None, None]

        def conv_chain(i):
            def f():
                ue = ue_s[i] if smp else ue_p[i]
                bidx = i * 6 + h
                cv = T2(f"b_cv{i}")
                cvv = cv[:, :c].re("p (a b) -> p a b", a=nseq)
                TS(cvv, ue[:, :, 0:L], cw[:, l, bidx, 0:1], ALU.mult)
                for j in range(1, 4):
                    STT(cvv, ue[:, :, j:j + L], cw[:, l, bidx, j:j + 1], cvv, ALU.mult, ALU.add)
                ex = T2(f"b_ex{i}")
                ACT(ex[:, :c], cv[:, :c], AF.Exp, scale=-1.0)
                RECIP1P(ex[:, :c], ex[:, :c])
                c_ = T2(f"b_cs{i}")
                TT(c_[:, :c], cv[:, :c], ex[:, :c], ALU.mult)
                cs[i] = c_
                if smp:
                    ch0 = i * 768 + h * 128
                    for j_ in range(3):
                        S.D("sp", oconv[l, :, j_, ch0:ch0 + 128].rearrange("g p -> p g"), ue[:, :, 8 + j_])
                else:
                    CP(ue[:, :, 0:3], ue[:, :, L:L + 3])
                    if tile["last"]:
                        ch0 = i * 768 + h * 128
                        S.D("sp", pconv[l, :, ch0:ch0 + 128].rearrange("j p -> p j"), ue[:, 0, 0:3])
            return f

        GR = {}

        def gate_chain():
            ACT(xa[:, :c], xa[:, :c], AF.Exp)
            ACT(xa[:, :c], xa[:, :c], AF.Ln, bias=c_one[0:1, :])
            gn = R1("b_gn"); TS(gn[:, :c], xa[:, :c], al_row[0:1, l * 6 + h:l * 6 + h + 1], ALU.mult)
            Gn = R1("b_Gn"); scan(Gn[:, :c], resets[0:1, RESET_IDX[gs], :c], gn[:, :c])
            RECIP1P(beta[:, :c], beta[:, :c])
            eG = R1("b_eG"); ACT(eG[:, :c], Gn[:, :c], AF.Exp, scale=-1.0)
            eGe = R1("b_eGe", 16); ACT(eGe[:, :G], g3(Gn, c, G)[:, :, gs - 1], AF.Exp, scale=-1.0)
            df = R1("b_df"); TT(g3(df, c, G), g3(Gn, c, G), g3(Gn, c, G)[:, :, gs - 1:gs].bc([1, G, gs]), ALU.subtract)
            ACT(df[:, :c], df[:, :c], AF.Exp)
            bg = R1("b_bg"); TT(bg[:, :c], beta[:, :c], eG[:, :c], ALU.mult)
            Gr = R1("b_Gr"); TS(Gr[:, :c], Gn[:, :c], -1.0, ALU.mult)
            GR.update(Gn=Gn, eG=eG, eGe=eGe, df=df, bg=bg, Gr=Gr)

        par(conv_chain(0), conv_chain(1), conv_chain(2), gate_chain)
        Gn, eG, eGe, df, bg, Gr = GR["Gn"], GR["eG"], GR["eGe"], GR["df"], GR["bg"], GR["Gr"]
        yield
        nf = []
        for i in range(2):
            sq = T2("b_sq", BF16); ACT(sq[:, :c], cs[i][:, :c], AF.Square)
            ps = pbank(); MM(ps[:, :c], ones_b, sq[:, :c])
            rn = T2("b_rn"); ACT(rn[:, :c], ps[:, :c], AF.Ln, bias=c_eps)
            ACT(rn[:, :c], rn[:, :c], AF.Exp, scale=-0.5)
            f_ = T2(f"b_nf{i}")
            if i == 0:
                STT(f_[:, :c], cs[0][:, :c], RS, rn[:, :c], ALU.mult, ALU.mult)
            else:
                TT(f_[:, :c], cs[1][:, :c], rn[:, :c], ALU.mult)
            nf.append(f_)
        qf, kf = nf
        yield
        pa = pbank()
        MM(pa[:, 0:c], ones_f[0:1, :], beta[:, :c])
        MM(pa[:, 128:128 + c], ones_f[0:1, :], eG[:, :c])
        MM(pa[:, 256:256 + G], ones_f[0:1, :], eGe[:, :G])
        MM(pa[:c, 384:385], beta[:, :c], ones_f[0:1, 0:1])
        MM(pa[:c, 385:386], bg[:, :c], ones_f[0:1, 0:1])
        MM(pa[:c, 386:387], df[:, :c], ones_f[0:1, 0:1])
        bb = tmp("b_bb", [128, 512], F32)
        CP(bb[:, 0:128 + c], pa[:, 0:128 + c])
        CP(bb[:, 256:256 + G], pa[:, 256:256 + G], eng="dve")
        CP(bb[:c, 384:387], pa[:c, 384:387], eng="dve")
        beta_bc, eG_bc, eGe_bc = bb[:, 0:c], bb[:, 128:128 + c], bb[:, 256:256 + G]
        yield
        pd_ = pbank()
        MM(pd_[:c, :c], Gn[:, :c], ones_f[0:1, :c], start=True, stop=False)
        MM(pd_[:c, :c], ones_f[0:1, :c], Gr[:, :c], start=False, stop=True)
        dT = T2("b_dT"); TS(dT[:c, :c], pd_[:c, :c], 0.0, ALU.min)
        ACT(dT[:c, :c], dT[:c, :c], AF.Exp)
        mU, mSU = ("U8", "SU8") if smp else ("U64", "SU64")
        dTS = T2("b_dTS"); TT(dTS[:c, :c], dT[:c, :c], mask(mSU, c), ALU.mult)
        dTI = T2("b_dTI"); TT(dTI[:c, :c], dT[:c, :c], mask(mU, c), ALU.mult)
        yield
        khT = T2("b_khT", BF16); CP(khT[:, :c], kf[:, :c], eng="dve")
        kbT = T2("b_kbT", BF16); TT(kbT[:, :c], kf[:, :c], beta_bc, ALU.mult)
        qhT = T2("b_qhT", BF16); CP(qhT[:, :c], qf[:, :c], eng="dve")
        qgT = T2("b_qgT", BF16); TT(qgT[:, :c], qf[:, :c], eG_bc, ALU.mult)
        pe_ = pbank()
        MM(pe_[:c, 0:c], khT[:, :c], kbT[:, :c])
        MM(pe_[:c, 128:128 + c], khT[:, :c], qhT[:, :c])
        Bm = T2("b_B"); TT(Bm[:c, :c], pe_[:c, 0:c], dTS[:c, :c], ALU.mult)
        Aqk = T2("b_Aqk", BF16); TT(Aqk[:c, :c], pe_[:c, 128:128 + c], dTI[:c, :c], ALU.mult)
        X = T2("b_X"); TT(X[:c, :c], ident[:c, :c], Bm[:c, :c], ALU.subtract)
        pf = pbank(); TR(pf[:c, :c], Bm[:c, :c])
        Am = T2("b_A"); CP(Am[:c, :c], pf[:c, :c])
        yield
        nsq = int(math.log2(gs)) - 1
        for lev in range(nsq):
            pg = pbank()
            MM(pg[:c, 0:c], Bm[:c, :c], Am[:c, :c])
            if lev < nsq - 1:
                MM(pg[:c, 128:128 + c], Am[:c, :c], Bm[:c, :c])
            A2 = T2("b_A"); CP(A2[:c, :c], pg[:c, 0:c])
            if lev < nsq - 1:
                B2 = T2("b_B"); CP(B2[:c, :c], pg[:c, 128:128 + c], eng="dve")
            ph = pbank(); MM(ph[:c, :c], A2[:c, :c], X[:c, :c])
            Xn = T2("b_X"); TT(Xn[:c, :c], X[:c, :c], ph[:c, :c], ALU.add)
            yield
            X, Am = Xn, A2
            if lev < nsq - 1:
                Bm = B2
        pt = pbank()
        TR(pt[:c, 0:128], kf[:, :c])
        TR(pt[:c, 128:256], cs[2][:, :c])
        kbg = T2("b_kbg", BF16); TS(kbg[:c, :], pt[:c, 0:128], bb[:c, 385:386], ALU.mult)
        kd = T2("b_kd", BF16); TS(kd[:c, :], pt[:c, 0:128], bb[:c, 386:387], ALU.mult)
        bv = T2("b_bv", BF16); TS(bv[:c, :], pt[:c, 128:256], bb[:c, 384:385], ALU.mult)
        yield
        Xb = T2("b_Xb", BF16); CP(Xb[:c, :c], X[:c, :c], eng="dve")
        pw = pbank(); MM(pw[:, :c], kbg[:c, :], Xb[:c, :c])
        nWk = T2("b_nWk", BF16); S.I("act", "mul", out=nWk[:, :c], in_=pw[:, :c], mul=-1.0)
        yield
        if G > 1:
            kdm = tmp("b_kdm", [128, 16, 128], BF16, 1)
            o_, g_ = RM_OFF[gs]
            TT(kdm[:c, :G, :], kd[:c, :].un(1).bc([c, G, 128]), rmk[:c, o_:o_ + G].un(2).bc([c, G, 128]), ALU.mult)
        po = LN["X"]
        for g in range(G):
            gi_ = g if smp else 0
            cols = slice(g * gs, (g + 1) * gs)
            pv = pbank()
            MM(pv[:c, 0:128], Xb[:c, :c], bv[:c, :], start=True, stop=False)
            MM(pv[:c, 0:128], nWk[:, :c], Sb_[:, gi_, :], start=False, stop=True)
            Wg = T2("b_Wg", BF16); CP(Wg[:c, :], pv[:c, 0:128])
            yield
            MM(po[:, cols], Sb_[:, gi_, :], qgT[:, cols], start=True, stop=False)
            MM(po[:, cols], Wg[:c, :], Aqk[:c, cols], start=False, stop=True)
            pu = pbank()
            MM(pu[:, 0:128], kdm[:c, g, :] if G > 1 else kd[:c, :], Wg[:c, :])
            STT(Sf[:, gi_, :], Sf[:, gi_, :], eGe_bc[:, g:g + 1], pu[:, 0:128], ALU.mult, ALU.add)
            yield
            if not smp:
                CP(Sb_[:, 0, :], Sf[:, 0, :])
        hT = T2("b_hT"); CP(hT[:, :c], po[:, :c])
        y_finalize(l, 6 + h, ti, tile, hT, zf[:, :c])
        yield
        if smp:
            S.D("sp", oS[l, :, h, :, :].rearrange("g d e -> d g e"), Sf)
        elif tile["last"]:
            S.D("sp", pS[l, h, :, :], Sf[:, 0, :])
        yield

    def hgrn_setup(l, h):
        LN = LANES[cur[0]]
        LN["wp"] = 0
        pCs = LN["pCs"]
        pCb = LN["pCb"]
        wbs = [wblock(l, C_Q + h * 128), wblock(l, C_F + h * 128), wblock(l, C_I + h * 128), wblock(l, Z0 + (12 + h) * 128)]
        S.op("dve", lambda e: e.memset(pCs.ap, 0.0), [], [pCs.tk])
        S.op("dve", lambda e: e.memset(pCb.ap, 0.0), [], [pCb.tk])
        return wbs, None

    def hgrn_tile(l, h, ti, tile, pj, prefetch):
        LN = LANES[cur[0]]
        lbv, omlv = lb[:, l, h:h + 1], oml[:, l, h:h + 1]
        pCs = LN["pCs"]
        pCb = LN["pCb"]
        c, gs, t0 = tile["c"], tile["gC"], tile["t0"]
        G = c // gs
        smp = tile["kind"] == "s"
        if smp:
            Sf, Sb_ = sCc[0], sCb
            S.D("sp", Sf, sH[l, :, h, :, :].rearrange("g d e -> d g e"))
            CP(Sb_, Sf)
        else:
            Sf, Sb_ = pCs, pCb
        (q_ps, f_ps, i_ps, z_ps), _ = pj
        qraw = T2("c_qraw"); CP(qraw[:, :c], q_ps)
        e1 = T2("c_e1"); ACT(e1[:, :c], f_ps, AF.Exp, scale=-1.0)
        vf = T2("c_vf"); CP(vf[:, :c], i_ps)
        zf = T2("c_zf"); CP(zf[:, :c], z_ps)
        prefetch()
        yield
        eq = T2("c_eq"); ACT(eq[:, :c], qraw[:, :c], AF.Exp, scale=-1.0)
        RECIP1P(eq[:, :c], eq[:, :c])
        qf = T2("c_qf"); TT(qf[:, :c], qraw[:, :c], eq[:, :c], ALU.mult)
        yield
        TS(e1[:, :c], e1[:, :c], float(np.exp(60.0)), ALU.min)
        l1 = T2("c_l1"); ACT(l1[:, :c], e1[:, :c], AF.Ln, scale=lbv, bias=c_one)
        l2 = T2("c_l2"); ACT(l2[:, :c], e1[:, :c], AF.Ln, bias=c_one)
        nlf = T2("c_nlf"); TT(nlf[:, :c], l2[:, :c], l1[:, :c], ALU.subtract)
        r_ = T2("c_r"); ACT(r_[:, :c], l2[:, :c], AF.Exp, scale=-1.0)
        kf = T2("c_kf"); STT(kf[:, :c], e1[:, :c], omlv, r_[:, :c], ALU.mult, ALU.mult)
        yield
        bn = T2("c_bn"); scan(bn[:, :c], resets[:, RESET_IDX[gs], :c], nlf[:, :c])
        eb = T2("c_eb"); ACT(eb[:, :c], bn[:, :c], AF.Exp, scale=-1.0)
        enb = T2("c_enb"); ACT(enb[:, :c], bn[:, :c], AF.Exp)
        qeb = T2("c_qeb", BF16); TT(qeb[:, :c], qf[:, :c], eb[:, :c], ALU.mult)
        keb = T2("c_keb", BF16); TT(keb[:, :c], kf[:, :c], enb[:, :c], ALU.mult)
        yield
        ebe = tmp("c_ebe", [128, 16], F32); ACT(ebe[:, :G], g3(bn, c, G)[:, :, gs - 1], AF.Exp, scale=-1.0)
        kdT = T2("c_kdT"); TT(g3(kdT, c, G), g3(bn, c, G), g3(bn, c, G)[:, :, gs - 1:gs].bc([128, G, gs]), ALU.subtract)
        ACT(kdT[:, :c], kdT[:, :c], AF.Exp)
        TT(kdT[:, :c], kdT[:, :c], kf[:, :c], ALU.mult)
        yield
        pt = pbank()
        TR(pt[:c, 0:128], kdT[:, :c])
        TR(pt[:c, 128:256], vf[:, :c])
        kd = T2("c_kd", BF16); CP(kd[:c, :], pt[:c, 0:128])
        vb = T2("c_vb", BF16); CP(vb[:c, :], pt[:c, 128:256], eng="dve")
        yield
        if G > 1:
            kdm = tmp("c_kdm", [128, 16, 128], BF16, 1)
            o_, g_ = RM_OFF[gs]
            TT(kdm[:c, :G, :], kd[:c, :].un(1).bc([c, G, 128]), rmk[:c, o_:o_ + G].un(2).bc([c, G, 128]), ALU.mult)
        pa = pbank(); MM(pa[:c, :c], keb[:, :c], qeb[:, :c])
        mname = "U8" if smp else ("U16" if c == 128 else "U128")
        AT = T2("c_AT", BF16); TT(AT[:c, :c], pa[:c, :c], mask(mname, c), ALU.mult)
        yield
        po = LN["X"]
        MM(po[:, :c], vb[:c, :], AT[:c, :c], start=True, stop=False)
        for g in range(G):
            gi_ = g if smp else 0
            cols = slice(g * gs, (g + 1) * gs)
            MM(po[:, cols], Sb_[:, gi_, :], qeb[:, cols], start=False, stop=(g == G - 1))
            pu = pbank()
            MM(pu[:, 0:128], kdm[:c, g, :] if G > 1 else kd[:c, :], vb[:c, :])
            STT(Sf[:, gi_, :], Sf[:, gi_, :], ebe[:, g:g + 1], pu[:, 0:128], ALU.mult, ALU.add)
            yield
            if not smp:
                CP(Sb_[:, 0, :], Sf[:, 0, :])
        hT = T2("c_hT"); CP(hT[:, :c], po[:, :c])
        y_finalize(l, 12 + h, ti, tile, hT, zf[:, :c])
        yield
        if smp:
            S.D("sp", oH[l, :, h, :, :].rearrange("g d e -> d g e"), Sf)
        elif tile["last"]:
            S.D("sp", pH[l, h, :, :], Sf[:, 0, :])
        yield

    wo_c = []

    def out_stage(l):
        S.region("O")
        if not wo_c:
            wo_c.append(S.sb("wo", [128, 16, D], BF16))
        wo = wo_c[0]
        for q4 in range(4):
            S.D("pool", wo[:, q4 * 4:(q4 + 1) * 4, :], w_out[l, q4 * 512:(q4 + 1) * 512, :].rearrange("(h p) n -> p h n", p=128))
        for ti, tile in enumerate(TILES):
            c, t0 = tile["c"], tile["t0"]
            yt = tmp("op_y", [128, 16, 128], BF16, 2)
            S.D("sp", yt[:, :, :c], yTv[ti].re("h p t -> p h t"))
            xo = tmp("op_xo", [128, NCH, 128], F32, 2)
            S.D("sp", xo[:, :, :c], xrv[ti].re("j p t -> p j t"))
            xn = tmp("xT", [128, NCH, 128], F32, 2)
            for j in range(NCH):
                ps = pbank()
                for hh in range(16):
                    MM(ps[:, :c], wo[:, hh, j * 128:(j + 1) * 128], yt[:, hh, :c], start=(hh == 0), stop=(hh == 15))
                TT(xn[:, j, :c], xo[:, j, :c], ps[:, :c], ALU.add)
            finish_x(ti, tile, xn, l + 1)

    def run_phase(kind, l, pairs):
        setup = {"a": mlstm_setup, "b": gdn_setup, "c": hgrn_setup}[kind]
        tilef = {"a": mlstm_tile, "b": gdn_tile, "c": hgrn_tile}[kind]
        S.region("H")
        pt_ = [(ti, t) for ti, t in enumerate(TILES) if t["kind"] == "p"]

        def nsteps(items):
            n, depth = 0, 0
            for it in items:
                if it[0] == "gs":
                    if depth == 0:
                        n += 1
                    depth += 1
                elif it[0] == "ge":
                    depth -= 1
                elif depth == 0:
                    n += 1
            return n

        def head_segments(ln, h):
            cur[0] = ln
            S.rec = []
            ws = setup(l, h)
            for ti, tile in enumerate(TILES):
                if tile["kind"] == "s":
                    for _ in tilef(l, h, ti, tile, proj(tile, *ws), lambda: None):
                        pass
            segs = [S.rec]
            S.rec = []
            nxt = {0: proj(pt_[0][1], *ws)}
            bnd = [0]
            for k, (ti, tile) in enumerate(pt_):
                if k > 0:
                    bnd.append(len(S.rec))

                def prefetch(k=k):
                    if k + 1 < len(pt_):
                        nxt[k + 1] = proj(pt_[k + 1][1], *ws)
                for _ in tilef(l, h, ti, tile, nxt[k], prefetch):
                    pass
            full_ = S.rec
            S.rec = None
            bnd.append(len(full_))
            segs += [full_[bnd[i]:bnd[i + 1]] for i in range(len(bnd) - 1)]
            return segs

        lane_heads = [[p[0] for p in pairs], [p[1] for p in pairs]]
        all_segs = [[head_segments(ln, h) for h in lane_heads[ln]] for ln in range(2)]
        T = max(nsteps(sg) for segs in all_segs[0] for sg in segs[1:])
        n_s = max(nsteps(segs[0]) for ln in range(2) for segs in all_segs[ln])
        off = n_s + ((-n_s) % T)
        streams = []
        for ln in range(2):
            st = []
            for segs in all_segs[ln]:
                for i, sg in enumerate(segs):
                    tgt = off if i == 0 else T
                    st += sg
                    st += [("nop",)] * (tgt - nsteps(sg))
            streams.append(st)
        assert 2 * off <= off + 17 * T
        S.replay(streams, offset=off)
        cur[0] = -1

    def full():
        cur[0] = -1
        stage_a()
        S.barrier()
        for l in range(n_layers):
            run_phase("a", l, [(0, 1), (2, 3), (4, 5)])
            run_phase("b", l, [(0, 1), (2, 3), (4, 5)])
            run_phase("c", l, [(0, 1), (2, 3)])
            S.barrier()
            cur[0] = -1
            out_stage(l)
            S.barrier()

    return nc, S, locals()


def core_inputs(inp, c, consts):
    s = c // 2
    sl = slice(16 * c, 16 * c + 16)
    f = lambda a: np.ascontiguousarray(np.asarray(a, dtype=np.float32))
    m = {
        "xp": f(np.concatenate([inp["meta_tokens"], inp["x_prompt"][s]], axis=0)),
        "xs": f(np.asarray(inp["x_sample"])[sl].reshape(128, D)),
        "w_in": consts["_w_in_r"], "w_gc": consts["_w_gc"], "w_out": f(inp["w_out"]),
        "norm_w": f(inp["norm_w"]), "out_norm_w": f(inp["out_norm_w"]), "final_norm_w": f(inp["final_norm_w"]),
        "mlstm_gate_b": f(inp["mlstm_gate_b"]), "gdn_A_log": f(inp["gdn_A_log"]), "gdn_dt_bias": f(inp["gdn_dt_bias"]),
        "gdn_conv_w": f(inp["gdn_conv_w"]), "hgrn_lower_bounds": f(inp["hgrn_lower_bounds"]),
        "sC": f(np.asarray(inp["state_mlstm_C"])[:, sl]), "sn": f(np.asarray(inp["state_mlstm_n"])[:, sl]),
        "sm": f(np.asarray(inp["state_mlstm_m"])[:, sl]), "sS": f(np.asarray(inp["state_gdn_S"])[:, sl]),
        "sconv": f(np.asarray(inp["state_gdn_conv"])[:, sl]), "sH": f(np.asarray(inp["state_hgrn_S"])[:, sl]),
    }
    m.update({k: v for k, v in consts.items() if not k.startswith("_")})
    return m


def relayout_w_in(w_in):
    w = np.asarray(w_in, dtype=np.float32)
    out = np.empty((2, 70, 128, NCH * 128), np.float32)
    for i, c0 in enumerate(_blk_cols):
        blk = w[:, :, c0:c0 + 128].reshape(2, NCH, 128, 128)
        out[:, i] = blk.transpose(0, 2, 1, 3).reshape(2, 128, NCH * 128)
    g = w[:, :, _gate_cols].reshape(2, NCH, 128, 24).transpose(0, 2, 1, 3)
    return out, np.ascontiguousarray(g)


_CACHE = {}


def kernel(**inputs):
    if "nc" not in _CACHE:
        nc, S, L = build_nc(n_layers=2)
        L["full"]()
        S.finish()
        _CACHE["nc"] = nc
    nc = _CACHE["nc"]
    consts = host_consts()
    consts["_w_in_r"], consts["_w_gc"] = relayout_w_in(inputs["w_in"])
    in_maps = [core_inputs(inputs, c, consts) for c in range(8)]
    res = run_bass_kernel_spmd(nc, in_maps, core_ids=list(range(8)))
    R = res.results
    f = lambda a: np.asarray(a, dtype=np.float32)
    y_prompt = np.stack([f(R[2 * s]["yp"])[16:] for s in range(4)], axis=0)
    y_sample = np.concatenate([f(R[c]["ys"]).reshape(16, 8, D) for c in range(8)], axis=0)
    pst = lambda k: np.stack([f(R[2 * s][k]) for s in range(4)], axis=1)
    sst = lambda k: np.concatenate([f(R[c][k]) for c in range(8)], axis=1)
    return (y_prompt, y_sample,
            pst("pC"), pst("pn"), pst("pm"), pst("pS"), pst("pconv"), pst("pH"),
            sst("oC"), sst("on"), sst("om"), sst("oS"), sst("oconv"), sst("oH"))
```

```python
import contextlib
import numpy as np
import concourse.bass as bass
import concourse.mybir as mybir
from concourse.bass_utils import run_bass_kernel_spmd

F32 = mybir.dt.float32
BF16 = mybir.dt.bfloat16
I32 = mybir.dt.int32
AF = mybir.ActivationFunctionType
ALU = mybir.AluOpType
AX = mybir.AxisListType


class Tk:
    def __init__(self, name, psum=False):
        self.name = name
        self.psum = psum
        self.w = None
        self.r = []


class V:
    def __init__(self, tk, ap):
        self.tk = tk
        self.ap = ap

    def __getitem__(self, k):
        return V(self.tk, self.ap[k])

    def bc(self, shape):
        return V(self.tk, self.ap.broadcast_to(list(shape)))

    def un(self, axis):
        return V(self.tk, self.ap.unsqueeze(axis))

    def re(self, pat, **kw):
        return V(self.tk, self.ap.rearrange(pat, **kw))

    @property
    def shape(self):
        return self.ap.shape


class Sched:
    ENGS = ("pe", "act", "dve", "pool", "sp")
    ROT = 30000

    def __init__(self, nc):
        self.nc = nc
        self.es = contextlib.ExitStack()
        self.q = {e: [] for e in self.ENGS}
        self.cnt = {e: 0 for e in self.ENGS}
        self.nsem = 0
        self.sem = {e: self._newsem(e) for e in self.ENGS}
        self.waited = {e: {} for e in self.ENGS}
        self.dsem = {}
        self.n_ops = 0
        self.sb_off = 0
        self.sb_max = 0
        self.nalloc = 0
        self.rec = None
        self.offs = {"P": 16512}
        self.cur = "P"

    def split(self):
        r0 = self.offs["P"]
        self.offs["H"] = r0
        self.offs["O"] = r0

    def region(self, r):
        self.cur = r

    def _newsem(self, tag):
        self.nsem += 1
        return self.es.enter_context(self.nc.semaphore(f"s{self.nsem}_{tag}"))

    def sb(self, name, shape, dtype):
        nb = 2 if dtype == BF16 else 4
        n = 1
        for d in shape[1:]:
            n *= d
        size = (n * nb + 63) // 64 * 64
        off = self.offs[self.cur]
        self.offs[self.cur] = off + size
        self.sb_off = off + size
        self.sb_max = max(self.sb_max, self.sb_off)
        assert self.sb_off <= 229376, f"SBUF overflow at {name}: {self.sb_off} region {self.cur}"
        self.nalloc += 1
        t = self.nc.alloc_sbuf_tensor_at(f"{name}_{self.nalloc}", list(shape), dtype, offset=off)
        return V(Tk(name), t[:])

    def ps(self, name, shape, dtype=None):
        t = self.nc.alloc_psum_tensor(name, list(shape), F32)
        return V(Tk(name, psum=True), t[:])

    def I(self, eng, meth, **kw):
        reads, writes, args = [], [], {}
        for k, v in kw.items():
            if isinstance(v, V):
                if k in ("out", "accum_out") or v.tk.psum:
                    writes.append(v.tk)
                else:
                    reads.append(v.tk)
                args[k] = v.ap
            else:
                args[k] = v
        self.op(eng, lambda e: getattr(e, meth)(**args), reads, writes)

    def D(self, eng, out, in_, **kw):
        reads, writes = [], []
        names = []
        if isinstance(in_, V):
            reads.append(in_.tk)
            names.append(in_.tk.name)
            in_ = in_.ap
        if isinstance(out, V):
            writes.append(out.tk)
            names.append(out.tk.name)
            out = out.ap
        sbn = [n for n in names if not n.startswith("dram:")]
        key = sbn[0] if sbn else names[0]
        self.dma(eng, out, in_, reads, writes, key=key, **kw)

    def replay(self, lists, offset=0):
        self.rec = None
        idx = [0] * len(lists)
        step = 0
        while any(idx[k] < len(lists[k]) for k in range(len(lists))):
            for k, lst in enumerate(lists):
                if step < k * offset or idx[k] >= len(lst):
                    continue
                depth = 0
                while idx[k] < len(lst):
                    it = lst[idx[k]]
                    idx[k] += 1
                    if it[0] == "nop":
                        pass
                    elif it[0] == "gs":
                        depth += 1
                    elif it[0] == "ge":
                        depth -= 1
                    elif it[0] == "op":
                        self.op(*it[1:])
                    else:
                        self.dma(it[1], it[2], it[3], it[4], it[5], key=it[6], **it[7])
                    if depth == 0:
                        break
            step += 1

    def barrier(self):
        evs = [(self.sem[e], self.cnt[e]) for e in self.ENGS if self.cnt[e] > 0]
        evs += [(sem, val) for (sem, val) in self.dsem.values() if val > 0]
        for e in self.ENGS:
            wd = self.waited[e]
            for (sem, val) in evs:
                if sem is self.sem[e] and e == "pe":
                    continue
                if wd.get(id(sem), (None, 0))[1] >= val:
                    continue
                wd[id(sem)] = (sem, val)
                self.q[e].append(("wait", sem, val))

    def _deps(self, eng, reads, writes):
        deps = []
        for t in reads:
            if t.w is not None:
                deps.append(t.w)
        for t in writes:
            if t.w is not None:
                deps.append(t.w)
            deps.extend(t.r)
        wd = self.waited[eng]
        need = {}
        for (sem, val, e2) in deps:
            if e2 == "pe" and eng == "pe":
                continue
            k = id(sem)
            if wd.get(k, (None, 0))[1] >= val:
                continue
            if k not in need or need[k][1] < val:
                need[k] = (sem, val)
        for k, (sem, val) in need.items():
            wd[k] = (sem, val)
            self.q[eng].append(("wait", sem, val))

    def _record(self, ev, reads, writes):
        for t in reads:
            t.r.append(ev)
        for t in writes:
            t.w = ev
            t.r = []

    def op(self, eng, fn, reads, writes):
        if self.rec is not None:
            self.rec.append(("op", eng, fn, reads, writes))
            return
        self._deps(eng, reads, writes)
        if self.cnt[eng] >= self.ROT:
            self.sem[eng] = self._newsem(eng)
            self.cnt[eng] = 0
        self.cnt[eng] += 1
        ev = (self.sem[eng], self.cnt[eng], eng)
        self.q[eng].append(("op", fn, self.sem[eng], 1))
        self._record(ev, reads, writes)
        self.n_ops += 1

    def dma(self, eng, out, in_, reads, writes, key=None, **kw):
        if self.rec is not None:
            self.rec.append(("dma", eng, out, in_, reads, writes, key, kw))
            return
        self._deps(eng, reads, writes)
        if key is None:
            key = (writes[0] if writes else reads[0]).name
        if key not in self.dsem:
            self.dsem[key] = [self._newsem("d"), 0]
        ds = self.dsem[key]
        ds[1] += 16
        ev = (ds[0], ds[1], "dma")
        self.q[eng].append(("op", (lambda e, out=out, in_=in_, kw=kw: e.dma_start(out=out, in_=in_, **kw)), ds[0], 16))
        self._record(ev, reads, writes)
        self.n_ops += 1

    def finish(self):
        for key, (sem, val) in self.dsem.items():
            if val > 0:
                self.q["sp"].append(("wait", sem, val))
        for e in self.ENGS:
            if e != "sp" and self.cnt[e] > 0:
                self.q["sp"].append(("wait", self.sem[e], self.cnt[e]))
        nc = self.nc
        q = self.q

        def emit(engine, lst):
            for it in lst:
                if it[0] == "wait":
                    engine.wait_ge(it[1], it[2])
                else:
                    ins = it[1](engine)
                    ins.then_inc(it[2], it[3])

        with nc.allow_non_contiguous_dma(reason="small strided state/param transfers"), nc.Block() as block:
            @block.tensor
            def _(e):
                emit(e, q["pe"])

            @block.scalar
            def _(e):
                emit(e, q["act"])

            @block.vector
            def _(e):
                emit(e, q["dve"])

            @block.gpsimd
            def _(e):
                emit(e, q["pool"])

            @block.sync
            def _(e):
                emit(e, q["sp"])
        self.es.close()


D = 2048
NCH = 16
NP = 2064
NS = 128
NT = NP + NS
N_IN = 8984
A_Q, A_K, A_V, A_O, A_I, A_F = 0, 768, 1536, 2304, 3072, 3078
B_Q, B_K, B_V, B_A, B_B = 3084, 3852, 4620, 5388, 5394
C_Q, C_F, C_I, Z0 = 5400, 5912, 6424, 6936
EPS = 1e-6
RS = 128 ** -0.5
LANE_OFFSET = 110
_blk_cols = ([A_Q + i * 128 for i in range(24)] + [B_Q + i * 128 for i in range(18)]
             + [C_Q + i * 128 for i in range(12)] + [Z0 + i * 128 for i in range(16)])
BLK_OF = {c: i for i, c in enumerate(_blk_cols)}
_gate_cols = list(range(A_I, A_I + 12)) + list(range(B_A, B_A + 12))
GC_OF = {c: i for i, c in enumerate(_gate_cols)}
MASK_IDX = {"U128": 0, "U64": 1, "SU64": 2, "U32": 3, "U8": 4, "SU8": 5}
RESET_IDX = {128: 0, 64: 1, 32: 2, 8: 3, 16: 0}
RM_OFF = {64: (0, 2), 32: (2, 4), 8: (10, 16)}


def host_consts():
    idx = np.arange(128)
    masks = np.zeros((6, 128, 128), np.float32)
    for name, gs, strict in (("U128", 128, False), ("U64", 64, False), ("SU64", 64, True),
                             ("U32", 32, False), ("U8", 8, False), ("SU8", 8, True)):
        same = (idx[:, None] // gs) == (idx[None, :] // gs)
        tri = (idx[:, None] < idx[None, :]) if strict else (idx[:, None] <= idx[None, :])
        masks[MASK_IDX[name]] = (same & tri).astype(np.float32)
    resets = np.ones((4, 128, 128), np.float32)
    for gs, i in RESET_IDX.items():
        if gs != 16:
            resets[i][:, (idx % gs) == 0] = 0.0
    rm = np.zeros((128, 26), np.float32)
    for gs, (o, g) in RM_OFF.items():
        for j in range(g):
            rm[(idx // gs) == j, o + j] = 1.0
    return {"c_ident": np.eye(128, dtype=np.float32), "c_masks": masks, "c_resets": resets, "c_rm": rm}


def make_tiles():
    tiles = [dict(kind="p", t0=0, c=16, gA=16, gB=16, gC=16, first=True, last=False)]
    for i in range(16):
        tiles.append(dict(kind="p", t0=16 + 128 * i, c=128, gA=128, gB=64, gC=32, first=False, last=(i == 15)))
    tiles.append(dict(kind="s", t0=NP, c=128, gA=8, gB=8, gC=8, first=True, last=True))
    return tiles


def build_nc(n_layers=2, heads=None, do_out=True, debug=False):
    nc = bass.Bass("TRN2", target_bir_lowering=False)
    S = Sched(nc)

    def din(name, shape):
        return nc.dram_tensor(name, list(shape), F32, kind="ExternalInput").ap()

    def dout(name, shape):
        return nc.dram_tensor(name, list(shape), F32, kind="ExternalOutput").ap()

    xp = din("xp", [NP, D]); xs = din("xs", [NS, D])
    w_in = din("w_in", [2, 70, 128, NCH * 128]); w_gc = din("w_gc", [2, 128, NCH, 24]); w_out = din("w_out", [2, D, D])
    norm_w = din("norm_w", [2, D]); out_norm_w = din("out_norm_w", [2, D]); final_norm_w = din("final_norm_w", [D])
    gate_b = din("mlstm_gate_b", [2, 2, 6]); A_log = din("gdn_A_log", [2, 6]); dt_bias = din("gdn_dt_bias", [2, 6])
    conv_w = din("gdn_conv_w", [2, 4, 2304]); lbp = din("hgrn_lower_bounds", [2, 512])
    sC = din("sC", [2, 16, 6, 128, 128]); sn = din("sn", [2, 16, 6, 128]); sm = din("sm", [2, 16, 6])
    sS = din("sS", [2, 16, 6, 128, 128]); sconv = din("sconv", [2, 16, 3, 2304]); sH = din("sH", [2, 16, 4, 128, 128])
    c_ident = din("c_ident", [128, 128]); c_masks = din("c_masks", [6, 128, 128])
    c_resets = din("c_resets", [4, 128, 128]); c_rm = din("c_rm", [128, 26])

    yp = dout("yp", [NP, D]); ys = dout("ys", [NS, D])
    pC = dout("pC", [2, 6, 128, 128]); pn = dout("pn", [2, 6, 128]); pm = dout("pm", [2, 6])
    pS = dout("pS", [2, 6, 128, 128]); pconv = dout("pconv", [2, 3, 2304]); pH = dout("pH", [2, 4, 128, 128])
    oC = dout("oC", [2, 16, 6, 128, 128]); on = dout("on", [2, 16, 6, 128]); om = dout("om", [2, 16, 6])
    oS = dout("oS", [2, 16, 6, 128, 128]); oconv = dout("oconv", [2, 16, 3, 2304]); oH = dout("oH", [2, 16, 4, 128, 128])

    skind = "ExternalOutput" if debug else "Internal"
    xres_t = nc.dram_tensor("xres", [NCH, 128, NT], F32, kind=skind).ap()
    yT_t = nc.dram_tensor("yTs", [16, 128, NT], BF16, kind=skind).ap()
    xres = V(Tk("dram:xres"), xres_t)
    yTd = V(Tk("dram:yT"), yT_t)

    TILES = make_tiles()

    def ACT(out, in_, func, scale=1.0, bias=None):
        kw = dict(out=out, in_=in_, func=func, scale=scale)
        if bias is not None:
            kw["bias"] = bias
        S.I("act", "activation", **kw)

    def TT(out, in0, in1, op, eng="dve"):
        S.I(eng, "tensor_tensor", out=out, in0=in0, in1=in1, op=op)

    def TS(out, in0, s1, op0, s2=None, op1=None, eng="dve"):
        if op1 is None:
            S.I(eng, "tensor_scalar", out=out, in0=in0, scalar1=s1, scalar2=None, op0=op0)
        else:
            S.I(eng, "tensor_scalar", out=out, in0=in0, scalar1=s1, scalar2=s2, op0=op0, op1=op1)

    def STT(out, in0, scalar, in1, op0, op1):
        S.I("dve", "scalar_tensor_tensor", out=out, in0=in0, scalar=scalar, in1=in1, op0=op0, op1=op1)

    def MM(out, lhsT, rhs, start=True, stop=True):
        S.I("pe", "matmul", out=out, lhsT=lhsT, rhs=rhs, start=start, stop=stop)

    def CP(out, in_, eng="act"):
        if eng == "act":
            S.I("act", "copy", out=out, in_=in_)
        else:
            S.I(eng, "tensor_copy", out=out, in_=in_)

    def RECIP(out, in_):
        ACT(out, in_, AF.Ln)
        ACT(out, out, AF.Exp, scale=-1.0)

    def RECIP1P(out, in_):
        ACT(out, in_, AF.Ln, bias=c_one[0:out.shape[0], :])
        ACT(out, out, AF.Exp, scale=-1.0)

    rings = {}
    cur = [-1]
    LANES = []

    slots = {}

    def tmp(tag, shape, dtype, n=2):
        lane_tag = tag[:2] in ("a_", "b_", "c_") or tag[:3] == "yf_"
        if lane_tag:
            n = 1
        if tag[:2] in ("a_", "b_", "c_"):
            k = (tag[:2], tuple(shape), str(dtype), n)
            d = slots.setdefault(k, {})
            if tag not in d:
                d[tag] = len(d)
            tag = f"mix_{tuple(shape)}_{dtype}_{n}_{d[tag]}"
        if lane_tag:
            tag = f"{tag}@{max(cur[0], 0)}"
        if tag not in rings:
            rings[tag] = [[S.sb(f"{tag}{i}", shape, dtype) for i in range(n)], 0]
        r = rings[tag]
        v = r[0][r[1] % n]
        r[1] += 1
        return v

    ident = S.sb("ident", [128, 128], F32)
    masks = S.sb("masks", [128, 6, 128], F32)
    resets = S.sb("resets", [128, 4, 128], F32)
    rmk = S.sb("rmk", [128, 26], F32)
    ones_f = S.sb("ones_f", [128, 128], F32)
    ones_b = S.sb("ones_b", [128, 128], BF16)
    c_one = S.sb("c_one", [128, 1], F32)
    c_eps = S.sb("c_eps", [128, 1], F32)
    S.D("sp", ident, c_ident)
    S.D("sp", masks, c_masks.rearrange("m p n -> p m n"))
    S.D("sp", resets, c_resets.rearrange("m p n -> p m n"))
    S.D("sp", rmk, c_rm)
    S.I("dve", "memset", ap=ones_f, constant=1.0) if False else S.op("dve", lambda e: e.memset(ones_f.ap, 1.0), [], [ones_f.tk])
    S.op("dve", lambda e: e.memset(ones_b.ap, 1.0), [], [ones_b.tk])
    S.op("dve", lambda e: e.memset(c_one.ap, 1.0), [], [c_one.tk])
    S.op("dve", lambda e: e.memset(c_eps.ap, EPS), [], [c_eps.tk])

    def mask(name, c):
        return masks[:c, MASK_IDX[name], :c]

    def TR(out, in_):
        k = in_.shape[0]
        S.I("pe", "transpose", out=out, in_=in_, identity=ident[:k, :k])

    pb = [S.ps(f"pb{i}", [128, 512]) for i in range(8)]
    pj_sets = [(pb[0], pb[1]), (pb[2], pb[3])]
    pring = [0]

    def pbank():
        if cur[0] < 0:
            b = pb[4 + pring[0] % 4]
            pring[0] += 1
            return b
        L = LANES[cur[0]]
        b = L["ring"][L["rp"] % 2]
        L["rp"] += 1
        return b

    gb_row = S.sb("gb_row", [1, 24], F32)
    ngb_row = S.sb("ngb_row", [1, 24], F32)
    al_row = S.sb("al_row", [1, 12], F32)
    dtb_row = S.sb("dtb_row", [1, 12], F32)
    S.D("sp", gb_row, gate_b.rearrange("l w h -> (l w h)").unsqueeze(0))
    S.D("sp", al_row, A_log.rearrange("l h -> (l h)").unsqueeze(0))
    S.D("sp", dtb_row, dt_bias.rearrange("l h -> (l h)").unsqueeze(0))
    TS(ngb_row, gb_row, -1.0, ALU.mult)
    ACT(al_row, al_row, AF.Exp)
    lbraw = S.sb("lbraw", [128, 2, 4], F32)
    S.D("sp", lbraw, lbp.rearrange("l (h p) -> p l h", p=128))
    lb = S.sb("lb", [128, 2, 4], F32)
    oml = S.sb("oml", [128, 2, 4], F32)
    S.op("dve", lambda e: e.memset(lb.ap, 0.0), [], [lb.tk])
    TT(lb[:, 1, :], lbraw[:, 0, :], lbraw[:, 1, :], ALU.subtract)
    ACT(lb[:, 1, :], lb[:, 1, :], AF.Exp)
    RECIP1P(lb[:, 1, :], lb[:, 1, :])
    TS(oml, lb, -1.0, ALU.mult, 1.0, ALU.add)
    nw = S.sb("nw", [128, 2, NCH], F32)
    onw = S.sb("onw", [128, 2, NCH], F32)
    fnw = S.sb("fnw", [128, NCH], F32)
    S.D("sp", nw, norm_w.rearrange("l (j p) -> p l j", p=128))
    S.D("sp", onw, out_norm_w.rearrange("l (j p) -> p l j", p=128))
    S.D("sp", fnw, final_norm_w.rearrange("(j p) -> p j", p=128))
    cw = S.sb("cw", [128, 2, 18, 4], F32)
    for l_ in range(2):
        for j_ in range(4):
            S.D("sp", cw[:, l_, :, j_], conv_w[l_, j_, :].rearrange("(b p) -> p b", p=128))

    xnT = S.sb("xnT", [128, NCH, NT], BF16)

    def finish_x(ti, tile, xT, layer_next):
        c, t0 = tile["c"], tile["t0"]
        if layer_next < n_layers:
            S.D("sp", xrv[ti].re("j p t -> p j t"), xT[:, :, :c])
        sq = tmp("fx_sq", [128, NCH, 128], BF16, 1)
        S.I("act", "activation", out=sq[:, :, :c], in_=xT[:, :, :c], func=AF.Square)
        ps = pbank()
        for j in range(NCH):
            MM(ps[:, :c], ones_b, sq[:, j, :c], start=(j == 0), stop=(j == NCH - 1))
        rstd = tmp("fx_rstd", [128, 128], F32, 2)
        ACT(rstd[:, :c], ps[:, :c], AF.Ln, scale=1.0 / D, bias=c_eps)
        ACT(rstd[:, :c], rstd[:, :c], AF.Exp, scale=-0.5)
        t1 = tmp("fx_t1", [128, NCH, 128], F32, 1)
        TT(t1[:, :, :c], xT[:, :, :c], rstd[:, :c].un(1).bc([128, NCH, c]), ALU.mult)
        if layer_next < n_layers:
            TT(xnT[:, :, t0:t0 + c], t1[:, :, :c], nw[:, layer_next, :].un(2).bc([128, NCH, c]), ALU.mult)
        else:
            TT(t1[:, :, :c], t1[:, :, :c], fnw.un(2).bc([128, NCH, c]), ALU.mult)
            ot = tmp("sa_x", [128, D], F32, 1)
            for q4 in range(4):
                pt = pbank()
                for jj in range(4):
                    j = q4 * 4 + jj
                    TR(pt[:c, jj * 128:(jj + 1) * 128], t1[:, j, :c])
                CP(ot[:c, q4 * 512:(q4 + 1) * 512], pt[:c, :])
            dst = yp[t0:t0 + c, :] if tile["kind"] == "p" else ys[:, :]
            S.D("sp", dst, ot[:c, :])

    def stage_a():
        S.region("O")
        for ti, tile in enumerate(TILES):
            c, t0 = tile["c"], tile["t0"]
            xt = tmp("sa_x", [128, D], F32, 1)
            src = xp[t0:t0 + c, :] if tile["kind"] == "p" else xs[:, :]
            S.D("sp", xt[:c, :], src)
            xT = tmp("xT", [128, NCH, 128], F32, 2)
            for q4 in range(4):
                pt = pbank()
                for jj in range(4):
                    j = q4 * 4 + jj
                    TR(pt[:, jj * 128:jj * 128 + c], xt[:c, j * 128:(j + 1) * 128])
                CP(xT[:, q4 * 4:(q4 + 1) * 4, :c], pt.re("p (a b) -> p a b", a=4)[:, :, :c])
            finish_x(ti, tile, xT, 0)


    S.split()
    S.region("H")
    sA = [S.sb("sA0", [128, 16, 129], F32)]
    sSb = S.sb("sSb", [128, 16, 128], BF16)
    sB = [sA[0][:, :, 0:128]]
    sBb = sSb
    sCc = [sA[0][:, :, 0:128]]
    sCb = sSb
    mrow_s = S.sb("mrow_s", [1, 16], F32)
    ue_s = [S.sb(f"ue_s{i}", [128, 16, 11], F32) for i in range(3)]
    Cb = S.sb("Cb", [128, 16, 128], BF16)
    nbb = S.sb("nbb", [128, 16, 128], BF16)
    for i_ in range(2):
        bk = pb[4 * i_:4 * i_ + 4]
        LANES.append(dict(
            i=i_, PJ=bk[0], X=bk[1], ring=[bk[2], bk[3]], rp=0,
            wring=[S.sb(f"wblk{i_}_{k}", [128, NCH, 128], BF16) for k in range(5)], wp=0,
            wg=S.sb(f"wg{i_}", [128, NCH, 2], BF16),
            pA=S.sb(f"pA{i_}", [128, 1, 129], F32), mrow_p=S.sb(f"mrow_p{i_}", [1, 16], F32),
            pBs=S.sb(f"pBs{i_}", [128, 1, 128], F32), pBb=S.sb(f"pBb{i_}", [128, 1, 128], BF16),
            pCs=S.sb(f"pCs{i_}", [128, 1, 128], F32), pCb=S.sb(f"pCb{i_}", [128, 1, 128], BF16),
            ue_p=[S.sb(f"ue_p{i_}_{k}", [128, 1, 131], F32) for k in range(3)],
            Cbp=S.sb(f"Cbp{i_}", [128, 1, 128], BF16), nbp=S.sb(f"nbp{i_}", [128, 1, 128], BF16),
        ))

    def wblock(l, col0):
        L = LANES[cur[0]]
        wb = L["wring"][L["wp"] % 5]
        L["wp"] += 1
        S.D("pool", wb, w_in[l, BLK_OF[col0], :, :].rearrange("p (j n) -> p j n", j=NCH))
        return wb

    def wgate(l, col_a, col_b):
        wg = LANES[cur[0]]["wg"]
        S.D("pool", wg[:, :, 0:1], w_gc[l, :, :, GC_OF[col_a]:GC_OF[col_a] + 1])
        S.D("pool", wg[:, :, 1:2], w_gc[l, :, :, GC_OF[col_b]:GC_OF[col_b] + 1])
        return wg

    def proj(tile, wbs, wg):
        if S.rec is not None:
            S.rec.append(("gs",))
        r = proj_(tile, wbs, wg)
        if S.rec is not None:
            S.rec.append(("ge",))
        return r

    def proj_(tile, wbs, wg):
        c, t0 = tile["c"], tile["t0"]
        L = LANES[cur[0]]
        outs = []
        for b, wb in enumerate(wbs):
            bank = L["PJ"] if b < 4 else L["X"]
            o = bank[:, (b % 4) * 128:(b % 4) * 128 + c]
            for j in range(NCH):
                MM(o, wb[:, j, :], xnT[:, j, t0:t0 + c], start=(j == 0), stop=(j == NCH - 1))
            outs.append(o)
        gr = []
        if wg is not None:
            for i in range(2):
                o = L["X"][0:1, 256 + i * 128:256 + i * 128 + c]
                for j in range(NCH):
                    MM(o, wg[:, j, i:i + 1], xnT[:, j, t0:t0 + c], start=(j == 0), stop=(j == NCH - 1))
                gr.append(o)
        return outs, gr

    def par(*fns):
        outer = S.rec
        lists = []
        for f in fns:
            S.rec = []
            f()
            lists.append(S.rec)
        S.rec = outer
        idx = [0] * len(lists)
        while any(idx[k] < len(lists[k]) for k in range(len(lists))):
            for k, lst in enumerate(lists):
                if idx[k] < len(lst):
                    it = lst[idx[k]]
                    idx[k] += 1
                    if outer is not None:
                        outer.append(it)
                    elif it[0] == "op":
                        S.op(*it[1:])
                    else:
                        S.dma(it[1], it[2], it[3], it[4], it[5], key=it[6], **it[7])

    def scan(out, msk, data):
        S.I("dve", "tensor_tensor_scan", out=out, data0=msk, data1=data, initial=0.0, op0=ALU.mult, op1=ALU.add)

    def T2(tag, dtype=F32, n=2):
        return tmp(tag, [128, 128], dtype, n)

    def R1(tag, w=128, n=2):
        return tmp(tag, [1, w], F32, n)

    def g3(v, c, G):
        return v[:, :c].re("p (a b) -> p a b", a=G)

    def y_finalize(l, hg, ti, tile, hT, z_ps):
        c, t0 = tile["c"], tile["t0"]
        zs = T2("yf_zs")
        y32 = T2("yf_y32")

        def chain_z():
            ez = T2("yf_ez")
            ACT(ez[:, :c], z_ps, AF.Exp, scale=-1.0)
            RECIP1P(ez[:, :c], ez[:, :c])
            STT(zs[:, :c], z_ps, onw[:, l, hg:hg + 1], ez[:, :c], ALU.mult, ALU.mult)

        def chain_h():
            sq = T2("yf_sq", BF16)
            ACT(sq[:, :c], hT[:, :c], AF.Square)
            ps = pbank()
            MM(ps[:, :c], ones_b, sq[:, :c])
            rstd = T2("yf_rstd")
            ACT(rstd[:, :c], ps[:, :c], AF.Ln, scale=1.0 / 128, bias=c_eps)
            ACT(rstd[:, :c], rstd[:, :c], AF.Exp, scale=-0.5)
            TT(y32[:, :c], hT[:, :c], rstd[:, :c], ALU.mult)

        par(chain_h, chain_z)
        yb = T2("yf_yb", BF16)
        TT(yb[:, :c], y32[:, :c], zs[:, :c], ALU.mult)
        S.D("sp", yTv[ti][hg, :, :], yb[:, :c])

    yTv = [V(Tk(f"dram:yT{ti}"), yT_t[:, :, t["t0"]:t["t0"] + t["c"]]) for ti, t in enumerate(TILES)]
    xrv = [V(Tk(f"dram:xr{ti}"), xres_t[:, :, t["t0"]:t["t0"] + t["c"]]) for ti, t in enumerate(TILES)]

    def mlstm_setup(l, h):
        LN = LANES[cur[0]]
        LN["wp"] = 0
        pA = LN["pA"]
        mrow_p = LN["mrow_p"]
        wbs = [wblock(l, A_Q + h * 128), wblock(l, A_K + h * 128), wblock(l, A_V + h * 128),
               wblock(l, A_O + h * 128), wblock(l, Z0 + h * 128)]
        wg = wgate(l, A_I + h, A_F + h)
        S.op("dve", lambda e: e.memset(pA.ap, 0.0), [], [pA.tk])
        S.op("dve", lambda e: e.memset(mrow_p.ap, 0.0), [], [mrow_p.tk])
        return wbs, wg

    def mlstm_tile(l, h, ti, tile, pj, prefetch):
        LN = LANES[cur[0]]
        pA = LN["pA"]
        mrow_p = LN["mrow_p"]
        c, gs, t0 = tile["c"], tile["gA"], tile["t0"]
        G = c // gs
        smp = tile["kind"] == "s"
        if smp:
            St = sA[0]
            S.D("sp", St[:, :, 0:128], sC[l, :, h, :, :].rearrange("g d e -> d g e"))
            S.D("sp", St[:, :, 128:129], sn[l, :, h, :].rearrange("g d -> d g").unsqueeze(2))
            mrow = mrow_s
            S.D("sp", mrow, sm[l, :, h].unsqueeze(0))
        else:
            St, mrow = pA, mrow_p
        (q_ps, k_ps, v_ps, o_ps, z_ps), (gi_ps, gf_ps) = pj
        qT = T2("a_qT", BF16); CP(qT[:, :c], q_ps)
        kTf = T2("a_kTf"); S.I("act", "mul", out=kTf[:, :c], in_=k_ps, mul=RS)
        kT = T2("a_kT", BF16); CP(kT[:, :c], kTf[:, :c], eng="dve")
        vTf = T2("a_vTf"); CP(vTf[:, :c], v_ps)
        eo = T2("a_eo"); ACT(eo[:, :c], o_ps, AF.Exp, scale=-1.0)
        zf = T2("a_zf"); CP(zf[:, :c], z_ps)
        li = R1("a_li"); TS(li[:, :c], gi_ps, gb_row[0:1, l * 12 + h:l * 12 + h + 1], ALU.add)
        e1 = R1("a_e1"); ACT(e1[:, :c], gf_ps, AF.Exp, scale=-1.0, bias=ngb_row[0:1, l * 12 + 6 + h:l * 12 + 7 + h])
        prefetch()
        yield
        sp_ = R1("a_sp"); ACT(sp_[:, :c], e1[:, :c], AF.Ln, bias=c_one[0:1, :])
        Fn = R1("a_Fn"); scan(Fn[:, :c], resets[0:1, RESET_IDX[gs], :c], sp_[:, :c])
        gg = R1("a_g"); TT(gg[:, :c], li[:, :c], Fn[:, :c], ALU.add)
        gmax = R1("a_gmax", 16)
        S.I("dve", "tensor_reduce", out=gmax[:, :G], in_=g3(gg, c, G), axis=AX.X, op=ALU.max)
        Mb = R1("a_Mb", 16); TT(Mb[:, :G], gmax[:, :G], mrow[:, :G], ALU.max)
        al = R1("a_al", 16); TT(al[:, :G], mrow[:, :G], Mb[:, :G], ALU.subtract)
        ACT(al[:, :G], al[:, :G], AF.Exp)
        TT(mrow[:, :G], Mb[:, :G], g3(Fn, c, G)[:, :, gs - 1], ALU.subtract)
        w = R1("a_w"); TT(g3(w, c, G), g3(gg, c, G), Mb[:, :G].un(2).bc([1, G, gs]), ALU.subtract)
        ACT(w[:, :c], w[:, :c], AF.Exp)
        thr = R1("a_thr"); TT(g3(thr, c, G), g3(Fn, c, G), Mb[:, :G].un(2).bc([1, G, gs]), ALU.subtract)
        yield
        pa = pbank()
        MM(pa[:, 0:c], ones_f[0:1, :], thr[:, :c])
        MM(pa[:, 128:128 + G], ones_f[0:1, :], al[:, :G])
        MM(pa[:c, 256:257], w[:, :c], ones_f[0:1, 0:1])
        thrS = T2("a_thrS"); ACT(thrS[:, :c], pa[:, 0:c], AF.Exp)
        ab = tmp("a_ab", [128, 16], F32); CP(ab[:, :G], pa[:, 128:128 + G])
        wcol = tmp("a_wcol", [128, 1], F32); CP(wcol[:c, :], pa[:c, 256:257])
        yield
        pbt = pbank()
        TR(pbt[:c, 0:128], vTf[:, :c])
        TR(pbt[:c, 128:256], kTf[:, :c])
        vp = tmp("a_vp", [128, 129], BF16)
        TS(vp[:c, 0:128], pbt[:c, 0:128], wcol[:c, 0:1], ALU.mult)
        CP(vp[:c, 128:129], wcol[:c, :], eng="dve")
        kt = T2("a_kt", BF16); CP(kt[:c, :], pbt[:c, 128:256])
        wbc = T2("a_wbc", BF16); TS(wbc[:c, :], ones_f[:c, :], wcol[:c, 0:1], ALU.mult)
        yield
        pc = pbank(); MM(pc[:c, :c], kT[:, :c], qT[:, :c])
        PT = T2("a_PT", BF16); TT(PT[:c, :c], pc[:c, :c], mask("U8" if smp else "U128", c), ALU.mult)
        yield
        Cb_, nb_ = (Cb, nbb) if smp else (LN["Cbp"], LN["nbp"])
        for g in range(G):
            ACT(Cb_[:, g, :], St[:, g, 0:128], AF.Copy, scale=ab[:, g:g + 1])
            ACT(nb_[:, g, :], St[:, g, 128:129].bc([128, 128]), AF.Copy, scale=ab[:, g:g + 1])
        pd_ = pbank()
        MM(pd_[:, :c], wbc[:c, :], PT[:c, :c], start=True, stop=False)
        for g in range(G):
            MM(pd_[:, g * gs:(g + 1) * gs], nb_[:, g, :], qT[:, g * gs:(g + 1) * gs], start=False, stop=(g == G - 1))
        denS = T2("a_denS"); CP(denS[:, :c], pd_[:, :c])
        dmax = T2("a_dmax"); STT(dmax[:, :c], denS[:, :c], -1.0, denS[:, :c], ALU.mult, ALU.max)
        TT(dmax[:, :c], dmax[:, :c], thrS[:, :c], ALU.max)
        yield
        RECIP(dmax[:, :c], dmax[:, :c])
        pn_ = pbank()
        MM(pn_[:, :c], vp[:c, 0:128], PT[:c, :c], start=True, stop=False)
        for g in range(G):
            MM(pn_[:, g * gs:(g + 1) * gs], Cb_[:, g, :], qT[:, g * gs:(g + 1) * gs], start=False, stop=(g == G - 1))
        hT = T2("a_hT"); TT(hT[:, :c], pn_[:, :c], dmax[:, :c], ALU.mult)
        RECIP1P(eo[:, :c], eo[:, :c])
        TT(hT[:, :c], hT[:, :c], eo[:, :c], ALU.mult)
        y_finalize(l, h, ti, tile, hT, zf[:, :c])
        yield
        if G > 1:
            vpm = tmp("a_vpm", [128, 16, 129], BF16, 1)
            o_, g_ = RM_OFF[gs]
            TT(vpm[:c, :G, :], vp[:c, :].un(1).bc([c, G, 129]), rmk[:c, o_:o_ + G].un(2).bc([c, G, 129]), ALU.mult)
        for g in range(G):
            pu = pbank()
            MM(pu[:, 0:129], kt[:c, :], vpm[:c, g, :] if G > 1 else vp[:c, :])
            STT(St[:, g, :], St[:, g, :], ab[:, g:g + 1], pu[:, 0:129], ALU.mult, ALU.add)
            yield
        if smp:
            S.D("sp", oC[l, :, h, :, :].rearrange("g d e -> d g e"), St[:, :, 0:128])
            S.D("sp", on[l, :, h, :].rearrange("g d -> d g").unsqueeze(2), St[:, :, 128:129])
            S.D("sp", om[l, :, h].unsqueeze(0), mrow)
        elif tile["last"]:
            S.D("sp", pC[l, h, :, :], St[:, 0, 0:128])
            S.D("sp", pn[l, h, :].unsqueeze(1), St[:, 0, 128:129])
            S.D("sp", pm[l, h:h + 1].unsqueeze(0), mrow[:, 0:1])
        yield

    import math

    def gdn_setup(l, h):
        LN = LANES[cur[0]]
        LN["wp"] = 0
        pBs = LN["pBs"]
        pBb = LN["pBb"]
        ue_p = LN["ue_p"]
        wbs = [wblock(l, B_Q + h * 128), wblock(l, B_K + h * 128), wblock(l, B_V + h * 128), wblock(l, Z0 + (6 + h) * 128)]
        wg = wgate(l, B_A + h, B_B + h)
        S.op("dve", lambda e: e.memset(pBs.ap, 0.0), [], [pBs.tk])
        S.op("dve", lambda e: e.memset(pBb.ap, 0.0), [], [pBb.tk])
        for i in range(3):
            S.op("dve", lambda e, i=i: e.memset(ue_p[i].ap, 0.0), [], [ue_p[i].tk])
        return wbs, wg

    def gdn_tile(l, h, ti, tile, pj, prefetch):
        LN = LANES[cur[0]]
        pBs = LN["pBs"]
        pBb = LN["pBb"]
        ue_p = LN["ue_p"]
        c, gs, t0 = tile["c"], tile["gB"], tile["t0"]
        G = c // gs
        smp = tile["kind"] == "s"
        nseq, L = (16, 8) if smp else (1, c)
        if smp:
            Sf, Sb_ = sB[0], sBb
            S.D("sp", Sf, sS[l, :, h, :, :].rearrange("g d e -> d g e"))
            CP(Sb_, Sf)
            for i in range(3):
                ch0 = i * 768 + h * 128
                for j_ in range(3):
                    S.D("sp", ue_s[i][:, :, j_], sconv[l, :, j_, ch0:ch0 + 128].rearrange("g p -> p g"))
        else:
            Sf, Sb_ = pBs, pBb
        (q_ps, k_ps, v_ps, z_ps), (ga_ps, gb_ps) = pj
        for i, p_ in enumerate((q_ps, k_ps, v_ps)):
            ue = ue_s[i] if smp else ue_p[i]
            CP(ue[:, :, 3:3 + L], p_.re("p (a b) -> p a b", a=nseq))
        zf = T2("b_zf"); CP(zf[:, :c], z_ps)
        xa = R1("b_xa"); TS(xa[:, :c], ga_ps, dtb_row[0:1, l * 6 + h:l * 6 + h + 1], ALU.add)
        beta = R1("b_beta"); ACT(beta[:, :c], gb_ps, AF.Exp, scale=-1.0)
        prefetch()
        yield
        cs = [None, None, None]

        def conv_chain(i):
            def f():
                ue = ue_s[i] if smp else ue_p[i]
                bidx = i * 6 + h
                cv = T2(f"b_cv{i}")
                cvv = cv[:, :c].re("p (a b) -> p a b", a=nseq)
                TS(cvv, ue[:, :, 0:L], cw[:, l, bidx, 0:1], ALU.mult)
                for j in range(1, 4):
                    STT(cvv, ue[:, :, j:j + L], cw[:, l, bidx, j:j + 1], cvv, ALU.mult, ALU.add)
                ex = T2(f"b_ex{i}")
                ACT(ex[:, :c], cv[:, :c], AF.Exp, scale=-1.0)
                RECIP1P(ex[:, :c], ex[:, :c])
                c_ = T2(f"b_cs{i}")
                TT(c_[:, :c], cv[:, :c], ex[:, :c], ALU.mult)
                cs[i] = c_
                if smp:
                    ch0 = i * 768 + h * 128
                    for j_ in range(3):
                        S.D("sp", oconv[l, :, j_, ch0:ch0 + 128].rearrange("g p -> p g"), ue[:, :, 8 + j_])
                else:
                    CP(ue[:, :, 0:3], ue[:, :, L:L + 3])
                    if tile["last"]:
                        ch0 = i * 768 + h * 128
                        S.D("sp", pconv[l, :, ch0:ch0 + 128].rearrange("j p -> p j"), ue[:, 0, 0:3])
            return f

        GR = {}

        def gate_chain():
            ACT(xa[:, :c], xa[:, :c], AF.Exp)
            ACT(xa[:, :c], xa[:, :c], AF.Ln, bias=c_one[0:1, :])
            gn = R1("b_gn"); TS(gn[:, :c], xa[:, :c], al_row[0:1, l * 6 + h:l * 6 + h + 1], ALU.mult)
            Gn = R1("b_Gn"); scan(Gn[:, :c], resets[0:1, RESET_IDX[gs], :c], gn[:, :c])
            RECIP1P(beta[:, :c], beta[:, :c])
            eG = R1("b_eG"); ACT(eG[:, :c], Gn[:, :c], AF.Exp, scale=-1.0)
            eGe = R1("b_eGe", 16); ACT(eGe[:, :G], g3(Gn, c, G)[:, :, gs - 1], AF.Exp, scale=-1.0)
            df = R1("b_df"); TT(g3(df, c, G), g3(Gn, c, G), g3(Gn, c, G)[:, :, gs - 1:gs].bc([1, G, gs]), ALU.subtract)
            ACT(df[:, :c], df[:, :c], AF.Exp)
            bg = R1("b_bg"); TT(bg[:, :c], beta[:, :c], eG[:, :c], ALU.mult)
            Gr = R1("b_Gr"); TS(Gr[:, :c], Gn[:, :c], -1.0, ALU.mult)
            GR.update(Gn=Gn, eG=eG, eGe=eGe, df=df, bg=bg, Gr=Gr)

        par(conv_chain(0), conv_chain(1), conv_chain(2), gate_chain)
        Gn, eG, eGe, df, bg, Gr = GR["Gn"], GR["eG"], GR["eGe"], GR["df"], GR["bg"], GR["Gr"]
        yield
        nf = []
        for i in range(2):
            sq = T2("b_sq", BF16); ACT(sq[:, :c], cs[i][:, :c], AF.Square)
            ps = pbank(); MM(ps[:, :c], ones_b, sq[:, :c])
            rn = T2("b_rn"); ACT(rn[:, :c], ps[:, :c], AF.Ln, bias=c_eps)
            ACT(rn[:, :c], rn[:, :c], AF.Exp, scale=-0.5)
            f_ = T2(f"b_nf{i}")
            if i == 0:
                STT(f_[:, :c], cs[0][:, :c], RS, rn[:, :c], ALU.mult, ALU.mult)
            else:
                TT(f_[:, :c], cs[1][:, :c], rn[:, :c], ALU.mult)
            nf.append(f_)
        qf, kf = nf
        yield
        pa = pbank()
        MM(pa[:, 0:c], ones_f[0:1, :], beta[:, :c])
        MM(pa[:, 128:128 + c], ones_f[0:1, :], eG[:, :c])
        MM(pa[:, 256:256 + G], ones_f[0:1, :], eGe[:, :G])
        MM(pa[:c, 384:385], beta[:, :c], ones_f[0:1, 0:1])
        MM(pa[:c, 385:386], bg[:, :c], ones_f[0:1, 0:1])
        MM(pa[:c, 386:387], df[:, :c], ones_f[0:1, 0:1])
        bb = tmp("b_bb", [128, 512], F32)
        CP(bb[:, 0:128 + c], pa[:, 0:128 + c])
        CP(bb[:, 256:256 + G], pa[:, 256:256 + G], eng="dve")
        CP(bb[:c, 384:387], pa[:c, 384:387], eng="dve")
        beta_bc, eG_bc, eGe_bc = bb[:, 0:c], bb[:, 128:128 + c], bb[:, 256:256 + G]
        yield
        pd_ = pbank()
        MM(pd_[:c, :c], Gn[:, :c], ones_f[0:1, :c], start=True, stop=False)
        MM(pd_[:c, :c], ones_f[0:1, :c], Gr[:, :c], start=False, stop=True)
        dT = T2("b_dT"); TS(dT[:c, :c], pd_[:c, :c], 0.0, ALU.min)
        ACT(dT[:c, :c], dT[:c, :c], AF.Exp)
        mU, mSU = ("U8", "SU8") if smp else ("U64", "SU64")
        dTS = T2("b_dTS"); TT(dTS[:c, :c], dT[:c, :c], mask(mSU, c), ALU.mult)
        dTI = T2("b_dTI"); TT(dTI[:c, :c], dT[:c, :c], mask(mU, c), ALU.mult)
        yield
        khT = T2("b_khT", BF16); CP(khT[:, :c], kf[:, :c], eng="dve")
        kbT = T2("b_kbT", BF16); TT(kbT[:, :c], kf[:, :c], beta_bc, ALU.mult)
        qhT = T2("b_qhT", BF16); CP(qhT[:, :c], qf[:, :c], eng="dve")
        qgT = T2("b_qgT", BF16); TT(qgT[:, :c], qf[:, :c], eG_bc, ALU.mult)
        pe_ = pbank()
        MM(pe_[:c, 0:c], khT[:, :c], kbT[:, :c])
        MM(pe_[:c, 128:128 + c], khT[:, :c], qhT[:, :c])
        Bm = T2("b_B"); TT(Bm[:c, :c], pe_[:c, 0:c], dTS[:c, :c], ALU.mult)
        Aqk = T2("b_Aqk", BF16); TT(Aqk[:c, :c], pe_[:c, 128:128 + c], dTI[:c, :c], ALU.mult)
        X = T2("b_X"); TT(X[:c, :c], ident[:c, :c], Bm[:c, :c], ALU.subtract)
        pf = pbank(); TR(pf[:c, :c], Bm[:c, :c])
        Am = T2("b_A"); CP(Am[:c, :c], pf[:c, :c])
        yield
        nsq = int(math.log2(gs)) - 1
        for lev in range(nsq):
            pg = pbank()
            MM(pg[:c, 0:c], Bm[:c, :c], Am[:c, :c])
            if lev < nsq - 1:
                MM(pg[:c, 128:128 + c], Am[:c, :c], Bm[:c, :c])
            A2 = T2("b_A"); CP(A2[:c, :c], pg[:c, 0:c])
            if lev < nsq - 1:
                B2 = T2("b_B"); CP(B2[:c, :c], pg[:c, 128:128 + c], eng="dve")
            ph = pbank(); MM(ph[:c, :c], A2[:c, :c], X[:c, :c])
            Xn = T2("b_X"); TT(Xn[:c, :c], X[:c, :c], ph[:c, :c], ALU.add)
            yield
            X, Am = Xn, A2
            if lev < nsq - 1:
                Bm = B2
        pt = pbank()
        TR(pt[:c, 0:128], kf[:, :c])
        TR(pt[:c, 128:256], cs[2][:, :c])
        kbg = T2("b_kbg", BF16); TS(kbg[:c, :], pt[:c, 0:128], bb[:c, 385:386], ALU.mult)
        kd = T2("b_kd", BF16); TS(kd[:c, :], pt[:c, 0:128], bb[:c, 386:387], ALU.mult)
        bv = T2("b_bv", BF16); TS(bv[:c, :], pt[:c, 128:256], bb[:c, 384:385], ALU.mult)
        yield
        Xb = T2("b_Xb", BF16); CP(Xb[:c, :c], X[:c, :c], eng="dve")
        pw = pbank(); MM(pw[:, :c], kbg[:c, :], Xb[:c, :c])
        nWk = T2("b_nWk", BF16); S.I("act", "mul", out=nWk[:, :c], in_=pw[:, :c], mul=-1.0)
        yield
        if G > 1:
            kdm = tmp("b_kdm", [128, 16, 128], BF16, 1)
            o_, g_ = RM_OFF[gs]
            TT(kdm[:c, :G, :], kd[:c, :].un(1).bc([c, G, 128]), rmk[:c, o_:o_ + G].un(2).bc([c, G, 128]), ALU.mult)
        po = LN["X"]
        for g in range(G):
            gi_ = g if smp else 0
            cols = slice(g * gs, (g + 1) * gs)
            pv = pbank()
            MM(pv[:c, 0:128], Xb[:c, :c], bv[:c, :], start=True, stop=False)
            MM(pv[:c, 0:128], nWk[:, :c], Sb_[:, gi_, :], start=False, stop=True)
            Wg = T2("b_Wg", BF16); CP(Wg[:c, :], pv[:c, 0:128])
            yield
            MM(po[:, cols], Sb_[:, gi_, :], qgT[:, cols], start=True, stop=False)
            MM(po[:, cols], Wg[:c, :], Aqk[:c, cols], start=False, stop=True)
            pu = pbank()
            MM(pu[:, 0:128], kdm[:c, g, :] if G > 1 else kd[:c, :], Wg[:c, :])
            STT(Sf[:, gi_, :], Sf[:, gi_, :], eGe_bc[:, g:g + 1], pu[:, 0:128], ALU.mult, ALU.add)
            yield
            if not smp:
                CP(Sb_[:, 0, :], Sf[:, 0, :])
        hT = T2("b_hT"); CP(hT[:, :c], po[:, :c])
        y_finalize(l, 6 + h, ti, tile, hT, zf[:, :c])
        yield
        if smp:
            S.D("sp", oS[l, :, h, :, :].rearrange("g d e -> d g e"), Sf)
        elif tile["last"]:
            S.D("sp", pS[l, h, :, :], Sf[:, 0, :])
        yield

    def hgrn_setup(l, h):
        LN = LANES[cur[0]]
        LN["wp"] = 0
        pCs = LN["pCs"]
        pCb = LN["pCb"]
        wbs = [wblock(l, C_Q + h * 128), wblock(l, C_F + h * 128), wblock(l, C_I + h * 128), wblock(l, Z0 + (12 + h) * 128)]
        S.op("dve", lambda e: e.memset(pCs.ap, 0.0), [], [pCs.tk])
        S.op("dve", lambda e: e.memset(pCb.ap, 0.0), [], [pCb.tk])
        return wbs, None

    def hgrn_tile(l, h, ti, tile, pj, prefetch):
        LN = LANES[cur[0]]
        lbv, omlv = lb[:, l, h:h + 1], oml[:, l, h:h + 1]
        pCs = LN["pCs"]
        pCb = LN["pCb"]
        c, gs, t0 = tile["c"], tile["gC"], tile["t0"]
        G = c // gs
        smp = tile["kind"] == "s"
        if smp:
            Sf, Sb_ = sCc[0], sCb
            S.D("sp", Sf, sH[l, :, h, :, :].rearrange("g d e -> d g e"))
            CP(Sb_, Sf)
        else:
            Sf, Sb_ = pCs, pCb
        (q_ps, f_ps, i_ps, z_ps), _ = pj
        qraw = T2("c_qraw"); CP(qraw[:, :c], q_ps)
        e1 = T2("c_e1"); ACT(e1[:, :c], f_ps, AF.Exp, scale=-1.0)
        vf = T2("c_vf"); CP(vf[:, :c], i_ps)
        zf = T2("c_zf"); CP(zf[:, :c], z_ps)
        prefetch()
        yield
        eq = T2("c_eq"); ACT(eq[:, :c], qraw[:, :c], AF.Exp, scale=-1.0)
        RECIP1P(eq[:, :c], eq[:, :c])
        qf = T2("c_qf"); TT(qf[:, :c], qraw[:, :c], eq[:, :c], ALU.mult)
        yield
        TS(e1[:, :c], e1[:, :c], float(np.exp(60.0)), ALU.min)
        l1 = T2("c_l1"); ACT(l1[:, :c], e1[:, :c], AF.Ln, scale=lbv, bias=c_one)
        l2 = T2("c_l2"); ACT(l2[:, :c], e1[:, :c], AF.Ln, bias=c_one)
        nlf = T2("c_nlf"); TT(nlf[:, :c], l2[:, :c], l1[:, :c], ALU.subtract)
        r_ = T2("c_r"); ACT(r_[:, :c], l2[:, :c], AF.Exp, scale=-1.0)
        kf = T2("c_kf"); STT(kf[:, :c], e1[:, :c], omlv, r_[:, :c], ALU.mult, ALU.mult)
        yield
        bn = T2("c_bn"); scan(bn[:, :c], resets[:, RESET_IDX[gs], :c], nlf[:, :c])
        eb = T2("c_eb"); ACT(eb[:, :c], bn[:, :c], AF.Exp, scale=-1.0)
        enb = T2("c_enb"); ACT(enb[:, :c], bn[:, :c], AF.Exp)
        qeb = T2("c_qeb", BF16); TT(qeb[:, :c], qf[:, :c], eb[:, :c], ALU.mult)
        keb = T2("c_keb", BF16); TT(keb[:, :c], kf[:, :c], enb[:, :c], ALU.mult)
        yield
        ebe = tmp("c_ebe", [128, 16], F32); ACT(ebe[:, :G], g3(bn, c, G)[:, :, gs - 1], AF.Exp, scale=-1.0)
        kdT = T2("c_kdT"); TT(g3(kdT, c, G), g3(bn, c, G), g3(bn, c, G)[:, :, gs - 1:gs].bc([128, G, gs]), ALU.subtract)
        ACT(kdT[:, :c], kdT[:, :c], AF.Exp)
        TT(kdT[:, :c], kdT[:, :c], kf[:, :c], ALU.mult)
        yield
        pt = pbank()
        TR(pt[:c, 0:128], kdT[:, :c])
        TR(pt[:c, 128:256], vf[:, :c])
        kd = T2("c_kd", BF16); CP(kd[:c, :], pt[:c, 0:128])
        vb = T2("c_vb", BF16); CP(vb[:c, :], pt[:c, 128:256], eng="dve")
        yield
        if G > 1:
            kdm = tmp("c_kdm", [128, 16, 128], BF16, 1)
            o_, g_ = RM_OFF[gs]
            TT(kdm[:c, :G, :], kd[:c, :].un(1).bc([c, G, 128]), rmk[:c, o_:o_ + G].un(2).bc([c, G, 128]), ALU.mult)
        pa = pbank(); MM(pa[:c, :c], keb[:, :c], qeb[:, :c])
        mname = "U8" if smp else ("U32" if c == 128 else "U128")
        AT = T2("c_AT", BF16); TT(AT[:c, :c], pa[:c, :c], mask(mname, c), ALU.mult)
        yield
        po = LN["X"]
        MM(po[:, :c], vb[:c, :], AT[:c, :c], start=True, stop=False)
        for g in range(G):
            gi_ = g if smp else 0
            cols = slice(g * gs, (g + 1) * gs)
            MM(po[:, cols], Sb_[:, gi_, :], qeb[:, cols], start=False, stop=(g == G - 1))
            pu = pbank()
            MM(pu[:, 0:128], kdm[:c, g, :] if G > 1 else kd[:c, :], vb[:c, :])
            STT(Sf[:, gi_, :], Sf[:, gi_, :], ebe[:, g:g + 1], pu[:, 0:128], ALU.mult, ALU.add)
            yield
            if not smp:
                CP(Sb_[:, 0, :], Sf[:, 0, :])
        hT = T2("c_hT"); CP(hT[:, :c], po[:, :c])
        y_finalize(l, 12 + h, ti, tile, hT, zf[:, :c])
        yield
        if smp:
            S.D("sp", oH[l, :, h, :, :].rearrange("g d e -> d g e"), Sf)
        elif tile["last"]:
            S.D("sp", pH[l, h, :, :], Sf[:, 0, :])
        yield

    wo_c = []

    def out_stage(l):
        S.region("O")
        if not wo_c:
            wo_c.extend(S.sb(f"wo{q4}", [128, 16, 512], BF16) for q4 in range(4))
        for q4 in range(4):
            S.D("pool", wo_c[q4], w_out[l, :, q4 * 512:(q4 + 1) * 512].rearrange("(h p) n -> p h n", p=128))
        for ti, tile in enumerate(TILES):
            c, t0 = tile["c"], tile["t0"]
            yt = tmp("op_y", [128, 16, 128], BF16, 2)
            S.D("sp", yt[:, :, :c], yTv[ti].re("h p t -> p h t"))
            xo = tmp("op_xo", [128, NCH, 128], F32, 2)
            S.D("sp", xo[:, :, :c], xrv[ti].re("j p t -> p j t"))
            xn = tmp("xT", [128, NCH, 128], F32, 2)
            for j in range(NCH):
                ps = pbank()
                for hh in range(16):
                    MM(ps[:, :c], wo_c[j // 4][:, hh, (j % 4) * 128:(j % 4 + 1) * 128], yt[:, hh, :c], start=(hh == 0), stop=(hh == 15))
                TT(xn[:, j, :c], xo[:, j, :c], ps[:, :c], ALU.add)
            finish_x(ti, tile, xn, l + 1)

    def run_phase(kind, l, pairs):
        setup = {"a": mlstm_setup, "b": gdn_setup, "c": hgrn_setup}[kind]
        tilef = {"a": mlstm_tile, "b": gdn_tile, "c": hgrn_tile}[kind]
        S.region("H")
        pt_ = [(ti, t) for ti, t in enumerate(TILES) if t["kind"] == "p"]

        def nsteps(items):
            n, depth = 0, 0
            for it in items:
                if it[0] == "gs":
                    if depth == 0:
                        n += 1
                    depth += 1
                elif it[0] == "ge":
                    depth -= 1
                elif depth == 0:
                    n += 1
            return n

        def head_segments(ln, h):
            cur[0] = ln
            S.rec = []
            ws = setup(l, h)
            for ti, tile in enumerate(TILES):
                if tile["kind"] == "s":
                    for _ in tilef(l, h, ti, tile, proj(tile, *ws), lambda: None):
                        pass
            segs = [S.rec]
            S.rec = []
            nxt = {0: proj(pt_[0][1], *ws)}
            bnd = [0]
            for k, (ti, tile) in enumerate(pt_):
                if k > 0:
                    bnd.append(len(S.rec))

                def prefetch(k=k):
                    if k + 1 < len(pt_):
                        nxt[k + 1] = proj(pt_[k + 1][1], *ws)
                for _ in tilef(l, h, ti, tile, nxt[k], prefetch):
                    pass
            full_ = S.rec
            S.rec = None
            bnd.append(len(full_))
            segs += [full_[bnd[i]:bnd[i + 1]] for i in range(len(bnd) - 1)]
            return segs

        lane_heads = [[p[0] for p in pairs], [p[1] for p in pairs]]
        all_segs = [[head_segments(ln, h) for h in lane_heads[ln]] for ln in range(2)]
        T = max(nsteps(sg) for segs in all_segs[0] for sg in segs[1:])
        n_s = max(nsteps(segs[0]) for ln in range(2) for segs in all_segs[ln])
        off = n_s + ((-n_s) % T)
        streams = []
        for ln in range(2):
            st = []
            for segs in all_segs[ln]:
                for i, sg in enumerate(segs):
                    tgt = off if i == 0 else T
                    st += sg
                    st += [("nop",)] * (tgt - nsteps(sg))
            streams.append(st)
        assert 2 * off <= off + 17 * T
        S.replay(streams, offset=off)
        cur[0] = -1

    def full():
        cur[0] = -1
        stage_a()
        S.barrier()
        for l in range(n_layers):
            run_phase("a", l, [(0, 1), (2, 3), (4, 5)])
            run_phase("b", l, [(0, 1), (2, 3), (4, 5)])
            run_phase("c", l, [(0, 1), (2, 3)])
            S.barrier()
            cur[0] = -1
            out_stage(l)
            S.barrier()

    return nc, S, locals()


def core_inputs(inp, c, consts):
    s = c // 2
    sl = slice(16 * c, 16 * c + 16)
    f = lambda a: np.ascontiguousarray(np.asarray(a, dtype=np.float32))
    m = {
        "xp": f(np.concatenate([inp["meta_tokens"], inp["x_prompt"][s]], axis=0)),
        "xs": f(np.asarray(inp["x_sample"])[sl].reshape(128, D)),
        "w_in": consts["_w_in_r"], "w_gc": consts["_w_gc"], "w_out": f(inp["w_out"]),
        "norm_w": f(inp["norm_w"]), "out_norm_w": f(inp["out_norm_w"]), "final_norm_w": f(inp["final_norm_w"]),
        "mlstm_gate_b": f(inp["mlstm_gate_b"]), "gdn_A_log": f(inp["gdn_A_log"]), "gdn_dt_bias": f(inp["gdn_dt_bias"]),
        "gdn_conv_w": f(inp["gdn_conv_w"]), "hgrn_lower_bounds": f(inp["hgrn_lower_bounds"]),
        "sC": f(np.asarray(inp["state_mlstm_C"])[:, sl]), "sn": f(np.asarray(inp["state_mlstm_n"])[:, sl]),
        "sm": f(np.asarray(inp["state_mlstm_m"])[:, sl]), "sS": f(np.asarray(inp["state_gdn_S"])[:, sl]),
        "sconv": f(np.asarray(inp["state_gdn_conv"])[:, sl]), "sH": f(np.asarray(inp["state_hgrn_S"])[:, sl]),
    }
    m.update({k: v for k, v in consts.items() if not k.startswith("_")})
    return m


def relayout_w_in(w_in):
    w = np.asarray(w_in, dtype=np.float32)
    out = np.empty((2, 70, 128, NCH * 128), np.float32)
    for i, c0 in enumerate(_blk_cols):
        blk = w[:, :, c0:c0 + 128].reshape(2, NCH, 128, 128)
        out[:, i] = blk.transpose(0, 2, 1, 3).reshape(2, 128, NCH * 128)
    g = w[:, :, _gate_cols].reshape(2, NCH, 128, 24).transpose(0, 2, 1, 3)
    return out, np.ascontiguousarray(g)


_CACHE = {}


def kernel(**inputs):
    if "nc" not in _CACHE:
        nc, S, L = build_nc(n_layers=2)
        L["full"]()
        S.finish()
        _CACHE["nc"] = nc
    nc = _CACHE["nc"]
    consts = host_consts()
    consts["_w_in_r"], consts["_w_gc"] = relayout_w_in(inputs["w_in"])
    in_maps = [core_inputs(inputs, c, consts) for c in range(8)]
    res = run_bass_kernel_spmd(nc, in_maps, core_ids=list(range(8)))
    R = res.results
    f = lambda a: np.asarray(a, dtype=np.float32)
    y_prompt = np.stack([f(R[2 * s]["yp"])[16:] for s in range(4)], axis=0)
    y_sample = np.concatenate([f(R[c]["ys"]).reshape(16, 8, D) for c in range(8)], axis=0)
    pst = lambda k: np.stack([f(R[2 * s][k]) for s in range(4)], axis=1)
    sst = lambda k: np.concatenate([f(R[c][k]) for c in range(8)], axis=1)
    return (y_prompt, y_sample,
            pst("pC"), pst("pn"), pst("pm"), pst("pS"), pst("pconv"), pst("pH"),
            sst("oC"), sst("on"), sst("om"), sst("oS"), sst("oconv"), sst("oH"))
```

```python
import contextlib
import numpy as np
import concourse.bass as bass
import concourse.mybir as mybir
from concourse.bass_utils import run_bass_kernel_spmd

F32 = mybir.dt.float32
BF16 = mybir.dt.bfloat16
I32 = mybir.dt.int32
AF = mybir.ActivationFunctionType
ALU = mybir.AluOpType
AX = mybir.AxisListType


class Tk:
    def __init__(self, name, psum=False):
        self.name = name
        self.psum = psum
        self.w = None
        self.r = []


class V:
    def __init__(self, tk, ap):
        self.tk = tk
        self.ap = ap

    def __getitem__(self, k):
        return V(self.tk, self.ap[k])

    def bc(self, shape):
        return V(self.tk, self.ap.broadcast_to(list(shape)))

    def un(self, axis):
        return V(self.tk, self.ap.unsqueeze(axis))

    def re(self, pat, **kw):
        return V(self.tk, self.ap.rearrange(pat, **kw))

    @property
    def shape(self):
        return self.ap.shape


class Sched:
    ENGS = ("pe", "act", "dve", "pool", "sp")
    ROT = 30000

    def __init__(self, nc):
        self.nc = nc
        self.es = contextlib.ExitStack()
        self.q = {e: [] for e in self.ENGS}
        self.cnt = {e: 0 for e in self.ENGS}
        self.nsem = 0
        self.sem = {e: self._newsem(e) for e in self.ENGS}
        self.waited = {e: {} for e in self.ENGS}
        self.dsem = {}
        self.n_ops = 0
        self.sb_off = 0
        self.sb_max = 0
        self.nalloc = 0
        self.rec = None
        self.offs = {"P": 16512}
        self.cur = "P"

    def split(self):
        r0 = self.offs["P"]
        self.offs["H"] = r0
        self.offs["O"] = r0

    def region(self, r):
        self.cur = r

    def _newsem(self, tag):
        self.nsem += 1
        return self.es.enter_context(self.nc.semaphore(f"s{self.nsem}_{tag}"))

    def sb(self, name, shape, dtype):
        nb = 2 if dtype == BF16 else 4
        n = 1
        for d in shape[1:]:
            n *= d
        size = (n * nb + 63) // 64 * 64
        off = self.offs[self.cur]
        self.offs[self.cur] = off + size
        self.sb_off = off + size
        self.sb_max = max(self.sb_max, self.sb_off)
        assert self.sb_off <= 229376, f"SBUF overflow at {name}: {self.sb_off} region {self.cur}"
        self.nalloc += 1
        t = self.nc.alloc_sbuf_tensor_at(f"{name}_{self.nalloc}", list(shape), dtype, offset=off)
        return V(Tk(name), t[:])

    def ps(self, name, shape, dtype=None):
        t = self.nc.alloc_psum_tensor(name, list(shape), F32)
        return V(Tk(name, psum=True), t[:])

    def I(self, eng, meth, **kw):
        reads, writes, args = [], [], {}
        for k, v in kw.items():
            if isinstance(v, V):
                if k in ("out", "accum_out") or v.tk.psum:
                    writes.append(v.tk)
                else:
                    reads.append(v.tk)
                args[k] = v.ap
            else:
                args[k] = v
        self.op(eng, lambda e: getattr(e, meth)(**args), reads, writes)

    def D(self, eng, out, in_, **kw):
        reads, writes = [], []
        names = []
        if isinstance(in_, V):
            reads.append(in_.tk)
            names.append(in_.tk.name)
            in_ = in_.ap
        if isinstance(out, V):
            writes.append(out.tk)
            names.append(out.tk.name)
            out = out.ap
        sbn = [n for n in names if not n.startswith("dram:")]
        key = sbn[0] if sbn else names[0]
        self.dma(eng, out, in_, reads, writes, key=key, **kw)

    def replay(self, lists, offset=0):
        self.rec = None
        idx = [0] * len(lists)
        step = 0
        while any(idx[k] < len(lists[k]) for k in range(len(lists))):
            for k, lst in enumerate(lists):
                if step < k * offset or idx[k] >= len(lst):
                    continue
                depth = 0
                while idx[k] < len(lst):
                    it = lst[idx[k]]
                    idx[k] += 1
                    if it[0] == "nop":
                        pass
                    elif it[0] == "gs":
                        depth += 1
                    elif it[0] == "ge":
                        depth -= 1
                    elif it[0] == "op":
                        self.op(*it[1:])
                    else:
                        self.dma(it[1], it[2], it[3], it[4], it[5], key=it[6], **it[7])
                    if depth == 0:
                        break
            step += 1

    def barrier(self):
        evs = [(self.sem[e], self.cnt[e]) for e in self.ENGS if self.cnt[e] > 0]
        evs += [(sem, val) for (sem, val) in self.dsem.values() if val > 0]
        for e in self.ENGS:
            wd = self.waited[e]
            for (sem, val) in evs:
                if sem is self.sem[e] and e == "pe":
                    continue
                if wd.get(id(sem), (None, 0))[1] >= val:
                    continue
                wd[id(sem)] = (sem, val)
                self.q[e].append(("wait", sem, val))

    def _deps(self, eng, reads, writes):
        deps = []
        for t in reads:
            if t.w is not None:
                deps.append(t.w)
        for t in writes:
            if t.w is not None:
                deps.append(t.w)
            deps.extend(t.r)
        wd = self.waited[eng]
        need = {}
        for (sem, val, e2) in deps:
            if e2 == "pe" and eng == "pe":
                continue
            k = id(sem)
            if wd.get(k, (None, 0))[1] >= val:
                continue
            if k not in need or need[k][1] < val:
                need[k] = (sem, val)
        for k, (sem, val) in need.items():
            wd[k] = (sem, val)
            self.q[eng].append(("wait", sem, val))

    def _record(self, ev, reads, writes):
        for t in reads:
            t.r.append(ev)
        for t in writes:
            t.w = ev
            t.r = []

    def op(self, eng, fn, reads, writes):
        if self.rec is not None:
            self.rec.append(("op", eng, fn, reads, writes))
            return
        self._deps(eng, reads, writes)
        if self.cnt[eng] >= self.ROT:
            self.sem[eng] = self._newsem(eng)
            self.cnt[eng] = 0
        self.cnt[eng] += 1
        ev = (self.sem[eng], self.cnt[eng], eng)
        self.q[eng].append(("op", fn, self.sem[eng], 1))
        self._record(ev, reads, writes)
        self.n_ops += 1

    def dma(self, eng, out, in_, reads, writes, key=None, **kw):
        if self.rec is not None:
            self.rec.append(("dma", eng, out, in_, reads, writes, key, kw))
            return
        self._deps(eng, reads, writes)
        if key is None:
            key = (writes[0] if writes else reads[0]).name
        if key not in self.dsem:
            self.dsem[key] = [self._newsem("d"), 0]
        ds = self.dsem[key]
        ds[1] += 16
        ev = (ds[0], ds[1], "dma")
        self.q[eng].append(("op", (lambda e, out=out, in_=in_, kw=kw: e.dma_start(out=out, in_=in_, **kw)), ds[0], 16))
        self._record(ev, reads, writes)
        self.n_ops += 1

    def finish(self):
        for key, (sem, val) in self.dsem.items():
            if val > 0:
                self.q["sp"].append(("wait", sem, val))
        for e in self.ENGS:
            if e != "sp" and self.cnt[e] > 0:
                self.q["sp"].append(("wait", self.sem[e], self.cnt[e]))
        nc = self.nc
        q = self.q

        def emit(engine, lst, fuse=False):
            pending = []
            for it in lst:
                if it[0] == "wait":
                    pending.append(it)
                    continue
                keep = pending[:-1] if (fuse and pending) else pending
                for w in keep:
                    engine.wait_ge(w[1], w[2])
                ins = it[1](engine)
                if fuse and pending:
                    ins._wait_ge(pending[-1][1], pending[-1][2])
                ins.then_inc(it[2], it[3])
                pending = []
            for w in pending:
                engine.wait_ge(w[1], w[2])

        with nc.allow_non_contiguous_dma(reason="small strided state/param transfers"), nc.Block() as block:
            @block.tensor
            def _(e):
                emit(e, q["pe"], fuse=True)

            @block.scalar
            def _(e):
                emit(e, q["act"], fuse=True)

            @block.vector
            def _(e):
                emit(e, q["dve"], fuse=True)

            @block.gpsimd
            def _(e):
                emit(e, q["pool"])

            @block.sync
            def _(e):
                emit(e, q["sp"])
        self.es.close()


D = 2048
NCH = 16
NP = 2064
NS = 128
NT = NP + NS
N_IN = 8984
A_Q, A_K, A_V, A_O, A_I, A_F = 0, 768, 1536, 2304, 3072, 3078
B_Q, B_K, B_V, B_A, B_B = 3084, 3852, 4620, 5388, 5394
C_Q, C_F, C_I, Z0 = 5400, 5912, 6424, 6936
EPS = 1e-6
RS = 128 ** -0.5
LANE_OFFSET = 110
_blk_cols = ([A_Q + i * 128 for i in range(24)] + [B_Q + i * 128 for i in range(18)]
             + [C_Q + i * 128 for i in range(12)] + [Z0 + i * 128 for i in range(16)])
BLK_OF = {c: i for i, c in enumerate(_blk_cols)}
_gate_cols = list(range(A_I, A_I + 12)) + list(range(B_A, B_A + 12))
GC_OF = {c: i for i, c in enumerate(_gate_cols)}
MASK_IDX = {"U128": 0, "U64": 1, "SU64": 2, "U32": 3, "U8": 4, "SU8": 5}
RESET_IDX = {128: 0, 64: 1, 32: 2, 8: 3, 16: 0}
RM_OFF = {64: (0, 2), 32: (2, 4), 8: (10, 16)}


def host_consts():
    idx = np.arange(128)
    masks = np.zeros((6, 128, 128), np.float32)
    for name, gs, strict in (("U128", 128, False), ("U64", 64, False), ("SU64", 64, True),
                             ("U32", 32, False), ("U8", 8, False), ("SU8", 8, True)):
        same = (idx[:, None] // gs) == (idx[None, :] // gs)
        tri = (idx[:, None] < idx[None, :]) if strict else (idx[:, None] <= idx[None, :])
        masks[MASK_IDX[name]] = (same & tri).astype(np.float32)
    resets = np.ones((4, 128, 128), np.float32)
    for gs, i in RESET_IDX.items():
        if gs != 16:
            resets[i][:, (idx % gs) == 0] = 0.0
    rm = np.zeros((128, 26), np.float32)
    for gs, (o, g) in RM_OFF.items():
        for j in range(g):
            rm[(idx // gs) == j, o + j] = 1.0
    return {"c_ident": np.eye(128, dtype=np.float32), "c_masks": masks, "c_resets": resets, "c_rm": rm}


def make_tiles():
    tiles = [dict(kind="p", t0=0, c=16, gA=16, gB=16, gC=16, first=True, last=False)]
    for i in range(16):
        tiles.append(dict(kind="p", t0=16 + 128 * i, c=128, gA=128, gB=64, gC=32, first=False, last=(i == 15)))
    tiles.append(dict(kind="s", t0=NP, c=128, gA=8, gB=8, gC=8, first=True, last=True))
    return tiles


def build_nc(n_layers=2, heads=None, do_out=True, debug=False):
    nc = bass.Bass("TRN2", target_bir_lowering=False)
    S = Sched(nc)

    def din(name, shape):
        return nc.dram_tensor(name, list(shape), F32, kind="ExternalInput").ap()

    def dout(name, shape):
        return nc.dram_tensor(name, list(shape), F32, kind="ExternalOutput").ap()

    xp = din("xp", [NP, D]); xs = din("xs", [NS, D])
    w_in = din("w_in", [2, 70, 128, NCH * 128]); w_gc = din("w_gc", [2, 128, NCH, 24]); w_out = din("w_out", [2, D, D])
    norm_w = din("norm_w", [2, D]); out_norm_w = din("out_norm_w", [2, D]); final_norm_w = din("final_norm_w", [D])
    gate_b = din("mlstm_gate_b", [2, 2, 6]); A_log = din("gdn_A_log", [2, 6]); dt_bias = din("gdn_dt_bias", [2, 6])
    conv_w = din("gdn_conv_w", [2, 4, 2304]); lbp = din("hgrn_lower_bounds", [2, 512])
    sC = din("sC", [2, 16, 6, 128, 128]); sn = din("sn", [2, 16, 6, 128]); sm = din("sm", [2, 16, 6])
    sS = din("sS", [2, 16, 6, 128, 128]); sconv = din("sconv", [2, 16, 3, 2304]); sH = din("sH", [2, 16, 4, 128, 128])
    c_ident = din("c_ident", [128, 128]); c_masks = din("c_masks", [6, 128, 128])
    c_resets = din("c_resets", [4, 128, 128]); c_rm = din("c_rm", [128, 26])

    yp = dout("yp", [NP, D]); ys = dout("ys", [NS, D])
    pC = dout("pC", [2, 6, 128, 128]); pn = dout("pn", [2, 6, 128]); pm = dout("pm", [2, 6])
    pS = dout("pS", [2, 6, 128, 128]); pconv = dout("pconv", [2, 3, 2304]); pH = dout("pH", [2, 4, 128, 128])
    oC = dout("oC", [2, 16, 6, 128, 128]); on = dout("on", [2, 16, 6, 128]); om = dout("om", [2, 16, 6])
    oS = dout("oS", [2, 16, 6, 128, 128]); oconv = dout("oconv", [2, 16, 3, 2304]); oH = dout("oH", [2, 16, 4, 128, 128])

    skind = "ExternalOutput" if debug else "Internal"
    xres_t = nc.dram_tensor("xres", [NCH, 128, NT], F32, kind=skind).ap()
    yT_t = nc.dram_tensor("yTs", [16, 128, NT], BF16, kind=skind).ap()
    xres = V(Tk("dram:xres"), xres_t)
    yTd = V(Tk("dram:yT"), yT_t)

    TILES = make_tiles()

    def ACT(out, in_, func, scale=1.0, bias=None):
        kw = dict(out=out, in_=in_, func=func, scale=scale)
        if bias is not None:
            kw["bias"] = bias
        S.I("act", "activation", **kw)

    def TT(out, in0, in1, op, eng="dve"):
        S.I(eng, "tensor_tensor", out=out, in0=in0, in1=in1, op=op)

    def TS(out, in0, s1, op0, s2=None, op1=None, eng="dve"):
        if op1 is None:
            S.I(eng, "tensor_scalar", out=out, in0=in0, scalar1=s1, scalar2=None, op0=op0)
        else:
            S.I(eng, "tensor_scalar", out=out, in0=in0, scalar1=s1, scalar2=s2, op0=op0, op1=op1)

    def STT(out, in0, scalar, in1, op0, op1):
        S.I("dve", "scalar_tensor_tensor", out=out, in0=in0, scalar=scalar, in1=in1, op0=op0, op1=op1)

    def MM(out, lhsT, rhs, start=True, stop=True):
        S.I("pe", "matmul", out=out, lhsT=lhsT, rhs=rhs, start=start, stop=stop)

    def CP(out, in_, eng="act"):
        if eng == "act":
            S.I("act", "copy", out=out, in_=in_)
        else:
            S.I(eng, "tensor_copy", out=out, in_=in_)

    def RECIP(out, in_):
        ACT(out, in_, AF.Ln)
        ACT(out, out, AF.Exp, scale=-1.0)

    def RECIP1P(out, in_):
        ACT(out, in_, AF.Ln, bias=c_one[0:out.shape[0], :])
        ACT(out, out, AF.Exp, scale=-1.0)

    rings = {}
    cur = [-1]
    LANES = []

    slots = {}

    def tmp(tag, shape, dtype, n=2):
        lane_tag = tag[:2] in ("a_", "b_", "c_") or tag[:3] == "yf_"
        if lane_tag:
            n = 1
        if tag[:2] in ("a_", "b_", "c_"):
            k = (tag[:2], tuple(shape), str(dtype), n)
            d = slots.setdefault(k, {})
            if tag not in d:
                d[tag] = len(d)
            tag = f"mix_{tuple(shape)}_{dtype}_{n}_{d[tag]}"
        if lane_tag:
            tag = f"{tag}@{max(cur[0], 0)}"
        if tag not in rings:
            rings[tag] = [[S.sb(f"{tag}{i}", shape, dtype) for i in range(n)], 0]
        r = rings[tag]
        v = r[0][r[1] % n]
        r[1] += 1
        return v

    ident = S.sb("ident", [128, 128], F32)
    masks = S.sb("masks", [128, 6, 128], F32)
    resets = S.sb("resets", [128, 4, 128], F32)
    rmk = S.sb("rmk", [128, 26], F32)
    ones_f = S.sb("ones_f", [128, 128], F32)
    ones_b = S.sb("ones_b", [128, 128], BF16)
    c_one = S.sb("c_one", [128, 1], F32)
    c_eps = S.sb("c_eps", [128, 1], F32)
    S.D("sp", ident, c_ident)
    S.D("sp", masks, c_masks.rearrange("m p n -> p m n"))
    S.D("sp", resets, c_resets.rearrange("m p n -> p m n"))
    S.D("sp", rmk, c_rm)
    S.I("dve", "memset", ap=ones_f, constant=1.0) if False else S.op("dve", lambda e: e.memset(ones_f.ap, 1.0), [], [ones_f.tk])
    S.op("dve", lambda e: e.memset(ones_b.ap, 1.0), [], [ones_b.tk])
    S.op("dve", lambda e: e.memset(c_one.ap, 1.0), [], [c_one.tk])
    S.op("dve", lambda e: e.memset(c_eps.ap, EPS), [], [c_eps.tk])

    def mask(name, c):
        return masks[:c, MASK_IDX[name], :c]

    def TR(out, in_):
        k = in_.shape[0]
        S.I("pe", "transpose", out=out, in_=in_, identity=ident[:k, :k])

    pb = [S.ps(f"pb{i}", [128, 512]) for i in range(8)]
    pj_sets = [(pb[0], pb[1]), (pb[2], pb[3])]
    pring = [0]

    def pbank():
        if cur[0] < 0:
            b = pb[4 + pring[0] % 4]
            pring[0] += 1
            return b
        L = LANES[cur[0]]
        b = L["ring"][L["rp"] % 2]
        L["rp"] += 1
        return b

    gb_row = S.sb("gb_row", [1, 24], F32)
    ngb_row = S.sb("ngb_row", [1, 24], F32)
    al_row = S.sb("al_row", [1, 12], F32)
    dtb_row = S.sb("dtb_row", [1, 12], F32)
    S.D("sp", gb_row, gate_b.rearrange("l w h -> (l w h)").unsqueeze(0))
    S.D("sp", al_row, A_log.rearrange("l h -> (l h)").unsqueeze(0))
    S.D("sp", dtb_row, dt_bias.rearrange("l h -> (l h)").unsqueeze(0))
    TS(ngb_row, gb_row, -1.0, ALU.mult)
    ACT(al_row, al_row, AF.Exp)
    lbraw = S.sb("lbraw", [128, 2, 4], F32)
    S.D("sp", lbraw, lbp.rearrange("l (h p) -> p l h", p=128))
    lb = S.sb("lb", [128, 2, 4], F32)
    oml = S.sb("oml", [128, 2, 4], F32)
    S.op("dve", lambda e: e.memset(lb.ap, 0.0), [], [lb.tk])
    TT(lb[:, 1, :], lbraw[:, 0, :], lbraw[:, 1, :], ALU.subtract)
    ACT(lb[:, 1, :], lb[:, 1, :], AF.Exp)
    RECIP1P(lb[:, 1, :], lb[:, 1, :])
    TS(oml, lb, -1.0, ALU.mult, 1.0, ALU.add)
    nw = S.sb("nw", [128, 2, NCH], F32)
    onw = S.sb("onw", [128, 2, NCH], F32)
    fnw = S.sb("fnw", [128, NCH], F32)
    S.D("sp", nw, norm_w.rearrange("l (j p) -> p l j", p=128))
    S.D("sp", onw, out_norm_w.rearrange("l (j p) -> p l j", p=128))
    S.D("sp", fnw, final_norm_w.rearrange("(j p) -> p j", p=128))
    cw = S.sb("cw", [128, 2, 18, 4], F32)
    for l_ in range(2):
        for j_ in range(4):
            S.D("sp", cw[:, l_, :, j_], conv_w[l_, j_, :].rearrange("(b p) -> p b", p=128))

    xnT = S.sb("xnT", [128, NCH, NT], BF16)

    def finish_x(ti, tile, xT, layer_next):
        c, t0 = tile["c"], tile["t0"]
        if layer_next < n_layers:
            S.D("sp", xrv[ti].re("j p t -> p j t"), xT[:, :, :c])
        sq = tmp("fx_sq", [128, NCH, 128], BF16, 1)
        S.I("act", "activation", out=sq[:, :, :c], in_=xT[:, :, :c], func=AF.Square)
        ps = pbank()
        for j in range(NCH):
            MM(ps[:, :c], ones_b, sq[:, j, :c], start=(j == 0), stop=(j == NCH - 1))
        rstd = tmp("fx_rstd", [128, 128], F32, 2)
        ACT(rstd[:, :c], ps[:, :c], AF.Ln, scale=1.0 / D, bias=c_eps)
        ACT(rstd[:, :c], rstd[:, :c], AF.Exp, scale=-0.5)
        t1 = tmp("fx_t1", [128, NCH, 128], F32, 1)
        TT(t1[:, :, :c], xT[:, :, :c], rstd[:, :c].un(1).bc([128, NCH, c]), ALU.mult)
        if layer_next < n_layers:
            TT(xnT[:, :, t0:t0 + c], t1[:, :, :c], nw[:, layer_next, :].un(2).bc([128, NCH, c]), ALU.mult)
        else:
            TT(t1[:, :, :c], t1[:, :, :c], fnw.un(2).bc([128, NCH, c]), ALU.mult)
            ot = tmp("sa_x", [128, D], F32, 1)
            for q4 in range(4):
                pt = pbank()
                for jj in range(4):
                    j = q4 * 4 + jj
                    TR(pt[:c, jj * 128:(jj + 1) * 128], t1[:, j, :c])
                CP(ot[:c, q4 * 512:(q4 + 1) * 512], pt[:c, :])
            dst = yp[t0:t0 + c, :] if tile["kind"] == "p" else ys[:, :]
            S.D("sp", dst, ot[:c, :])

    def stage_a():
        S.region("O")
        for ti, tile in enumerate(TILES):
            c, t0 = tile["c"], tile["t0"]
            xt = tmp("sa_x", [128, D], F32, 1)
            src = xp[t0:t0 + c, :] if tile["kind"] == "p" else xs[:, :]
            S.D("sp", xt[:c, :], src)
            xT = tmp("xT", [128, NCH, 128], F32, 2)
            for q4 in range(4):
                pt = pbank()
                for jj in range(4):
                    j = q4 * 4 + jj
                    TR(pt[:, jj * 128:jj * 128 + c], xt[:c, j * 128:(j + 1) * 128])
                CP(xT[:, q4 * 4:(q4 + 1) * 4, :c], pt.re("p (a b) -> p a b", a=4)[:, :, :c])
            finish_x(ti, tile, xT, 0)


    S.split()
    S.region("H")
    sA = [S.sb("sA0", [128, 16, 129], F32)]
    sSb = S.sb("sSb", [128, 16, 128], BF16)
    sB = [sA[0][:, :, 0:128]]
    sBb = sSb
    sCc = [sA[0][:, :, 0:128]]
    sCb = sSb
    mrow_s = S.sb("mrow_s", [1, 16], F32)
    ue_s = [S.sb(f"ue_s{i}", [128, 16, 11], F32) for i in range(3)]
    Cb = S.sb("Cb", [128, 16, 128], BF16)
    nbb = S.sb("nbb", [128, 16, 128], BF16)
    for i_ in range(2):
        bk = pb[4 * i_:4 * i_ + 4]
        LANES.append(dict(
            i=i_, PJ=bk[0], X=bk[1], ring=[bk[2], bk[3]], rp=0,
            wring=[S.sb(f"wblk{i_}_{k}", [128, NCH, 128], BF16) for k in range(5)], wp=0,
            wg=S.sb(f"wg{i_}", [128, NCH, 2], BF16),
            pA=S.sb(f"pA{i_}", [128, 1, 129], F32), mrow_p=S.sb(f"mrow_p{i_}", [1, 16], F32),
            pBs=S.sb(f"pBs{i_}", [128, 1, 128], F32), pBb=S.sb(f"pBb{i_}", [128, 1, 128], BF16),
            pCs=S.sb(f"pCs{i_}", [128, 1, 128], F32), pCb=S.sb(f"pCb{i_}", [128, 1, 128], BF16),
            ue_p=[S.sb(f"ue_p{i_}_{k}", [128, 1, 131], F32) for k in range(3)],
            Cbp=S.sb(f"Cbp{i_}", [128, 1, 128], BF16), nbp=S.sb(f"nbp{i_}", [128, 1, 128], BF16),
        ))

    def wblock(l, col0):
        L = LANES[cur[0]]
        wb = L["wring"][L["wp"] % 5]
        L["wp"] += 1
        S.D("pool", wb, w_in[l, BLK_OF[col0], :, :].rearrange("p (j n) -> p j n", j=NCH))
        return wb

    def wgate(l, col_a, col_b):
        wg = LANES[cur[0]]["wg"]
        S.D("pool", wg[:, :, 0:1], w_gc[l, :, :, GC_OF[col_a]:GC_OF[col_a] + 1])
        S.D("pool", wg[:, :, 1:2], w_gc[l, :, :, GC_OF[col_b]:GC_OF[col_b] + 1])
        return wg

    def proj(tile, wbs, wg):
        if S.rec is not None:
            S.rec.append(("gs",))
        r = proj_(tile, wbs, wg)
        if S.rec is not None:
            S.rec.append(("ge",))
        return r

    def proj_(tile, wbs, wg):
        c, t0 = tile["c"], tile["t0"]
        L = LANES[cur[0]]
        outs = []
        for b, wb in enumerate(wbs):
            bank = L["PJ"] if b < 4 else L["X"]
            o = bank[:, (b % 4) * 128:(b % 4) * 128 + c]
            for j in range(NCH):
                MM(o, wb[:, j, :], xnT[:, j, t0:t0 + c], start=(j == 0), stop=(j == NCH - 1))
            outs.append(o)
        gr = []
        if wg is not None:
            for i in range(2):
                o = L["X"][0:1, 256 + i * 128:256 + i * 128 + c]
                for j in range(NCH):
                    MM(o, wg[:, j, i:i + 1], xnT[:, j, t0:t0 + c], start=(j == 0), stop=(j == NCH - 1))
                gr.append(o)
        return outs, gr

    def par(*fns):
        outer = S.rec
        lists = []
        for f in fns:
            S.rec = []
            f()
            lists.append(S.rec)
        S.rec = outer
        idx = [0] * len(lists)
        while any(idx[k] < len(lists[k]) for k in range(len(lists))):
            for k, lst in enumerate(lists):
                if idx[k] < len(lst):
                    it = lst[idx[k]]
                    idx[k] += 1
                    if outer is not None:
                        outer.append(it)
                    elif it[0] == "op":
                        S.op(*it[1:])
                    else:
                        S.dma(it[1], it[2], it[3], it[4], it[5], key=it[6], **it[7])

    def scan(out, msk, data):
        S.I("dve", "tensor_tensor_scan", out=out, data0=msk, data1=data, initial=0.0, op0=ALU.mult, op1=ALU.add)

    def T2(tag, dtype=F32, n=2):
        return tmp(tag, [128, 128], dtype, n)

    def R1(tag, w=128, n=2):
        return tmp(tag, [1, w], F32, n)

    def g3(v, c, G):
        return v[:, :c].re("p (a b) -> p a b", a=G)

    def y_finalize(l, hg, ti, tile, hT, z_ps):
        c, t0 = tile["c"], tile["t0"]
        zs = T2("yf_zs")
        y32 = T2("yf_y32")

        def chain_z():
            ez = T2("yf_ez")
            ACT(ez[:, :c], z_ps, AF.Exp, scale=-1.0)
            RECIP1P(ez[:, :c], ez[:, :c])
            STT(zs[:, :c], z_ps, onw[:, l, hg:hg + 1], ez[:, :c], ALU.mult, ALU.mult)

        def chain_h():
            sq = T2("yf_sq", BF16)
            ACT(sq[:, :c], hT[:, :c], AF.Square)
            ps = pbank()
            MM(ps[:, :c], ones_b, sq[:, :c])
            rstd = T2("yf_rstd")
            ACT(rstd[:, :c], ps[:, :c], AF.Ln, scale=1.0 / 128, bias=c_eps)
            ACT(rstd[:, :c], rstd[:, :c], AF.Exp, scale=-0.5)
            TT(y32[:, :c], hT[:, :c], rstd[:, :c], ALU.mult)

        par(chain_h, chain_z)
        yb = T2("yf_yb", BF16)
        TT(yb[:, :c], y32[:, :c], zs[:, :c], ALU.mult)
        S.D("sp", yTv[ti][hg, :, :], yb[:, :c])

    yTv = [V(Tk(f"dram:yT{ti}"), yT_t[:, :, t["t0"]:t["t0"] + t["c"]]) for ti, t in enumerate(TILES)]
    xrv = [V(Tk(f"dram:xr{ti}"), xres_t[:, :, t["t0"]:t["t0"] + t["c"]]) for ti, t in enumerate(TILES)]

    def mlstm_setup(l, h):
        LN = LANES[cur[0]]
        LN["wp"] = 0
        pA = LN["pA"]
        mrow_p = LN["mrow_p"]
        wbs = [wblock(l, A_Q + h * 128), wblock(l, A_K + h * 128), wblock(l, A_V + h * 128),
               wblock(l, A_O + h * 128), wblock(l, Z0 + h * 128)]
        wg = wgate(l, A_I + h, A_F + h)
        S.op("dve", lambda e: e.memset(pA.ap, 0.0), [], [pA.tk])
        S.op("dve", lambda e: e.memset(mrow_p.ap, 0.0), [], [mrow_p.tk])
        return wbs, wg

    def mlstm_tile(l, h, ti, tile, pj, prefetch):
        LN = LANES[cur[0]]
        pA = LN["pA"]
        mrow_p = LN["mrow_p"]
        c, gs, t0 = tile["c"], tile["gA"], tile["t0"]
        G = c // gs
        smp = tile["kind"] == "s"
        if smp:
            St = sA[0]
            S.D("sp", St[:, :, 0:128], sC[l, :, h, :, :].rearrange("g d e -> d g e"))
            S.D("sp", St[:, :, 128:129], sn[l, :, h, :].rearrange("g d -> d g").unsqueeze(2))
            mrow = mrow_s
            S.D("sp", mrow, sm[l, :, h].unsqueeze(0))
        else:
            St, mrow = pA, mrow_p
        (q_ps, k_ps, v_ps, o_ps, z_ps), (gi_ps, gf_ps) = pj
        qT = T2("a_qT", BF16); CP(qT[:, :c], q_ps)
        kTf = T2("a_kTf"); S.I("act", "mul", out=kTf[:, :c], in_=k_ps, mul=RS)
        kT = T2("a_kT", BF16); CP(kT[:, :c], kTf[:, :c], eng="dve")
        vTf = T2("a_vTf"); CP(vTf[:, :c], v_ps)
        eo = T2("a_eo"); ACT(eo[:, :c], o_ps, AF.Exp, scale=-1.0)
        zf = T2("a_zf"); CP(zf[:, :c], z_ps)
        li = R1("a_li"); TS(li[:, :c], gi_ps, gb_row[0:1, l * 12 + h:l * 12 + h + 1], ALU.add)
        e1 = R1("a_e1"); ACT(e1[:, :c], gf_ps, AF.Exp, scale=-1.0, bias=ngb_row[0:1, l * 12 + 6 + h:l * 12 + 7 + h])
        prefetch()
        yield
        sp_ = R1("a_sp"); ACT(sp_[:, :c], e1[:, :c], AF.Ln, bias=c_one[0:1, :])
        Fn = R1("a_Fn"); scan(Fn[:, :c], resets[0:1, RESET_IDX[gs], :c], sp_[:, :c])
        gg = R1("a_g"); TT(gg[:, :c], li[:, :c], Fn[:, :c], ALU.add)
        gmax = R1("a_gmax", 16)
        S.I("dve", "tensor_reduce", out=gmax[:, :G], in_=g3(gg, c, G), axis=AX.X, op=ALU.max)
        Mb = R1("a_Mb", 16); TT(Mb[:, :G], gmax[:, :G], mrow[:, :G], ALU.max)
        al = R1("a_al", 16); TT(al[:, :G], mrow[:, :G], Mb[:, :G], ALU.subtract)
        ACT(al[:, :G], al[:, :G], AF.Exp)
        TT(mrow[:, :G], Mb[:, :G], g3(Fn, c, G)[:, :, gs - 1], ALU.subtract)
        w = R1("a_w"); TT(g3(w, c, G), g3(gg, c, G), Mb[:, :G].un(2).bc([1, G, gs]), ALU.subtract)
        ACT(w[:, :c], w[:, :c], AF.Exp)
        thr = R1("a_thr"); TT(g3(thr, c, G), g3(Fn, c, G), Mb[:, :G].un(2).bc([1, G, gs]), ALU.subtract)
        yield
        pa = pbank()
        MM(pa[:, 0:c], ones_f[0:1, :], thr[:, :c])
        MM(pa[:, 128:128 + G], ones_f[0:1, :], al[:, :G])
        MM(pa[:c, 256:257], w[:, :c], ones_f[0:1, 0:1])
        thrS = T2("a_thrS"); ACT(thrS[:, :c], pa[:, 0:c], AF.Exp)
        ab = tmp("a_ab", [128, 16], F32); CP(ab[:, :G], pa[:, 128:128 + G])
        wcol = tmp("a_wcol", [128, 1], F32); CP(wcol[:c, :], pa[:c, 256:257])
        yield
        pbt = pbank()
        TR(pbt[:c, 0:128], vTf[:, :c])
        TR(pbt[:c, 128:256], kTf[:, :c])
        vp = tmp("a_vp", [128, 129], BF16)
        TS(vp[:c, 0:128], pbt[:c, 0:128], wcol[:c, 0:1], ALU.mult)
        CP(vp[:c, 128:129], wcol[:c, :], eng="dve")
        kt = T2("a_kt", BF16); CP(kt[:c, :], pbt[:c, 128:256])
        wbc = T2("a_wbc", BF16); TS(wbc[:c, :], ones_f[:c, :], wcol[:c, 0:1], ALU.mult)
        yield
        pc = pbank(); MM(pc[:c, :c], kT[:, :c], qT[:, :c])
        PT = T2("a_PT", BF16); TT(PT[:c, :c], pc[:c, :c], mask("U8" if smp else "U128", c), ALU.mult)
        yield
        Cb_, nb_ = (Cb, nbb) if smp else (LN["Cbp"], LN["nbp"])
        for g in range(G):
            ACT(Cb_[:, g, :], St[:, g, 0:128], AF.Copy, scale=ab[:, g:g + 1])
            ACT(nb_[:, g, :], St[:, g, 128:129].bc([128, 128]), AF.Copy, scale=ab[:, g:g + 1])
        pd_ = pbank()
        MM(pd_[:, :c], wbc[:c, :], PT[:c, :c], start=True, stop=False)
        for g in range(G):
            MM(pd_[:, g * gs:(g + 1) * gs], nb_[:, g, :], qT[:, g * gs:(g + 1) * gs], start=False, stop=(g == G - 1))
        denS = T2("a_denS"); CP(denS[:, :c], pd_[:, :c])
        dmax = T2("a_dmax"); STT(dmax[:, :c], denS[:, :c], -1.0, denS[:, :c], ALU.mult, ALU.max)
        TT(dmax[:, :c], dmax[:, :c], thrS[:, :c], ALU.max)
        yield
        RECIP(dmax[:, :c], dmax[:, :c])
        pn_ = pbank()
        MM(pn_[:, :c], vp[:c, 0:128], PT[:c, :c], start=True, stop=False)
        for g in range(G):
            MM(pn_[:, g * gs:(g + 1) * gs], Cb_[:, g, :], qT[:, g * gs:(g + 1) * gs], start=False, stop=(g == G - 1))
        hT = T2("a_hT"); TT(hT[:, :c], pn_[:, :c], dmax[:, :c], ALU.mult)
        RECIP1P(eo[:, :c], eo[:, :c])
        TT(hT[:, :c], hT[:, :c], eo[:, :c], ALU.mult)
        y_finalize(l, h, ti, tile, hT, zf[:, :c])
        yield
        if G > 1:
            vpm = tmp("a_vpm", [128, 16, 129], BF16, 1)
            o_, g_ = RM_OFF[gs]
            TT(vpm[:c, :G, :], vp[:c, :].un(1).bc([c, G, 129]), rmk[:c, o_:o_ + G].un(2).bc([c, G, 129]), ALU.mult)
        for g in range(G):
            pu = pbank()
            MM(pu[:, 0:129], kt[:c, :], vpm[:c, g, :] if G > 1 else vp[:c, :])
            STT(St[:, g, :], St[:, g, :], ab[:, g:g + 1], pu[:, 0:129], ALU.mult, ALU.add)
            yield
        if smp:
            S.D("sp", oC[l, :, h, :, :].rearrange("g d e -> d g e"), St[:, :, 0:128])
            S.D("sp", on[l, :, h, :].rearrange("g d -> d g").unsqueeze(2), St[:, :, 128:129])
            S.D("sp", om[l, :, h].unsqueeze(0), mrow)
        elif tile["last"]:
            S.D("sp", pC[l, h, :, :], St[:, 0, 0:128])
            S.D("sp", pn[l, h, :].unsqueeze(1), St[:, 0, 128:129])
            S.D("sp", pm[l, h:h + 1].unsqueeze(0), mrow[:, 0:1])
        yield

    import math

    def gdn_setup(l, h):
        LN = LANES[cur[0]]
        LN["wp"] = 0
        pBs = LN["pBs"]
        pBb = LN["pBb"]
        ue_p = LN["ue_p"]
        wbs = [wblock(l, B_Q + h * 128), wblock(l, B_K + h * 128), wblock(l, B_V + h * 128), wblock(l, Z0 + (6 + h) * 128)]
        wg = wgate(l, B_A + h, B_B + h)
        S.op("dve", lambda e: e.memset(pBs.ap, 0.0), [], [pBs.tk])
        S.op("dve", lambda e: e.memset(pBb.ap, 0.0), [], [pBb.tk])
        for i in range(3):
            S.op("dve", lambda e, i=i: e.memset(ue_p[i].ap, 0.0), [], [ue_p[i].tk])
        return wbs, wg

    def gdn_tile(l, h, ti, tile, pj, prefetch):
        LN = LANES[cur[0]]
        pBs = LN["pBs"]
        pBb = LN["pBb"]
        ue_p = LN["ue_p"]
        c, gs, t0 = tile["c"], tile["gB"], tile["t0"]
        G = c // gs
        smp = tile["kind"] == "s"
        nseq, L = (16, 8) if smp else (1, c)
        if smp:
            Sf, Sb_ = sB[0], sBb
            S.D("sp", Sf, sS[l, :, h, :, :].rearrange("g d e -> d g e"))
            CP(Sb_, Sf)
            for i in range(3):
                ch0 = i * 768 + h * 128
                for j_ in range(3):
                    S.D("sp", ue_s[i][:, :, j_], sconv[l, :, j_, ch0:ch0 + 128].rearrange("g p -> p g"))
        else:
            Sf, Sb_ = pBs, pBb
        (q_ps, k_ps, v_ps, z_ps), (ga_ps, gb_ps) = pj
        for i, p_ in enumerate((q_ps, k_ps, v_ps)):
            ue = ue_s[i] if smp else ue_p[i]
            CP(ue[:, :, 3:3 + L], p_.re("p (a b) -> p a b", a=nseq))
        zf = T2("b_zf"); CP(zf[:, :c], z_ps)
        xa = R1("b_xa"); TS(xa[:, :c], ga_ps, dtb_row[0:1, l * 6 + h:l * 6 + h + 1], ALU.add)
        beta = R1("b_beta"); ACT(beta[:, :c], gb_ps, AF.Exp, scale=-1.0)
        prefetch()
        yield
        cs = [None, None, None]

        def conv_chain(i):
            def f():
                ue = ue_s[i] if smp else ue_p[i]
                bidx = i * 6 + h
                cv = T2(f"b_cv{i}")
                cvv = cv[:, :c].re("p (a b) -> p a b", a=nseq)
                TS(cvv, ue[:, :, 0:L], cw[:, l, bidx, 0:1], ALU.mult)
                for j in range(1, 4):
                    STT(cvv, ue[:, :, j:j + L], cw[:, l, bidx, j:j + 1], cvv, ALU.mult, ALU.add)
                ex = T2(f"b_ex{i}")
                ACT(ex[:, :c], cv[:, :c], AF.Exp, scale=-1.0)
                RECIP1P(ex[:, :c], ex[:, :c])
                c_ = T2(f"b_cs{i}")
                TT(c_[:, :c], cv[:, :c], ex[:, :c], ALU.mult)
                cs[i] = c_
                if smp:
                    ch0 = i * 768 + h * 128
                    for j_ in range(3):
                        S.D("sp", oconv[l, :, j_, ch0:ch0 + 128].rearrange("g p -> p g"), ue[:, :, 8 + j_])
                else:
                    CP(ue[:, :, 0:3], ue[:, :, L:L + 3])
                    if tile["last"]:
                        ch0 = i * 768 + h * 128
                        S.D("sp", pconv[l, :, ch0:ch0 + 128].rearrange("j p -> p j"), ue[:, 0, 0:3])
            return f

        GR = {}

        def gate_chain():
            ACT(xa[:, :c], xa[:, :c], AF.Exp)
            ACT(xa[:, :c], xa[:, :c], AF.Ln, bias=c_one[0:1, :])
            gn = R1("b_gn"); TS(gn[:, :c], xa[:, :c], al_row[0:1, l * 6 + h:l * 6 + h + 1], ALU.mult)
            Gn = R1("b_Gn"); scan(Gn[:, :c], resets[0:1, RESET_IDX[gs], :c], gn[:, :c])
            RECIP1P(beta[:, :c], beta[:, :c])
            eG = R1("b_eG"); ACT(eG[:, :c], Gn[:, :c], AF.Exp, scale=-1.0)
            eGe = R1("b_eGe", 16); ACT(eGe[:, :G], g3(Gn, c, G)[:, :, gs - 1], AF.Exp, scale=-1.0)
            df = R1("b_df"); TT(g3(df, c, G), g3(Gn, c, G), g3(Gn, c, G)[:, :, gs - 1:gs].bc([1, G, gs]), ALU.subtract)
            ACT(df[:, :c], df[:, :c], AF.Exp)
            bg = R1("b_bg"); TT(bg[:, :c], beta[:, :c], eG[:, :c], ALU.mult)
            Gr = R1("b_Gr"); TS(Gr[:, :c], Gn[:, :c], -1.0, ALU.mult)
            GR.update(Gn=Gn, eG=eG, eGe=eGe, df=df, bg=bg, Gr=Gr)

        par(conv_chain(0), conv_chain(1), conv_chain(2), gate_chain)
        Gn, eG, eGe, df, bg, Gr = GR["Gn"], GR["eG"], GR["eGe"], GR["df"], GR["bg"], GR["Gr"]
        yield
        nf = []
        for i in range(2):
            sq = T2("b_sq", BF16); ACT(sq[:, :c], cs[i][:, :c], AF.Square)
            ps = pbank(); MM(ps[:, :c], ones_b, sq[:, :c])
            rn = T2("b_rn"); ACT(rn[:, :c], ps[:, :c], AF.Ln, bias=c_eps)
            ACT(rn[:, :c], rn[:, :c], AF.Exp, scale=-0.5)
            f_ = T2(f"b_nf{i}")
            if i == 0:
                STT(f_[:, :c], cs[0][:, :c], RS, rn[:, :c], ALU.mult, ALU.mult)
            else:
                TT(f_[:, :c], cs[1][:, :c], rn[:, :c], ALU.mult)
            nf.append(f_)
        qf, kf = nf
        yield
        pa = pbank()
        MM(pa[:, 0:c], ones_f[0:1, :], beta[:, :c])
        MM(pa[:, 128:128 + c], ones_f[0:1, :], eG[:, :c])
        MM(pa[:, 256:256 + G], ones_f[0:1, :], eGe[:, :G])
        MM(pa[:c, 384:385], beta[:, :c], ones_f[0:1, 0:1])
        MM(pa[:c, 385:386], bg[:, :c], ones_f[0:1, 0:1])
        MM(pa[:c, 386:387], df[:, :c], ones_f[0:1, 0:1])
        bb = tmp("b_bb", [128, 512], F32)
        CP(bb[:, 0:128 + c], pa[:, 0:128 + c])
        CP(bb[:, 256:256 + G], pa[:, 256:256 + G], eng="dve")
        CP(bb[:c, 384:387], pa[:c, 384:387], eng="dve")
        beta_bc, eG_bc, eGe_bc = bb[:, 0:c], bb[:, 128:128 + c], bb[:, 256:256 + G]
        yield
        pd_ = pbank()
        MM(pd_[:c, :c], Gn[:, :c], ones_f[0:1, :c], start=True, stop=False)
        MM(pd_[:c, :c], ones_f[0:1, :c], Gr[:, :c], start=False, stop=True)
        dT = T2("b_dT"); TS(dT[:c, :c], pd_[:c, :c], 0.0, ALU.min)
        ACT(dT[:c, :c], dT[:c, :c], AF.Exp)
        mU, mSU = ("U8", "SU8") if smp else ("U64", "SU64")
        dTS = T2("b_dTS"); TT(dTS[:c, :c], dT[:c, :c], mask(mSU, c), ALU.mult)
        dTI = T2("b_dTI"); TT(dTI[:c, :c], dT[:c, :c], mask(mU, c), ALU.mult)
        yield
        khT = T2("b_khT", BF16); CP(khT[:, :c], kf[:, :c], eng="dve")
        kbT = T2("b_kbT", BF16); TT(kbT[:, :c], kf[:, :c], beta_bc, ALU.mult)
        qhT = T2("b_qhT", BF16); CP(qhT[:, :c], qf[:, :c], eng="dve")
        qgT = T2("b_qgT", BF16); TT(qgT[:, :c], qf[:, :c], eG_bc, ALU.mult)
        pe_ = pbank()
        MM(pe_[:c, 0:c], khT[:, :c], kbT[:, :c])
        MM(pe_[:c, 128:128 + c], khT[:, :c], qhT[:, :c])
        Bm = T2("b_B"); TT(Bm[:c, :c], pe_[:c, 0:c], dTS[:c, :c], ALU.mult)
        Aqk = T2("b_Aqk", BF16); TT(Aqk[:c, :c], pe_[:c, 128:128 + c], dTI[:c, :c], ALU.mult)
        X = T2("b_X"); TT(X[:c, :c], ident[:c, :c], Bm[:c, :c], ALU.subtract)
        pf = pbank(); TR(pf[:c, :c], Bm[:c, :c])
        Am = T2("b_A"); CP(Am[:c, :c], pf[:c, :c])
        yield
        nsq = int(math.log2(gs)) - 1
        for lev in range(nsq):
            pg = pbank()
            MM(pg[:c, 0:c], Bm[:c, :c], Am[:c, :c])
            if lev < nsq - 1:
                MM(pg[:c, 128:128 + c], Am[:c, :c], Bm[:c, :c])
            A2 = T2("b_A"); CP(A2[:c, :c], pg[:c, 0:c])
            if lev < nsq - 1:
                B2 = T2("b_B"); CP(B2[:c, :c], pg[:c, 128:128 + c], eng="dve")
            ph = pbank(); MM(ph[:c, :c], A2[:c, :c], X[:c, :c])
            Xn = T2("b_X"); TT(Xn[:c, :c], X[:c, :c], ph[:c, :c], ALU.add)
            yield
            X, Am = Xn, A2
            if lev < nsq - 1:
                Bm = B2
        pt = pbank()
        TR(pt[:c, 0:128], kf[:, :c])
        TR(pt[:c, 128:256], cs[2][:, :c])
        kbg = T2("b_kbg", BF16); TS(kbg[:c, :], pt[:c, 0:128], bb[:c, 385:386], ALU.mult)
        kd = T2("b_kd", BF16); TS(kd[:c, :], pt[:c, 0:128], bb[:c, 386:387], ALU.mult)
        bv = T2("b_bv", BF16); TS(bv[:c, :], pt[:c, 128:256], bb[:c, 384:385], ALU.mult)
        yield
        Xb = T2("b_Xb", BF16); CP(Xb[:c, :c], X[:c, :c], eng="dve")
        pw = pbank(); MM(pw[:, :c], kbg[:c, :], Xb[:c, :c])
        nWk = T2("b_nWk", BF16); S.I("act", "mul", out=nWk[:, :c], in_=pw[:, :c], mul=-1.0)
        yield
        if G > 1:
            kdm = tmp("b_kdm", [128, 16, 128], BF16, 1)
            o_, g_ = RM_OFF[gs]
            TT(kdm[:c, :G, :], kd[:c, :].un(1).bc([c, G, 128]), rmk[:c, o_:o_ + G].un(2).bc([c, G, 128]), ALU.mult)
        po = LN["X"]
        for g in range(G):
            gi_ = g if smp else 0
            cols = slice(g * gs, (g + 1) * gs)
            pv = pbank()
            MM(pv[:c, 0:128], Xb[:c, :c], bv[:c, :], start=True, stop=False)
            MM(pv[:c, 0:128], nWk[:, :c], Sb_[:, gi_, :], start=False, stop=True)
            Wg = T2("b_Wg", BF16); CP(Wg[:c, :], pv[:c, 0:128])
            yield
            MM(po[:, cols], Sb_[:, gi_, :], qgT[:, cols], start=True, stop=False)
            MM(po[:, cols], Wg[:c, :], Aqk[:c, cols], start=False, stop=True)
            pu = pbank()
            MM(pu[:, 0:128], kdm[:c, g, :] if G > 1 else kd[:c, :], Wg[:c, :])
            STT(Sf[:, gi_, :], Sf[:, gi_, :], eGe_bc[:, g:g + 1], pu[:, 0:128], ALU.mult, ALU.add)
            yield
            if not smp:
                CP(Sb_[:, 0, :], Sf[:, 0, :])
        hT = T2("b_hT"); CP(hT[:, :c], po[:, :c])
        y_finalize(l, 6 + h, ti, tile, hT, zf[:, :c])
        yield
        if smp:
            S.D("sp", oS[l, :, h, :, :].rearrange("g d e -> d g e"), Sf)
        elif tile["last"]:
            S.D("sp", pS[l, h, :, :], Sf[:, 0, :])
        yield

    def hgrn_setup(l, h):
        LN = LANES[cur[0]]
        LN["wp"] = 0
        pCs = LN["pCs"]
        pCb = LN["pCb"]
        wbs = [wblock(l, C_Q + h * 128), wblock(l, C_F + h * 128), wblock(l, C_I + h * 128), wblock(l, Z0 + (12 + h) * 128)]
        S.op("dve", lambda e: e.memset(pCs.ap, 0.0), [], [pCs.tk])
        S.op("dve", lambda e: e.memset(pCb.ap, 0.0), [], [pCb.tk])
        return wbs, None

    def hgrn_tile(l, h, ti, tile, pj, prefetch):
        LN = LANES[cur[0]]
        lbv, omlv = lb[:, l, h:h + 1], oml[:, l, h:h + 1]
        pCs = LN["pCs"]
        pCb = LN["pCb"]
        c, gs, t0 = tile["c"], tile["gC"], tile["t0"]
        G = c // gs
        smp = tile["kind"] == "s"
        if smp:
            Sf, Sb_ = sCc[0], sCb
            S.D("sp", Sf, sH[l, :, h, :, :].rearrange("g d e -> d g e"))
            CP(Sb_, Sf)
        else:
            Sf, Sb_ = pCs, pCb
        (q_ps, f_ps, i_ps, z_ps), _ = pj
        qraw = T2("c_qraw"); CP(qraw[:, :c], q_ps)
        e1 = T2("c_e1"); ACT(e1[:, :c], f_ps, AF.Exp, scale=-1.0)
        vf = T2("c_vf"); CP(vf[:, :c], i_ps)
        zf = T2("c_zf"); CP(zf[:, :c], z_ps)
        prefetch()
        yield
        eq = T2("c_eq"); ACT(eq[:, :c], qraw[:, :c], AF.Exp, scale=-1.0)
        RECIP1P(eq[:, :c], eq[:, :c])
        qf = T2("c_qf"); TT(qf[:, :c], qraw[:, :c], eq[:, :c], ALU.mult)
        yield
        TS(e1[:, :c], e1[:, :c], float(np.exp(60.0)), ALU.min)
        l1 = T2("c_l1"); ACT(l1[:, :c], e1[:, :c], AF.Ln, scale=lbv, bias=c_one)
        l2 = T2("c_l2"); ACT(l2[:, :c], e1[:, :c], AF.Ln, bias=c_one)
        nlf = T2("c_nlf"); TT(nlf[:, :c], l2[:, :c], l1[:, :c], ALU.subtract)
        r_ = T2("c_r"); ACT(r_[:, :c], l2[:, :c], AF.Exp, scale=-1.0)
        kf = T2("c_kf"); STT(kf[:, :c], e1[:, :c], omlv, r_[:, :c], ALU.mult, ALU.mult)
        yield
        bn = T2("c_bn"); scan(bn[:, :c], resets[:, RESET_IDX[gs], :c], nlf[:, :c])
        eb = T2("c_eb"); ACT(eb[:, :c], bn[:, :c], AF.Exp, scale=-1.0)
        enb = T2("c_enb"); ACT(enb[:, :c], bn[:, :c], AF.Exp)
        qeb = T2("c_qeb", BF16); TT(qeb[:, :c], qf[:, :c], eb[:, :c], ALU.mult)
        keb = T2("c_keb", BF16); TT(keb[:, :c], kf[:, :c], enb[:, :c], ALU.mult)
        yield
        ebe = tmp("c_ebe", [128, 16], F32); ACT(ebe[:, :G], g3(bn, c, G)[:, :, gs - 1], AF.Exp, scale=-1.0)
        kdT = T2("c_kdT"); TT(g3(kdT, c, G), g3(bn, c, G), g3(bn, c, G)[:, :, gs - 1:gs].bc([128, G, gs]), ALU.subtract)
        ACT(kdT[:, :c], kdT[:, :c], AF.Exp)
        TT(kdT[:, :c], kdT[:, :c], kf[:, :c], ALU.mult)
        yield
        pt = pbank()
        TR(pt[:c, 0:128], kdT[:, :c])
        TR(pt[:c, 128:256], vf[:, :c])
        kd = T2("c_kd", BF16); CP(kd[:c, :], pt[:c, 0:128])
        vb = T2("c_vb", BF16); CP(vb[:c, :], pt[:c, 128:256], eng="dve")
        yield
        if G > 1:
            kdm = tmp("c_kdm", [128, 16, 128], BF16, 1)
            o_, g_ = RM_OFF[gs]
            TT(kdm[:c, :G, :], kd[:c, :].un(1).bc([c, G, 128]), rmk[:c, o_:o_ + G].un(2).bc([c, G, 128]), ALU.mult)
        pa = pbank(); MM(pa[:c, :c], keb[:, :c], qeb[:, :c])
        mname = "U8" if smp else ("U32" if c == 128 else "U128")
        AT = T2("c_AT", BF16); TT(AT[:c, :c], pa[:c, :c], mask(mname, c), ALU.mult)
        yield
        po = LN["X"]
        MM(po[:, :c], vb[:c, :], AT[:c, :c], start=True, stop=False)
        for g in range(G):
            gi_ = g if smp else 0
            cols = slice(g * gs, (g + 1) * gs)
            MM(po[:, cols], Sb_[:, gi_, :], qeb[:, cols], start=False, stop=(g == G - 1))
            pu = pbank()
            MM(pu[:, 0:128], kdm[:c, g, :] if G > 1 else kd[:c, :], vb[:c, :])
            STT(Sf[:, gi_, :], Sf[:, gi_, :], ebe[:, g:g + 1], pu[:, 0:128], ALU.mult, ALU.add)
            yield
            if not smp:
                CP(Sb_[:, 0, :], Sf[:, 0, :])
        hT = T2("c_hT"); CP(hT[:, :c], po[:, :c])
        y_finalize(l, 12 + h, ti, tile, hT, zf[:, :c])
        yield
        if smp:
            S.D("sp", oH[l, :, h, :, :].rearrange("g d e -> d g e"), Sf)
        elif tile["last"]:
            S.D("sp", pH[l, h, :, :], Sf[:, 0, :])
        yield

    wo_c = []

    def out_stage(l):
        S.region("O")
        if not wo_c:
            wo_c.extend(S.sb(f"wo{q4}", [128, 16, 512], BF16) for q4 in range(4))
        for q4 in range(4):
            S.D("pool", wo_c[q4], w_out[l, :, q4 * 512:(q4 + 1) * 512].rearrange("(h p) n -> p h n", p=128))
        for ti, tile in enumerate(TILES):
            c, t0 = tile["c"], tile["t0"]
            yt = tmp("op_y", [128, 16, 128], BF16, 2)
            S.D("sp", yt[:, :, :c], yTv[ti].re("h p t -> p h t"))
            xo = tmp("op_xo", [128, NCH, 128], F32, 2)
            S.D("sp", xo[:, :, :c], xrv[ti].re("j p t -> p j t"))
            xn = tmp("xT", [128, NCH, 128], F32, 2)
            for j in range(NCH):
                ps = pbank()
                for hh in range(16):
                    MM(ps[:, :c], wo_c[j // 4][:, hh, (j % 4) * 128:(j % 4 + 1) * 128], yt[:, hh, :c], start=(hh == 0), stop=(hh == 15))
                TT(xn[:, j, :c], xo[:, j, :c], ps[:, :c], ALU.add)
            finish_x(ti, tile, xn, l + 1)

    def run_phase(kind, l, pairs):
        setup = {"a": mlstm_setup, "b": gdn_setup, "c": hgrn_setup}[kind]
        tilef = {"a": mlstm_tile, "b": gdn_tile, "c": hgrn_tile}[kind]
        S.region("H")
        pt_ = [(ti, t) for ti, t in enumerate(TILES) if t["kind"] == "p"]

        def nsteps(items):
            n, depth = 0, 0
            for it in items:
                if it[0] == "gs":
                    if depth == 0:
                        n += 1
                    depth += 1
                elif it[0] == "ge":
                    depth -= 1
                elif depth == 0:
                    n += 1
            return n

        def head_segments(ln, h):
            cur[0] = ln
            S.rec = []
            ws = setup(l, h)
            for ti, tile in enumerate(TILES):
                if tile["kind"] == "s":
                    for _ in tilef(l, h, ti, tile, proj(tile, *ws), lambda: None):
                        pass
            segs = [S.rec]
            S.rec = []
            nxt = {0: proj(pt_[0][1], *ws)}
            bnd = [0]
            for k, (ti, tile) in enumerate(pt_):
                if k > 0:
                    bnd.append(len(S.rec))

                def prefetch(k=k):
                    if k + 1 < len(pt_):
                        nxt[k + 1] = proj(pt_[k + 1][1], *ws)
                for _ in tilef(l, h, ti, tile, nxt[k], prefetch):
                    pass
            full_ = S.rec
            S.rec = None
            bnd.append(len(full_))
            segs += [full_[bnd[i]:bnd[i + 1]] for i in range(len(bnd) - 1)]
            return segs

        lane_heads = [[p[0] for p in pairs], [p[1] for p in pairs]]
        all_segs = [[head_segments(ln, h) for h in lane_heads[ln]] for ln in range(2)]
        T = max(nsteps(sg) for segs in all_segs[0] for sg in segs[1:])
        n_s = max(nsteps(segs[0]) for ln in range(2) for segs in all_segs[ln])
        off = n_s + ((-n_s) % T)
        streams = []
        for ln in range(2):
            st = []
            for segs in all_segs[ln]:
                for i, sg in enumerate(segs):
                    tgt = off if i == 0 else T
                    st += sg
                    st += [("nop",)] * (tgt - nsteps(sg))
            streams.append(st)
        assert 2 * off <= off + 17 * T
        S.replay(streams, offset=off)
        cur[0] = -1

    def full():
        cur[0] = -1
        stage_a()
        S.barrier()
        for l in range(n_layers):
            run_phase("a", l, [(0, 1), (2, 3), (4, 5)])
            run_phase("b", l, [(0, 1), (2, 3), (4, 5)])
            run_phase("c", l, [(0, 1), (2, 3)])
            S.barrier()
            cur[0] = -1
            out_stage(l)
            S.barrier()

    return nc, S, locals()


def core_inputs(inp, c, consts):
    s = c // 2
    sl = slice(16 * c, 16 * c + 16)
    f = lambda a: np.ascontiguousarray(np.asarray(a, dtype=np.float32))
    m = {
        "xp": f(np.concatenate([inp["meta_tokens"], inp["x_prompt"][s]], axis=0)),
        "xs": f(np.asarray(inp["x_sample"])[sl].reshape(128, D)),
        "w_in": consts["_w_in_r"], "w_gc": consts["_w_gc"], "w_out": f(inp["w_out"]),
        "norm_w": f(inp["norm_w"]), "out_norm_w": f(inp["out_norm_w"]), "final_norm_w": f(inp["final_norm_w"]),
        "mlstm_gate_b": f(inp["mlstm_gate_b"]), "gdn_A_log": f(inp["gdn_A_log"]), "gdn_dt_bias": f(inp["gdn_dt_bias"]),
        "gdn_conv_w": f(inp["gdn_conv_w"]), "hgrn_lower_bounds": f(inp["hgrn_lower_bounds"]),
        "sC": f(np.asarray(inp["state_mlstm_C"])[:, sl]), "sn": f(np.asarray(inp["state_mlstm_n"])[:, sl]),
        "sm": f(np.asarray(inp["state_mlstm_m"])[:, sl]), "sS": f(np.asarray(inp["state_gdn_S"])[:, sl]),
        "sconv": f(np.asarray(inp["state_gdn_conv"])[:, sl]), "sH": f(np.asarray(inp["state_hgrn_S"])[:, sl]),
    }
    m.update({k: v for k, v in consts.items() if not k.startswith("_")})
    return m


def relayout_w_in(w_in):
    w = np.asarray(w_in, dtype=np.float32)
    out = np.empty((2, 70, 128, NCH * 128), np.float32)
    for i, c0 in enumerate(_blk_cols):
        blk = w[:, :, c0:c0 + 128].reshape(2, NCH, 128, 128)
        out[:, i] = blk.transpose(0, 2, 1, 3).reshape(2, 128, NCH * 128)
    g = w[:, :, _gate_cols].reshape(2, NCH, 128, 24).transpose(0, 2, 1, 3)
    return out, np.ascontiguousarray(g)


_CACHE = {}


def kernel(**inputs):
    if "nc" not in _CACHE:
        nc, S, L = build_nc(n_layers=2)
        L["full"]()
        S.finish()
        _CACHE["nc"] = nc
    nc = _CACHE["nc"]
    consts = host_consts()
    consts["_w_in_r"], consts["_w_gc"] = relayout_w_in(inputs["w_in"])
    in_maps = [core_inputs(inputs, c, consts) for c in range(8)]
    res = run_bass_kernel_spmd(nc, in_maps, core_ids=list(range(8)))
    R = res.results
    f = lambda a: np.asarray(a, dtype=np.float32)
    y_prompt = np.stack([f(R[2 * s]["yp"])[16:] for s in range(4)], axis=0)
    y_sample = np.concatenate([f(R[c]["ys"]).reshape(16, 8, D) for c in range(8)], axis=0)
    pst = lambda k: np.stack([f(R[2 * s][k]) for s in range(4)], axis=1)
    sst = lambda k: np.concatenate([f(R[c][k]) for c in range(8)], axis=1)
    return (y_prompt, y_sample,
            pst("pC"), pst("pn"), pst("pm"), pst("pS"), pst("pconv"), pst("pH"),
            sst("oC"), sst("on"), sst("om"), sst("oS"), sst("oconv"), sst("oH"))
```
